# Optimizing a Trainium2 kernel written in Bass

```python
import jax, jax.numpy as jnp
from jax import lax
import numpy as np

D_MODEL = 2048
BATCH = 2
SEQ = 4096
DEPTH = 4

HEAD_DIM = 64
EPS = 1e-6
Q_BLOCK = 128
FOX_HEADS = 8
FOX_WIDTH = FOX_HEADS * HEAD_DIM
CONV_WIDTH = 512
CONV_TAPS = 3
SGU_GROUPS = 4
SGU_GROUP_DIM = 128
SGU_WIDTH = SGU_GROUPS * SGU_GROUP_DIM
SGU_CHUNK = 128
DIL_PATTERNS = ((128, 1), (512, 4), (2048, 16))
DIL_HEADS_PER_GROUP = 4
DIL_HEADS = DIL_HEADS_PER_GROUP * len(DIL_PATTERNS)
DIL_WIDTH = DIL_HEADS * HEAD_DIM
DIL_OUT = DIL_HEADS_PER_GROUP * HEAD_DIM
ROPE_THETA = 500000.0
ROPE_DIM = HEAD_DIM // 4
N_BRANCH = 4
IN_SIZES = (3 * FOX_WIDTH, FOX_HEADS, 3 * CONV_WIDTH, 2 * SGU_WIDTH, 3 * DIL_WIDTH, N_BRANCH * D_MODEL)
D_IN = 3 * FOX_WIDTH + FOX_HEADS + 3 * CONV_WIDTH + 2 * SGU_WIDTH + 3 * DIL_WIDTH + N_BRANCH * D_MODEL
D_FF = 5632
FFN_TAPS = 3
PLE_DIM = 256

kernel_name = "hybrid_parallel_gated_fox_conv_sgu_dilated"


def rms_norm(x, g):
    xf = x.astype(jnp.float32)
    var = jnp.mean(xf * xf, axis=-1, keepdims=True)
    return (xf * lax.rsqrt(var + EPS)).astype(x.dtype) * g


def causal_dwconv(z, w):
    K = w.shape[0]
    S = z.shape[1]
    zp = jnp.pad(z, ((0, 0), (K - 1, 0), (0, 0)))
    return sum(w[k] * zp[:, k:k + S] for k in range(K))


def partial_rope(x, positions):
    half = ROPE_DIM // 2
    inv = ROPE_THETA ** (-jnp.arange(half, dtype=jnp.float32) * (2.0 / ROPE_DIM))
    ang = positions.astype(jnp.float32)[..., None] * inv
    cos = jnp.cos(ang)[:, :, None, :]
    sin = jnp.sin(ang)[:, :, None, :]
    x1 = x[..., :half].astype(jnp.float32)
    x2 = x[..., half:ROPE_DIM].astype(jnp.float32)
    rot = jnp.concatenate([(x1 * cos - x2 * sin).astype(x.dtype),
                           (x1 * sin + x2 * cos).astype(x.dtype),
                           x[..., ROPE_DIM:]], axis=-1)
    return rot


def forgetting_attention(q, k, v, log_f):
    B, S, H, Dh = q.shape
    nb = S // Q_BLOCK
    F = jnp.cumsum(log_f, axis=1).transpose(0, 2, 1)
    qb = q.reshape(B, nb, Q_BLOCK, H, Dh).transpose(1, 0, 2, 3, 4)
    Fb = F.reshape(B, H, nb, Q_BLOCK).transpose(2, 0, 1, 3)
    kpos = jnp.arange(S)
    scale = Dh ** -0.5

    def one_block(args):
        blk, q_blk, f_blk = args
        s = jnp.einsum('bqhd,bkhd->bhqk', q_blk, k, preferred_element_type=jnp.float32) * scale
        s = s + (f_blk[..., :, None] - F[..., None, :])
        qpos = blk * Q_BLOCK + jnp.arange(Q_BLOCK)
        s = jnp.where(kpos[None, :] <= qpos[:, None], s, -jnp.inf)
        p = jax.nn.softmax(s, axis=-1).astype(v.dtype)
        return jnp.einsum('bhqk,bkhd->bqhd', p, v)

    out = lax.map(one_block, (jnp.arange(nb), qb, Fb))
    return out.transpose(1, 0, 2, 3, 4).reshape(B, S, H * Dh)


def dilated_window_attention(q, k, v, window, dilation):
    B, S, H, Dh = q.shape
    L = S // dilation
    span = window // dilation
    nb = -(-L // span)
    Lp = nb * span

    def strided_blocks(t):
        t = t.reshape(B, L, dilation, H, Dh).transpose(0, 2, 1, 3, 4)
        t = jnp.pad(t, ((0, 0), (0, 0), (0, Lp - L), (0, 0), (0, 0)))
        return t.reshape(B, dilation, nb, span, H, Dh)

    def with_prev(t):
        prev = jnp.pad(t[:, :, :-1], ((0, 0), (0, 0), (1, 0), (0, 0), (0, 0), (0, 0)))
        return jnp.concatenate([prev, t], axis=3)

    qs = strided_blocks(q)
    kc = with_prev(strided_blocks(k))
    vc = with_prev(strided_blocks(v))
    s = jnp.einsum('brnqhd,brnkhd->brnhqk', qs, kc, preferred_element_type=jnp.float32) * (Dh ** -0.5)
    blk = jnp.arange(nb)[:, None, None]
    qi = jnp.arange(span)[None, :, None]
    ki = jnp.arange(2 * span)[None, None, :]
    dist = span + qi - ki
    valid = (dist >= 0) & (dist <= span) & ((blk > 0) | (ki >= span))
    s = jnp.where(valid[:, None], s, -jnp.inf)
    lse = jax.nn.logsumexp(s, axis=-1, keepdims=True)
    p = jnp.exp(s - lse).astype(v.dtype)
    o = jnp.einsum('brnhqk,brnkhd->brnqhd', p, vc)
    o = o.reshape(B, dilation, Lp, H, Dh)[:, :, :L].transpose(0, 2, 1, 3, 4).reshape(B, S, H, Dh)
    lse = lse[..., 0].transpose(0, 1, 2, 4, 3).reshape(B, dilation, Lp, H)[:, :, :L]
    lse = lse.transpose(0, 2, 1, 3).reshape(B, S, H)
    return o, lse


def chunked_spatial_gating(z, norm_g, w_s, b_s):
    u, v = jnp.split(z, 2, axis=-1)
    v = rms_norm(v, norm_g)
    B, S, _ = v.shape
    nc = S // SGU_CHUNK
    vc = v.reshape(B, nc, SGU_CHUNK, SGU_GROUPS, SGU_GROUP_DIM)
    mask = jnp.tril(jnp.ones((SGU_CHUNK, SGU_CHUNK), dtype=bool))
    ws = jnp.where(mask[None], w_s, jnp.zeros_like(w_s))
    mixed = jnp.einsum('gts,bcsgd->bctgd', ws, vc) + b_s.T[None, None, :, :, None]
    return u * mixed.reshape(B, S, SGU_WIDTH)


def setup_inputs(seed: int = 0) -> dict:
    key = jax.random.key(seed)
    ks = jax.random.split(key, 24)

    def nrm(k, shape, scale):
        return jax.random.normal(k, shape, jnp.float32) * scale

    def gain(k, shape):
        return 1.0 + 0.1 * jax.random.normal(k, shape, jnp.float32)

    offset = jax.random.randint(ks[2], (BATCH, 1), 0, 1024, dtype=jnp.int32)
    positions = offset + jnp.arange(SEQ, dtype=jnp.int32)[None, :]
    return {
        "x": nrm(ks[0], (BATCH, SEQ, D_MODEL), 1.0),
        "p": nrm(ks[1], (DEPTH, BATCH, SEQ, PLE_DIM), 1.0),
        "positions": positions,
        "norm_mix_g": gain(ks[3], (DEPTH, D_MODEL)),
        "w_in": nrm(ks[4], (DEPTH, D_MODEL, D_IN), D_MODEL ** -0.5),
        "fox_forget_b": 2.0 + 3.0 * jax.random.uniform(ks[5], (DEPTH, FOX_HEADS), jnp.float32),
        "shortconv_w": nrm(ks[6], (DEPTH, CONV_TAPS, CONV_WIDTH), CONV_TAPS ** -0.5),
        "sgu_norm_g": gain(ks[7], (DEPTH, SGU_WIDTH)),
        "sgu_w": nrm(ks[8], (DEPTH, SGU_GROUPS, SGU_CHUNK, SGU_CHUNK), SGU_CHUNK ** -0.5),
        "sgu_b": gain(ks[9], (DEPTH, SGU_GROUPS, SGU_CHUNK)),
        "w_br_fox": nrm(ks[10], (DEPTH, FOX_WIDTH, D_MODEL), FOX_WIDTH ** -0.5),
        "w_br_conv": nrm(ks[11], (DEPTH, CONV_WIDTH, D_MODEL), CONV_WIDTH ** -0.5),
        "w_br_sgu": nrm(ks[12], (DEPTH, SGU_WIDTH, D_MODEL), SGU_WIDTH ** -0.5),
        "w_br_dil": nrm(ks[13], (DEPTH, DIL_OUT, D_MODEL), DIL_OUT ** -0.5),
        "w_out": nrm(ks[14], (DEPTH, D_MODEL, D_MODEL), D_MODEL ** -0.5),
        "norm_ffn_g": gain(ks[15], (DEPTH, D_MODEL)),
        "w_up": nrm(ks[16], (DEPTH, D_MODEL, 2 * D_FF), D_MODEL ** -0.5),
        "ffn_conv_w": nrm(ks[17], (DEPTH, FFN_TAPS, 2 * D_FF), FFN_TAPS ** -0.5),
        "w_down": nrm(ks[18], (DEPTH, D_FF, D_MODEL), D_FF ** -0.5),
        "norm_ple_g": gain(ks[19], (DEPTH, D_MODEL)),
        "w_ple_gate": nrm(ks[20], (DEPTH, D_MODEL, D_MODEL), D_MODEL ** -0.5),
        "w_ple_proj": nrm(ks[21], (DEPTH, PLE_DIM, D_MODEL), PLE_DIM ** -0.5),
        "final_norm_g": gain(ks[22], (D_MODEL,)),
    }


def reference(x, p, positions, norm_mix_g, w_in, fox_forget_b, shortconv_w, sgu_norm_g, sgu_w, sgu_b,
              w_br_fox, w_br_conv, w_br_sgu, w_br_dil, w_out, norm_ffn_g, w_up, ffn_conv_w, w_down,
              norm_ple_g, w_ple_gate, w_ple_proj, final_norm_g):
    B, S, _ = x.shape
    split_points = [int(c) for c in np.cumsum(IN_SIZES)[:-1]]
    for i in range(DEPTH):
        h = rms_norm(x, norm_mix_g[i])
        proj = h @ w_in[i]
        a_qkv, a_f, b_in, c_in, d_qkv, gate_logits = jnp.split(proj, split_points, axis=-1)

        a_qkv = a_qkv.reshape(B, S, 3, FOX_HEADS, HEAD_DIM)
        log_f = jax.nn.log_sigmoid(a_f.astype(jnp.float32) + fox_forget_b[i].astype(jnp.float32))
        o_a = forgetting_attention(a_qkv[:, :, 0], a_qkv[:, :, 1], a_qkv[:, :, 2], log_f)

        xb, gate_b, gate_c = jnp.split(b_in, 3, axis=-1)
        o_b = gate_b * causal_dwconv(gate_c * xb, shortconv_w[i])

        o_c = chunked_spatial_gating(jax.nn.gelu(c_in), sgu_norm_g[i], sgu_w[i], sgu_b[i])

        d_qkv = d_qkv.reshape(B, S, 3, DIL_HEADS, HEAD_DIM)
        qd = partial_rope(d_qkv[:, :, 0], positions)
        kd = partial_rope(d_qkv[:, :, 1], positions)
        vd = d_qkv[:, :, 2]
        outs, lses = [], []
        for g, (window, dil) in enumerate(DIL_PATTERNS):
            hs = slice(g * DIL_HEADS_PER_GROUP, (g + 1) * DIL_HEADS_PER_GROUP)
            o_g, l_g = dilated_window_attention(qd[:, :, hs], kd[:, :, hs], vd[:, :, hs], window, dil)
            outs.append(o_g)
            lses.append(l_g)
        wts = jax.nn.softmax(jnp.stack(lses, axis=0), axis=0)
        o_d = jnp.sum(wts[..., None] * jnp.stack(outs, axis=0), axis=0).astype(x.dtype).reshape(B, S, DIL_OUT)

        gates = jax.nn.sigmoid(gate_logits).reshape(B, S, N_BRANCH, D_MODEL)
        merged = (gates[:, :, 0] * (o_a @ w_br_fox[i]) + gates[:, :, 1] * (o_b @ w_br_conv[i])
                  + gates[:, :, 2] * (o_c @ w_br_sgu[i]) + gates[:, :, 3] * (o_d @ w_br_dil[i]))
        x = x + merged @ w_out[i]

        h = rms_norm(x, norm_ffn_g[i])
        up = causal_dwconv(h @ w_up[i], ffn_conv_w[i])
        up_gate, up_val = jnp.split(up, 2, axis=-1)
        x = x + (jax.nn.silu(up_gate) * up_val) @ w_down[i]

        ple_gate = jax.nn.sigmoid(rms_norm(x, norm_ple_g[i]) @ w_ple_gate[i])
        x = x + ple_gate * (p[i] @ w_ple_proj[i])
    return rms_norm(x, final_norm_g)
```

```python
import numpy as np
from contextlib import ExitStack
import concourse.bass as bass
import concourse.mybir as mybir
from concourse.bass_utils import run_bass_kernel_spmd

F32 = mybir.dt.float32
BF16 = mybir.dt.bfloat16
I32 = mybir.dt.int32
U8 = mybir.dt.uint8
AF = mybir.ActivationFunctionType
ALU = mybir.AluOpType

DEPTH = 4
NTOK = 1024
EPS = 1e-6
NEG = -30000.0
RG = [[0, 1, 2, 3], [4, 5, 6, 7]]
O_AQ, O_AK, O_AV, O_AF, O_XB, O_GB, O_GC, O_CU, O_CV, O_DQ, O_DK, O_DV, O_GT = (
    0, 512, 1024, 1536, 1544, 2056, 2568, 3080, 3592, 4104, 4872, 5640, 6408)
DFF = 5632
XB_KF, XB_VF, XB_KD, XB_VD, XBW = 0, 4096, 4096 + 8192, 4096 + 8192 + 6144, 4096 + 8192 + 6144 + 12288
XFW = 72
ENGS = ("pe", "act", "dve", "pool", "sp")


class T:
    __slots__ = ("w", "r")

    def __init__(self):
        self.w = None
        self.r = {}


class DSem:
    def __init__(self, sem, key):
        self.sem = sem
        self.key = key
        self.count = 0
        self.last = None


class Sched:
    def __init__(self, nc, stack, ndma=72):
        self.nc = nc
        self.eh = {"pe": nc.tensor, "act": nc.scalar, "dve": nc.vector, "pool": nc.gpsimd, "sp": nc.sync}
        self.sem = {e: stack.enter_context(nc.semaphore("s_" + e)) for e in ENGS}
        self.cnt = {e: 0 for e in ENGS}
        self.known = {e: {} for e in ENGS}
        self.snap = {}
        self.stack = stack
        self.nds = 0
        self.pool_ds = [self.dsem() for _ in range(ndma)]
        self.rr = 0
        self.recent = {}
        self.last_barrier = []
        self.ninstr = 0
        self.nwaits = 0

    def dsem(self):
        self.nds += 1
        return DSem(self.stack.enter_context(self.nc.semaphore("d%d" % self.nds)), "d%d" % self.nds)

    def _wait(self, eng, ticket):
        semobj, key, val = ticket
        kn = self.known[eng]
        if kn.get(key, 0) >= val:
            return
        self.nwaits += 1
        self.eh[eng].wait_ge(semobj, val)
        kn[key] = val
        sn = self.snap.get((key, val))
        if sn:
            for k, v in sn.items():
                if kn.get(k, 0) < v:
                    kn[k] = v

    def _deps(self, eng, reads, writes):
        for t in reads:
            if t.w is not None:
                self._wait(eng, t.w)
        for t in writes:
            if t.w is not None:
                self._wait(eng, t.w)
            for tk in t.r.values():
                self._wait(eng, tk)

    def _commit(self, ticket, reads, writes):
        key = ticket[1]
        for t in reads:
            t.r[key] = ticket
        for t in writes:
            t.w = ticket
            t.r = {}

    def op(self, eng, fns, reads=(), writes=(), arena=False):
        if not isinstance(fns, (list, tuple)):
            fns = [fns]
        if arena:
            for tk in self.last_barrier:
                self._wait(eng, tk)
        self._deps(eng, reads, writes)
        self.cnt[eng] += 1
        c = self.cnt[eng]
        s = self.sem[eng]
        if eng == "pe":
            self.known[eng][eng] = c
        ticket = (s, eng, c)
        self.snap[(eng, c)] = dict(self.known[eng])
        eh = self.eh[eng]
        for fn in fns[:-1]:
            fn(eh)
        fns[-1](eh).then_inc(s, 1)
        self.ninstr += len(fns)
        self._commit(ticket, reads, writes)
        return ticket

    def dma(self, eng, fn, reads=(), writes=(), dsem=None, arena=False, inc=16, track=True):
        if dsem is None:
            dsem = self.pool_ds[self.rr % len(self.pool_ds)]
            self.rr += 1
        if dsem.last is not None:
            self._wait(eng, dsem.last)
        if arena:
            for tk in self.last_barrier:
                self._wait(eng, tk)
        self._deps(eng, reads, writes)
        dsem.count += inc
        ticket = (dsem.sem, dsem.key, dsem.count)
        dsem.last = ticket
        self.snap[(dsem.key, dsem.count)] = dict(self.known[eng])
        if inc == 16:
            fn(self.eh[eng]).then_inc(dsem.sem, 16)
        else:
            fn(self.eh[eng]).then_inc(dsem.sem)
        self.ninstr += 1
        if track:
            self.recent[dsem.key] = ticket
        self._commit(ticket, reads, writes)
        return ticket

    def barrier(self):
        grp = ("pe", "act", "dve", "sp")
        tks = [(self.sem[e], e, self.cnt[e]) for e in grp if self.cnt[e] > 0]
        tks += list(self.recent.values())
        self.recent = {}
        for e in grp:
            for tk in tks:
                if tk[1] != e:
                    self._wait(e, tk)
        pk = (self.sem["pool"], "pool", self.cnt["pool"])
        if self.cnt["pool"] > 0:
            for e in grp:
                self._wait(e, pk)
        self.last_barrier = tks

    def wait_ticket(self, eng, ticket):
        self._wait(eng, ticket)

    def emit(self, block):
        m = {"pe": block.tensor, "act": block.scalar, "dve": block.vector, "pool": block.gpsimd, "sp": block.sync}
        for e in ENGS:
            lst = self.ops[e]
            if not lst:
                continue

            def body(eh, lst=lst):
                for f in lst:
                    f(eh)
            m[e](body)


def slab_b(w, kc, n):
    return np.ascontiguousarray(w.reshape(kc, 128, n, 128).transpose(2, 1, 0, 3)).reshape(n, 128, kc * 128)


def slab_a(w, kc, n):
    return np.ascontiguousarray(w.reshape(kc, 128, n).transpose(1, 0, 2)).reshape(128, kc * n)


def cols16(v):
    sh = v.shape
    k = sh[-1] // 128
    a = v.reshape(sh[:-1] + (k, 128))
    return np.ascontiguousarray(np.moveaxis(a, -1, 0))


STAGE_OF = {"w_k": 1, "w_v": 1, "w_f": 2, "w_cb": 3, "w_dv": 4, "w_dk": 4, "w_dq": 5, "w_q": 5, "w_cv": 8, "w_cu": 8,
            "sgug": 8, "sguwT": 8, "sgub": 8, "w_g": 9, "w_bf": 9, "w_bc": 9, "w_bs": 9, "w_bd": 9, "w_o": 9,
            "w_up": 10, "w_dn": 10, "w_pg": 11, "w_pp": 11, "pT": 11}


def input_specs(depth, stop=99):
    sp = input_specs_all(depth)
    return {k: v for k, v in sp.items() if STAGE_OF.get(k, 0) <= stop}


def input_specs_all(depth):
    L = depth
    sp = {
        "xT": ([128, 16, NTOK], F32), "pT": ([L, 128, 2 * NTOK], F32), "pos": ([1, NTOK], I32),
        "gcols": ([128, L * 3 * 16], F32), "gfin": ([128, 16], F32), "foxb": ([8, L], F32),
        "scw": ([128, L * 12], F32), "sgug": ([L, 128, 512], F32), "sguwT": ([L, 128, 512], F32),
        "sgub": ([L, 1, 512], F32), "fcw": ([128, L * 3 * 88], F32),
        "w_q": ([L, 4, 128, 2048], F32), "w_k": ([L, 4, 128, 2048], F32), "w_v": ([L, 2, 128, 4096], F32),
        "w_f": ([L, 128, 128], F32), "w_cb": ([L, 12, 128, 2048], F32), "w_cu": ([L, 4, 128, 2048], F32),
        "w_cv": ([L, 2, 128, 4096], F32), "w_dq": ([L, 6, 128, 2048], F32), "w_dk": ([L, 6, 128, 2048], F32),
        "w_dv": ([L, 3, 128, 4096], F32), "w_g": ([L, 64, 128, 2048], F32),
        "w_bf": ([L, 16, 128, 512], F32), "w_bc": ([L, 16, 128, 512], F32), "w_bs": ([L, 16, 128, 512], F32),
        "w_bd": ([L, 16, 128, 256], F32), "w_o": ([L, 16, 128, 2048], F32), "w_up": ([L, 88, 128, 2048], F32),
        "w_dn": ([L, 64, 128, 1408], F32), "w_pg": ([L, 16, 128, 2048], F32), "w_pp": ([L, 16, 128, 256], F32),
        "cm": ([128, 8 * 128], F32), "flags": ([128, 16], F32), "flg2": ([128, 48], F32), "ropec": ([128, 4], F32),
    }
    return sp


def host_consts():
    k = np.arange(128)[:, None]
    q = np.arange(128)[None, :]
    tri_le = (k <= q).astype(np.float32)
    tri_ge = (k >= q).astype(np.float32)
    same = ((k % 2) == (q % 2)).astype(np.float32)
    bd_le = same * ((k // 2) <= (q // 2))
    bd_ge = same * ((k // 2) >= (q // 2))
    ones = np.ones((128, 128), np.float32)
    ident = np.eye(128, dtype=np.float32)
    psw = np.zeros((128, 128), np.float32)
    for m in range(128):
        r = m % 64
        if r < 8:
            psw[m + 8, m] = 1.0
        elif r < 16:
            psw[m - 8, m] = 1.0
    cm = np.concatenate([tri_le, tri_ge, bd_le, same, bd_ge, ones, ident, psw], axis=1).astype(np.float32)
    ropec = np.zeros((128, 4), np.float32)
    half = 8
    inv = (500000.0 ** (-np.arange(half, dtype=np.float32) * (2.0 / 16))).astype(np.float32)
    for m in range(128):
        r = m % 64
        if r < 16:
            ropec[m, 0] = inv[r % 8]
            ropec[m, 1] = -1.0 if r < 8 else 1.0
    return cm, ropec


def prep_inputs(inp, depth):
    L = depth
    f = lambda a: np.asarray(a, dtype=np.float32)
    x = f(inp["x"])
    p = f(inp["p"])
    pos = np.asarray(inp["positions"]).astype(np.int32)
    w_in = f(inp["w_in"])
    sh = {}
    sh["gcols"] = np.ascontiguousarray(np.stack([cols16(f(inp[k])[:L]) for k in ("norm_mix_g", "norm_ffn_g", "norm_ple_g")], axis=2)
                                       ).reshape(128, L * 3 * 16)
    sh["gfin"] = cols16(f(inp["final_norm_g"]))
    sh["foxb"] = np.ascontiguousarray(f(inp["fox_forget_b"])[:L].T)
    sh["scw"] = np.ascontiguousarray(cols16(f(inp["shortconv_w"])[:L])).reshape(128, L * 12)
    sh["sgug"] = np.ascontiguousarray(np.broadcast_to(f(inp["sgu_norm_g"])[:L, None, :], (L, 128, 512)))
    sh["sguwT"] = np.ascontiguousarray(f(inp["sgu_w"])[:L].transpose(0, 3, 1, 2)).reshape(L, 128, 512)
    sh["sgub"] = np.ascontiguousarray(f(inp["sgu_b"])[:L]).reshape(L, 1, 512)
    fc = f(inp["ffn_conv_w"])[:L]
    fcc = cols16(fc)
    order = np.empty(88, np.int64)
    order[0::2] = np.arange(44)
    order[1::2] = 44 + np.arange(44)
    sh["fcw"] = np.ascontiguousarray(fcc[..., order]).reshape(128, L * 3 * 88)
    W = {k: [] for k in ("w_q", "w_k", "w_v", "w_f", "w_cb", "w_cu", "w_cv", "w_dq", "w_dk", "w_dv", "w_g", "w_bf", "w_bc",
                         "w_bs", "w_bd", "w_o", "w_up", "w_dn", "w_pg", "w_pp")}
    for l in range(L):
        wi = w_in[l]
        W["w_q"].append(slab_b(wi[:, O_AQ:O_AQ + 512], 16, 4))
        W["w_k"].append(slab_b(wi[:, O_AK:O_AK + 512], 16, 4))
        W["w_v"].append(np.stack([slab_a(wi[:, O_AV + i * 256:O_AV + (i + 1) * 256], 16, 256) for i in range(2)]))
        wf = np.zeros((128, 128), np.float32)
        wf[:, :] = slab_a(wi[:, O_AF:O_AF + 8], 16, 8)
        W["w_f"].append(wf)
        cb = []
        for j in range(4):
            for o in (O_XB, O_GC, O_GB):
                cb.append(slab_b(wi[:, o + j * 128:o + (j + 1) * 128], 16, 1)[0])
        W["w_cb"].append(np.stack(cb))
        W["w_cu"].append(slab_b(wi[:, O_CU:O_CU + 512], 16, 4))
        W["w_cv"].append(np.stack([slab_a(wi[:, O_CV + i * 256:O_CV + (i + 1) * 256], 16, 256) for i in range(2)]))
        W["w_dq"].append(slab_b(wi[:, O_DQ:O_DQ + 768], 16, 6))
        W["w_dk"].append(slab_b(wi[:, O_DK:O_DK + 768], 16, 6))
        W["w_dv"].append(np.stack([slab_a(wi[:, O_DV + g * 256:O_DV + (g + 1) * 256], 16, 256) for g in range(3)]))
        W["w_g"].append(slab_b(wi[:, O_GT:O_GT + 8192], 16, 64))
        W["w_bf"].append(slab_b(f(inp["w_br_fox"])[l], 4, 16))
        W["w_bc"].append(slab_b(f(inp["w_br_conv"])[l], 4, 16))
        W["w_bs"].append(slab_b(f(inp["w_br_sgu"])[l], 4, 16))
        W["w_bd"].append(slab_b(f(inp["w_br_dil"])[l], 2, 16))
        W["w_o"].append(slab_b(f(inp["w_out"])[l], 16, 16))
        wu = f(inp["w_up"])[l]
        su = slab_b(wu, 16, 88)
        W["w_up"].append(np.ascontiguousarray(su[order]))
        wd = f(inp["w_down"])[l]
        W["w_dn"].append(np.ascontiguousarray(wd.reshape(4, 11, 128, 16, 128).transpose(0, 3, 2, 1, 4)).reshape(64, 128, 1408))
        W["w_pg"].append(slab_b(f(inp["w_ple_gate"])[l], 16, 16))
        W["w_pp"].append(slab_b(f(inp["w_ple_proj"])[l], 2, 16))
    for k in W:
        sh[k] = np.stack(W[k])
    cm, ropec = host_consts()
    sh["cm"] = cm
    sh["ropec"] = ropec
    maps = []
    for c in range(8):
        b, R = c // 4, c % 4
        m = dict(sh)
        xs = x[b, R * NTOK:(R + 1) * NTOK, :]
        m["xT"] = np.ascontiguousarray(xs.T.reshape(16, 128, NTOK).transpose(1, 0, 2))
        ps = p[:L, b, R * NTOK:(R + 1) * NTOK, :]
        m["pT"] = np.ascontiguousarray(ps.transpose(0, 2, 1).reshape(L, 2, 128, NTOK).transpose(0, 2, 1, 3)).reshape(L, 128, 2 * NTOK)
        m["pos"] = np.ascontiguousarray(pos[b:b + 1, R * NTOK:(R + 1) * NTOK])
        fl = np.zeros((128, 16), np.float32)
        for r in range(3):
            fl[:, r] = 1.0 if r < R else 0.0
            fl[:, 3 + r] = 1.0 if r == R - 1 else 0.0
            fl[:, 6 + r] = 1.0 if r == R - 2 else 0.0
            fl[:, 9 + r] = 0.0 if r < R else NEG
        m["flags"] = fl
        f2 = np.zeros((128, 48), np.float32)
        for r in range(3):
            f2[:, r * 8:(r + 1) * 8] = fl[0, r]
            f2[:, 24 + r * 8:24 + (r + 1) * 8] = fl[0, 9 + r]
        m["flg2"] = f2
        maps.append(m)
    return maps


class Prog:
    def __init__(self, depth, taps=(), stop=99):
        self.L = depth
        self.taps = taps
        self.stop = stop
        self.nc = nc = bass.Bass("TRN2", target_bir_lowering=False)
        self.st = st = ExitStack()
        self.din = {}
        for name, (shape, dt) in input_specs(depth, stop).items():
            self.din[name] = nc.dram_tensor(name, shape, dt, kind="ExternalInput").ap()
        self.yT = nc.dram_tensor("yT", [128, 16, NTOK], F32, kind="ExternalOutput").ap()
        self.tap_out = {}
        for name, shape in taps:
            self.tap_out[name] = nc.dram_tensor(name, shape, F32, kind="ExternalOutput").ap()
        self.XW = [4096, 4096, 4096, 3072, 3072, 4096, 4096, 4096]
        self.xbs = [nc.dram_tensor("xbs%d" % i, [128, w], BF16) for i, w in enumerate(self.XW)]
        self.xbd = [nc.dram_tensor("xbd%d" % i, [512, w], BF16) for i, w in enumerate(self.XW)]
        self.rope_dram = nc.dram_tensor("rope_dram", [128, 2048], F32)
        self.Trope = T()
        self.Tbs = [T(), T()]
        self.xf_src = nc.dram_tensor("xf_src", [128, XFW], F32)
        self.xf_dst = nc.dram_tensor("xf_dst", [512, XFW], F32)
        self.xh_src = nc.dram_tensor("xh_src", [128, 32], BF16)
        self.xh_dst = nc.dram_tensor("xh_dst", [512, 32], BF16)
        sb = lambda n, s, d: st.enter_context(nc.sbuf_tensor(n, s, d))
        self.xT = sb("xT_sb", [128, 16, NTOK], F32)
        self.ring = sb("ring", [128, 4, 2048], BF16)
        self.rstd = sb("rstd", [128, NTOK], F32)
        self.cmb = sb("cmb", [128, 6 * 128], BF16)
        self.pswb = sb("pswb", [128, 128], BF16)
        self.ident = sb("ident", [128, 128], F32)
        self.mh = sb("mh", [128, 6 * 128], BF16)
        self.flags = sb("flags_sb", [128, 16], F32)
        self.flg2 = sb("flg2_sb", [128, 48], F32)
        self.onesf = sb("onesf", [128, 128], F32)
        self.gcols = sb("gcols_sb", [128, depth * 48], F32)
        self.gfin = sb("gfin_sb", [128, 16], F32)
        self.scw = sb("scw_sb", [128, depth * 12], F32)
        self.fcw = sb("fcw_sb", [128, depth * 264], F32)
        self.negb = sb("negb", [8, depth], F32)
        self.ropec = sb("ropec_sb", [128, 4], F32)
        self.cst = sb("cst", [128, 4], F32)
        self.nfk = sb("nfk", [128, 64], F32)
        self.biask = sb("biask", [128, 4 * 64], F32)
        self.dtmp = sb("dtmp", [128, 64], F32)
        self.zp = sb("zp", [128, 16], F32)
        self.gb2 = sb("gb2", [128, 8], F32)
        self.zl = sb("zl", [128, 8], F32)
        self.hcand = sb("hcand", [128, 96], BF16)
        self.hhalo = sb("hhalo", [128, 32], BF16)
        self.ARENA = 114944
        self.arena = sb("arena", [128, self.ARENA], U8)
        self.ps = [st.enter_context(nc.psum_tensor("ps%d" % i, [128, 512], F32)) for i in range(8)]
        self.Tps = [T() for _ in range(8)]
        self.S = Sched(nc, st)
        self.block = st.enter_context(nc.Block())
        self.ring_T = [T() for _ in range(4)]
        self.ring_ds = [self.S.dsem() for _ in range(4)]
        self.ring_i = 0
        self.deferred = []
        self.cc_ds = self.S.dsem()
        self.TxT = [T() for _ in range(16)]
        self.ThT = [T() for _ in range(16)]
        self.Tc = T()
        self.Trstd = T()
        self.pair_i = 0
        self.bank_i = 0
        self.dil_bs = 0
        self.Txbs = [T() for _ in range(8)]
        self.Txbd = [T() for _ in range(8)]
        self.Txfs = T()
        self.Txfd = T()
        self.Txhs = T()
        self.Txhd = T()

    def av(self, off, nbytes, dt, pat=None, **kw):
        assert off + nbytes <= self.ARENA, (off, nbytes)
        ap = self.arena[:, off:off + nbytes].bitcast(dt)
        if pat:
            ap = ap.rearrange(pat, **kw)
        return ap

    def load_w(self, src, n):
        i = self.ring_i % 4
        self.ring_i += 1
        dst = self.ring[:, i, 0:n]
        self.S.dma("pool", lambda e: e.dma_start(out=dst, in_=src), writes=[self.ring_T[i]], dsem=self.ring_ds[i], track=False)
        self.tick_deferred()
        return dst, self.ring_T[i]

    def defer(self, fn, n=3):
        self.deferred.append([n, fn])

    def tick_deferred(self, flush=False):
        keep = []
        for it in self.deferred:
            it[0] -= 1
            if it[0] <= 0 or flush:
                it[1]()
            else:
                keep.append(it)
        self.deferred = keep

    def load_big(self, src, dst, Tdst, n, arena=True):
        s3 = src.rearrange("p (a b) -> p a b", b=2048)
        d3 = dst.rearrange("p (a b) -> p a b", b=2048)
        self.S.dma("pool", lambda e: e.dma_start(out=d3, in_=s3), writes=[Tdst], arena=arena, track=arena)

    def next_pair(self, npairs=2):
        i = (self.pair_i % npairs) * 2
        self.pair_i += 1
        return i, i + 1

    def proj_b(self, lhs_fn, Tw, kc_n, rhs_fn, rhs_T, banks, m=128, halves=(0, 1)):
        S = self.S
        for hi, half in enumerate(halves):
            b = banks[hi]
            fns = []
            for kc in range(kc_n):
                fns.append(lambda e, b=b, kc=kc, half=half: e.matmul(
                    self.ps[b][0:m, :], lhs_fn(kc), rhs_fn(kc, half), start=(kc == 0), stop=(kc == kc_n - 1)))
            S.op("pe", fns, reads=[Tw] + list(rhs_T), writes=[self.Tps[b]])

    def rmsnorm(self, gcol_fn, hT, ThT, sq_off, out_f32=None, Tout=None):
        S = self.S
        sq = [self.av(sq_off + i * 2048, 2048, BF16) for i in range(2)]
        Tsq = [T(), T()]
        ones = self.cmb[:, 5 * 128:6 * 128]
        for kc in range(16):
            s = sq[kc % 2]
            S.op("act", lambda e, s=s, kc=kc: e.activation(out=s, in_=self.xT[:, kc, :], func=AF.Square),
                 reads=[self.TxT[kc]], writes=[Tsq[kc % 2]])
            for half in range(2):
                S.op("pe", lambda e, s=s, kc=kc, half=half: e.matmul(
                    self.ps[6 + half][:, :], ones, s[:, half * 512:(half + 1) * 512], start=(kc == 0), stop=(kc == 15)),
                    reads=[Tsq[kc % 2], self.Tc], writes=[self.Tps[6 + half]])
        for half in range(2):
            r = self.rstd[:, half * 512:(half + 1) * 512]
            S.op("act", lambda e, r=r, half=half: e.activation(out=r, in_=self.ps[6 + half][:, :], func=AF.Sqrt,
                                                              bias=self.cst[:, 0:1], scale=1.0 / 2048.0),
                 reads=[self.Tps[6 + half], self.Tc], writes=[self.Trstd])
        S.op("dve", lambda e: e.reciprocal(out=self.rstd[:, :], in_=self.rstd[:, :]), reads=[self.Trstd], writes=[self.Trstd])
        self.apply_norm(gcol_fn, hT, ThT, out_f32, Tout)

    def apply_norm(self, gcol_fn, hT, ThT, out_f32=None, Tout=None, tmp_off=None):
        S = self.S
        tmp = [self.av(tmp_off + i * 4096, 4096, F32) for i in range(2)] if tmp_off is not None else None
        Ttmp = [T(), T()]
        for kc in range(16):
            if out_f32 is None and tmp is not None and kc % 2 == 1:
                q = (kc // 2) % 2
                S.op("pool", lambda e, kc=kc, q=q: e.tensor_tensor(out=tmp[q], in0=self.xT[:, kc, :], in1=self.rstd[:, :], op=ALU.mult),
                     reads=[self.TxT[kc], self.Trstd], writes=[Ttmp[q]], arena=True)
                S.op("act", lambda e, kc=kc, q=q: e.activation(out=hT[:, kc, :], in_=tmp[q], func=AF.Copy, scale=gcol_fn(kc)),
                     reads=[Ttmp[q], self.Tc], writes=[ThT[kc]])
                continue
            if out_f32 is None:
                S.op("dve", lambda e, kc=kc: e.scalar_tensor_tensor(out=hT[:, kc, :], in0=self.xT[:, kc, :], scalar=gcol_fn(kc),
                                                                   in1=self.rstd[:, :], op0=ALU.mult, op1=ALU.mult),
                     reads=[self.TxT[kc], self.Trstd, self.Tc], writes=[ThT[kc]])
            else:
                S.op("dve", lambda e, kc=kc: e.scalar_tensor_tensor(out=out_f32[:, kc, :], in0=self.xT[:, kc, :], scalar=gcol_fn(kc),
                                                                   in1=self.rstd[:, :], op0=ALU.mult, op1=ALU.mult),
                     reads=[self.TxT[kc], self.Trstd, self.Tc], writes=[Tout[kc]])

    def tap(self, name, src_ap, Ts, shape_pat=None):
        if name not in self.tap_out:
            return
        S = self.S
        dst = self.tap_out[name]
        if src_ap.dtype != F32:
            S.barrier()
            tt = T()
            tmp = self.av(self.ARENA - 4096, 4096, F32)
            for a in range(src_ap.shape[1]):
                S.op("dve", lambda e, a=a: e.tensor_copy(out=tmp, in_=src_ap[:, a, :]), reads=Ts, writes=[tt])
                S.dma("sp", lambda e, a=a: e.dma_start(out=dst[:, a, :], in_=tmp), reads=[tt])
            S.barrier()
        else:
            S.dma("sp", lambda e: e.dma_start(out=dst, in_=src_ap), reads=Ts)
            S.barrier()

    def setup(self):
        S, nc, L = self.S, self.nc, self.L
        d = self.din
        for kc in range(16):
            S.dma("sp", lambda e, kc=kc: e.dma_start(out=self.xT[:, kc, :], in_=d["xT"][:, kc, :]), writes=[self.TxT[kc]])
        Tc = self.Tc
        S.dma("pool", lambda e: e.dma_start(out=self.cmb[:, :], in_=d["cm"][:, 0:768]), writes=[Tc])
        S.dma("pool", lambda e: e.dma_start(out=self.pswb[:, :], in_=d["cm"][:, 896:1024]), writes=[Tc])
        S.dma("sp", lambda e: e.dma_start(out=self.ident[:, :], in_=d["cm"][:, 768:896]), writes=[Tc])
        S.dma("sp", lambda e: e.dma_start(out=self.onesf[:, :], in_=d["cm"][:, 640:768]), writes=[Tc])
        for nm, dst in (("flags", self.flags), ("flg2", self.flg2), ("gcols", self.gcols), ("gfin", self.gfin),
                        ("scw", self.scw), ("fcw", self.fcw), ("ropec", self.ropec)):
            S.dma("sp", lambda e, nm=nm, dst=dst: e.dma_start(out=dst[:, :], in_=d[nm]), writes=[Tc])
        S.dma("sp", lambda e: e.dma_start(out=self.negb[:, :], in_=d["foxb"]), writes=[Tc])
        S.op("dve", lambda e: e.tensor_scalar(out=self.negb[:, :], in0=self.negb[:, :], scalar1=-1.0, scalar2=None, op0=ALU.mult),
             reads=[Tc], writes=[Tc])
        S.op("dve", lambda e: e.memset(self.cst[:, 0:1], EPS), writes=[Tc])
        S.op("dve", lambda e: e.memset(self.cst[:, 1:2], 1.0), writes=[Tc])
        S.op("dve", lambda e: e.memset(self.cst[:, 2:3], 0.0), writes=[Tc])
        for r in range(3):
            S.op("dve", lambda e, r=r: e.tensor_scalar(out=self.mh[:, r * 128:(r + 1) * 128], in0=self.cmb[:, 128:256],
                                                      scalar1=self.flags[:, 3 + r:4 + r], scalar2=None, op0=ALU.mult),
                 reads=[Tc], writes=[Tc])
            S.op("dve", lambda e, r=r: e.tensor_scalar(out=self.mh[:, (3 + r) * 128:(4 + r) * 128], in0=self.cmb[:, 384:512],
                                                      scalar1=self.flags[:, 3 + r:4 + r], scalar2=None, op0=ALU.mult),
                 reads=[Tc], writes=[Tc])
            S.op("dve", lambda e, r=r: e.scalar_tensor_tensor(out=self.mh[:, (3 + r) * 128:(4 + r) * 128], in0=self.cmb[:, 512:640],
                                                             scalar=self.flags[:, 6 + r:7 + r], in1=self.mh[:, (3 + r) * 128:(4 + r) * 128],
                                                             op0=ALU.mult, op1=ALU.add),
                 reads=[Tc], writes=[Tc])
        SC_ = 61440 + 20480
        d = self.din
        Ct = self.av(SC_, 4096, F32)
        Sg = self.av(SC_ + 4096, 4096, F32)
        TC = T()
        posi = self.av(SC_ + 8192, 4096, I32)
        yv = self.av(SC_ + 12288, 4096, F32)
        kf = self.av(SC_ + 16384, 4096, F32)
        g1 = self.av(SC_ + 20480, 4096, F32)
        ki = self.av(SC_ + 24576, 4096, I32)
        Tr = T()
        S.dma("sp", lambda e: e.dma_start(out=posi, in_=d["pos"][0:1, :].partition_broadcast(128).rearrange("p o c -> p (o c)")), writes=[Tr])
        S.op("dve", lambda e: e.tensor_copy(out=yv, in_=posi), reads=[Tr], writes=[Tr])
        S.op("dve", lambda e: e.tensor_scalar(out=yv, in0=yv, scalar1=self.ropec[:, 0:1], scalar2=float(1.0 / (2.0 * np.pi)),
                                              op0=ALU.mult, op1=ALU.mult), reads=[Tr, self.Tc], writes=[Tr])
        for which, dst in ((0, Sg), (1, Ct)):
            if which == 1:
                S.op("dve", lambda e: e.tensor_scalar(out=yv, in0=yv, scalar1=0.25, scalar2=None, op0=ALU.add), reads=[Tr], writes=[Tr])
            S.op("dve", lambda e: e.tensor_copy(out=ki, in_=yv), reads=[Tr], writes=[Tr])
            S.op("dve", lambda e: e.tensor_copy(out=kf, in_=ki), reads=[Tr], writes=[Tr])
            S.op("dve", lambda e: e.tensor_tensor(out=kf, in0=yv, in1=kf, op=ALU.subtract), reads=[Tr], writes=[Tr])
            S.op("dve", lambda e: e.tensor_scalar(out=g1, in0=kf, scalar1=0.5, scalar2=None, op0=ALU.is_gt), reads=[Tr], writes=[Tr])
            S.op("dve", lambda e: e.tensor_tensor(out=kf, in0=kf, in1=g1, op=ALU.subtract), reads=[Tr], writes=[Tr])
            S.op("dve", lambda e: e.tensor_scalar(out=g1, in0=kf, scalar1=-0.5, scalar2=None, op0=ALU.is_lt), reads=[Tr], writes=[Tr])
            S.op("dve", lambda e: e.tensor_tensor(out=kf, in0=kf, in1=g1, op=ALU.add), reads=[Tr], writes=[Tr])
            S.op("act", lambda e, dst=dst: e.activation(out=dst, in_=kf, func=AF.Sin, scale=6.283185), reads=[Tr], writes=[TC])
        S.op("dve", lambda e: e.tensor_scalar(out=Sg, in0=Sg, scalar1=self.ropec[:, 1:2], scalar2=None, op0=ALU.mult),
             reads=[TC, self.Tc], writes=[TC])
        S.dma("sp", lambda e: e.dma_start(out=self.rope_dram.ap(), in_=self.av(SC_, 8192, F32)), reads=[TC], writes=[self.Trope])
        S.barrier()
        S.barrier()

    def layer(self, l):
        S, nc, L = self.S, self.nc, self.L
        d = self.din
        ps, Tps = self.ps, self.Tps
        cmb = self.cmb
        OB, HT, R0 = 0, 28672, 61440
        QF, QD, SC = R0, R0 + 8192, R0 + 20480
        o_a = self.av(OB, 8192, BF16, "p (a b) -> p a b", a=4)
        o_b = self.av(OB + 8192, 8192, BF16, "p (a b) -> p a b", a=4)
        o_c = self.av(OB + 16384, 8192, BF16, "p (a b) -> p a b", a=4)
        o_d = self.av(OB + 24576, 4096, BF16, "p (a b) -> p a b", a=2)
        To = {k: T() for k in ("a", "b", "c", "d")}
        hT = self.av(HT, 32768, BF16, "p (a b) -> p a b", a=16)
        ThT = self.ThT
        gc = lambda gi: (lambda kc: self.gcols[:, l * 48 + gi * 16 + kc: l * 48 + gi * 16 + kc + 1])
        hrhs = lambda kc, half: hT[:, kc, half * 512:(half + 1) * 512]
        xfs = self.xf_src.ap()
        xfd = self.xf_dst.ap()

        def evac_act(dst, b, scale=1.0, reads=(), writes=(), func=AF.Copy, m=128):
            return S.op("act", lambda e: e.activation(out=dst, in_=ps[b][0:m, :], func=func, scale=scale),
                        reads=[Tps[b]] + list(reads), writes=list(writes))

        if l == 0:
            S.barrier()
        self.rmsnorm(gc(0), hT, ThT, OB)
        if l == 0:
            self.tap("t_h", hT, ThT)

        self.late_gathers = []

        def gather(i, n=3):
            self.defer(lambda i=i: S.dma("pool", lambda e: e.collective_compute("AllGather", ALU.bypass, replica_groups=RG, dma_qos="P3",
                                                                                 ins=[self.xbs[i].ap().opt()], outs=[self.xbd[i].ap().opt()]),
                                         reads=[self.Txbs[i]], writes=[self.Txbd[i]], dsem=self.cc_ds, inc=1, track=False), n)
        bigA = self.av(R0, 8192, BF16)
        bigB = self.av(R0 + 8192, 8192, BF16)
        self.load_big(d["w_v"][l, 0], bigB, self.Tbs[1], 4096, arena=False)
        self.load_big(d["w_v"][l, 1], bigA, self.Tbs[0], 4096, arena=False)
        Ct = self.av(SC, 4096, F32)
        Sg = self.av(SC + 4096, 4096, F32)
        TC = T()
        S.dma("sp", lambda e: e.dma_start(out=self.av(SC, 8192, F32), in_=self.rope_dram.ap()), reads=[self.Trope], writes=[TC])
        S.barrier()
        kst = [self.av(SC + 8192 + i * 2048, 2048, BF16) for i in range(2)]
        Tkst = [T(), T()]
        for c in range(4):
            w, Tw = self.load_w(d["w_k"][l, c], 2048)
            pr = self.next_pair()
            self.proj_b(lambda kc, w=w: w[:, kc * 128:(kc + 1) * 128], Tw, 16, hrhs, ThT, pr)
            for half in range(2):
                evac_act(kst[c % 2][:, half * 512:(half + 1) * 512], pr[half], writes=[Tkst[c % 2]])
            S.dma("sp", lambda e, c=c: e.dma_start(out=self.xbs[0].ap()[:, c * 1024:(c + 1) * 1024], in_=kst[c % 2]),
                  reads=[Tkst[c % 2]], writes=[self.Txbs[0]])
        gather(0)
        big = [self.av(R0 + 8192 - i * 8192, 8192, BF16) for i in range(2)]
        Tbig = [self.Tbs[1], self.Tbs[0]]
        vst = self.av(SC + 12288, 16384, BF16, "p (hp t c) -> p hp t c", t=8, hp=4)
        Tvst = T()
        S.op("dve", lambda e: e.memset(vst[:, :, :, 64:192], 1.0), writes=[Tvst])
        bi = 0
        for tt in range(8):
            for pc in range(2):
                b = 4 + bi % 2
                bi += 1
                fns = [lambda e, kc=kc, b=b, tt=tt, pc=pc: e.matmul(ps[b][:, 0:256], hT[:, kc, tt * 128:(tt + 1) * 128],
                                                                big[pc][:, kc * 256:(kc + 1) * 256], start=(kc == 0), stop=(kc == 15))
                       for kc in range(16)]
                S.op("pe", fns, reads=ThT + [Tbig[pc]], writes=[Tps[b]])
                pv4 = ps[b][:, 0:256].rearrange("p (hp hh c) -> p hp hh c", hp=2, hh=2)
                for hh, c0 in ((0, 0), (1, 192)):
                    S.op("dve", lambda e, tt=tt, pc=pc, pv4=pv4, hh=hh, c0=c0: e.tensor_copy(
                        out=vst[:, 2 * pc:2 * pc + 2, tt, c0:c0 + 64], in_=pv4[:, :, hh:hh + 1, :].rearrange("p hp o c -> p hp (o c)")),
                        reads=[Tps[b]], writes=[Tvst])
        for th in range(2):
            S.dma("sp", lambda e, th=th: e.dma_start(out=self.xbs[1 + th].ap(), in_=vst[:, 2 * th:2 * th + 2, :, :].rearrange("p hp t c -> p (hp t c)")),
                  reads=[Tvst], writes=[self.Txbs[1 + th]])
            gather(1 + th, 3 + 8 * th)
        self.load_big(d["w_dv"][l, 0], bigA, self.Tbs[0], 4096, arena=False)
        self.load_big(d["w_dv"][l, 1], bigB, self.Tbs[1], 4096, arena=False)
        S.barrier()
        if self.stop == 1:
            return
        spb = self.av(SC + 8192, 4096, F32)
        NFT = self.av(SC + 12288, 4096, F32)
        onf = self.av(SC + 16384, 4096, F32)
        nfhl = self.av(OB + 16384, 4096, BF16, "p (a b) -> p a b", a=2)
        Tsp, Tnf, Tonf, Tnfhl = T(), T(), T(), T()
        w, Tw = self.load_w(d["w_f"][l], 128)
        pr = self.next_pair()
        self.proj_b(lambda kc, w=w: w[:, kc * 8:(kc + 1) * 8], Tw, 16, hrhs, ThT, pr, m=8)
        for half in range(2):
            S.op("act", lambda e, half=half: e.activation(out=spb[0:8, half * 512:(half + 1) * 512], in_=ps[pr[half]][0:8, :], func=AF.Exp,
                                                          bias=self.negb[0:8, l:l + 1], scale=-1.0),
                 reads=[Tps[pr[half]], self.Tc], writes=[Tsp])
        S.op("act", lambda e: e.activation(out=spb[0:8, :], in_=spb[0:8, :], func=AF.Ln, bias=self.cst[0:8, 1:2], scale=1.0),
             reads=[Tsp, self.Tc], writes=[Tsp])
        S.op("dve", lambda e: e.memset(onf[0:8, :], 1.0), writes=[Tonf])
        S.op("dve", lambda e: e.tensor_tensor_scan(out=NFT[0:8, :], data0=onf[0:8, :], data1=spb[0:8, :], initial=0.0,
                                                   op0=ALU.mult, op1=ALU.add), reads=[Tsp, Tonf], writes=[Tnf])
        S.op("dve", lambda e: e.tensor_scalar(out=nfhl[0:8, 0, :], in0=NFT[0:8, :], scalar1=-1.0, scalar2=None, op0=ALU.mult),
             reads=[Tnf], writes=[Tnfhl])
        S.op("dve", lambda e: e.scalar_tensor_tensor(out=nfhl[0:8, 1, :], in0=NFT[0:8, :], scalar=-1.0, in1=nfhl[0:8, 0, :],
                                                     op0=ALU.mult, op1=ALU.subtract), reads=[Tnf, Tnfhl], writes=[Tnfhl])
        for t in range(8):
            S.op("pe", lambda e, t=t: e.transpose(out=ps[4][:, t * 8:(t + 1) * 8], in_=NFT[0:8, t * 128:(t + 1) * 128],
                                                  identity=self.ident[0:8, 0:8]), reads=[Tnf, self.Tc], writes=[Tps[4]])
        Tnfk = T()
        S.op("dve", lambda e: e.tensor_copy(out=self.nfk[:, :], in_=ps[4][:, 0:64]), reads=[Tps[4]], writes=[Tnfk])
        S.dma("sp", lambda e: e.dma_start(out=xfs[:, 0:64], in_=self.nfk[:, :]), reads=[Tnfk], writes=[self.Txfs])
        S.barrier()
        if self.stop == 2:
            return
        tmpA = [self.av(SC + 8192, 4096, F32) for i in range(2)]
        zbuf = [self.av(SC + 20480 + i * 4112, 4112, F32) for i in range(2)]
        gbt = [self.av(SC + 12288 + i * 4096, 4096, F32) for i in range(2)]
        t1s = [self.av(SC + 28704, 4096, F32) for i in range(2)]
        TtA0, Tt10 = T(), T()
        TtA, Tz, Tg, Tt1 = [TtA0, TtA0], [T(), T()], [T(), T()], [Tt10, Tt10]
        Tzp = T()
        for j in range(4):
            q = j % 2
            w, Tw = self.load_w(d["w_cb"][l, 3 * j + 0], 2048)
            pr = self.next_pair()
            self.proj_b(lambda kc, w=w: w[:, kc * 128:(kc + 1) * 128], Tw, 16, hrhs, ThT, pr)
            for half in range(2):
                evac_act(tmpA[q][:, half * 512:(half + 1) * 512], pr[half], writes=[TtA[q]])
            w, Tw = self.load_w(d["w_cb"][l, 3 * j + 1], 2048)
            pr = self.next_pair()
            self.proj_b(lambda kc, w=w: w[:, kc * 128:(kc + 1) * 128], Tw, 16, hrhs, ThT, pr)
            for half in range(2):
                S.op("dve", lambda e, half=half, q=q, pr=pr: e.tensor_tensor(
                    out=zbuf[q][:, 2 + half * 512:2 + (half + 1) * 512], in0=tmpA[q][:, half * 512:(half + 1) * 512],
                    in1=ps[pr[half]][:, :], op=ALU.mult), reads=[TtA[q], Tps[pr[half]]], writes=[Tz[q]])
            w, Tw = self.load_w(d["w_cb"][l, 3 * j + 2], 2048)
            pr = self.next_pair()
            self.proj_b(lambda kc, w=w: w[:, kc * 128:(kc + 1) * 128], Tw, 16, hrhs, ThT, pr)
            for half in range(2):
                evac_act(gbt[q][:, half * 512:(half + 1) * 512], pr[half], writes=[Tg[q]])
            wc = lambda k, j=j: self.scw[:, l * 12 + k * 4 + j: l * 12 + k * 4 + j + 1]
            S.op("dve", lambda e, q=q, wc=wc: e.tensor_scalar(out=t1s[q][:, 2:1024], in0=zbuf[q][:, 4:1026], scalar1=wc(2), scalar2=None,
                                                           op0=ALU.mult), reads=[Tz[q], self.Tc], writes=[Tt1[q]])
            S.op("dve", lambda e, q=q, wc=wc: e.scalar_tensor_tensor(out=t1s[q][:, 2:1024], in0=zbuf[q][:, 3:1025], scalar=wc(1),
                                                                  in1=t1s[q][:, 2:1024], op0=ALU.mult, op1=ALU.add),
                 reads=[Tz[q], self.Tc, Tt1[q]], writes=[Tt1[q]])
            S.op("dve", lambda e, q=q, wc=wc: e.scalar_tensor_tensor(out=t1s[q][:, 2:1024], in0=zbuf[q][:, 2:1024], scalar=wc(0),
                                                                  in1=t1s[q][:, 2:1024], op0=ALU.mult, op1=ALU.add),
                 reads=[Tz[q], self.Tc, Tt1[q]], writes=[Tt1[q]])
            S.op("dve", lambda e, q=q, j=j: e.tensor_tensor(out=o_b[:, j, 2:1024], in0=t1s[q][:, 2:1024], in1=gbt[q][:, 2:1024], op=ALU.mult),
                 reads=[Tt1[q], Tg[q]], writes=[To["b"]])
            S.op("dve", lambda e, q=q, j=j: e.tensor_copy(out=self.zp[:, 4 * j + 2:4 * j + 4], in_=zbuf[q][:, 2:4]), reads=[Tz[q]], writes=[Tzp])
            S.op("dve", lambda e, q=q, j=j: e.tensor_copy(out=self.gb2[:, 2 * j:2 * j + 2], in_=gbt[q][:, 0:2]), reads=[Tg[q]], writes=[Tzp])
            S.op("dve", lambda e, q=q, j=j: e.tensor_copy(out=self.zl[:, 2 * j:2 * j + 2], in_=zbuf[q][:, 1024:1026]), reads=[Tz[q]], writes=[Tzp])
        S.dma("sp", lambda e: e.dma_start(out=xfs[:, 64:72], in_=self.zl[:, :]), reads=[Tzp], writes=[self.Txfs])
        self.defer(lambda: S.dma("pool", lambda e: e.collective_compute("AllGather", ALU.bypass, replica_groups=RG, dma_qos="P3",
                                                                        ins=[self.xf_src.ap().opt()], outs=[self.xf_dst.ap().opt()]),
                                 reads=[self.Txfs], writes=[self.Txfd], dsem=self.cc_ds, inc=1, track=False))
        S.barrier()
        qraw = [self.av(SC + 8192 + i * 2048, 2048, BF16) for i in range(2)]
        tt2 = [self.av(SC + 12288 + i * 4096, 4096, F32) for i in range(2)]
        u12 = [self.av(SC + 20480 + i * 4096, 4096, F32) for i in range(2)]
        kst2 = [self.av(SC + 28672 + i * 2048, 2048, BF16) for i in range(2)]
        Tq, Ttt2, Tu12, Tk2 = [T(), T()], [T(), T()], [T(), T()], [T(), T()]

        def rope_proj(wname, c, scale, dst, Tdst, cnt):
            dil = (1, 4, 8)[c // 2]
            q = cnt % 2
            tt_, u1, Ttt, Tu1 = tt2[q], u12[q], Ttt2[q], Tu12[q]
            w, Tw = self.load_w(d[wname][l, c], 2048)
            pr = self.next_pair()
            self.proj_b(lambda kc, w=w: w[:, kc * 128:(kc + 1) * 128], Tw, 16, hrhs, ThT, pr)
            for half in range(2):
                evac_act(qraw[q][:, half * 512:(half + 1) * 512], pr[half], scale=scale, writes=[Tq[q]])
            pr2 = self.next_pair()
            for half in range(2):
                S.op("pe", lambda e, half=half, pr2=pr2, q=q: e.matmul(ps[pr2[half]][:, :], self.pswb[:, :], qraw[q][:, half * 512:(half + 1) * 512],
                                                                  start=True, stop=True), reads=[Tq[q], self.Tc], writes=[Tps[pr2[half]]])
            S.op("dve", lambda e, q=q: e.tensor_tensor(out=tt_, in0=qraw[q], in1=Ct, op=ALU.mult), reads=[Tq[q], TC], writes=[Ttt])
            for half in range(2):
                S.op("dve", lambda e, half=half, pr2=pr2: e.tensor_tensor(out=u1[:, half * 512:(half + 1) * 512], in0=ps[pr2[half]][:, :],
                                                                         in1=Sg[:, half * 512:(half + 1) * 512], op=ALU.mult),
                     reads=[Tps[pr2[half]], TC], writes=[Tu1])
            if dil == 1:
                S.op("dve", lambda e: e.tensor_tensor(out=dst, in0=u1, in1=tt_, op=ALU.add), reads=[Tu1, Ttt], writes=[Tdst])
            else:
                S.op("dve", lambda e, dil=dil: e.tensor_tensor(out=dst.rearrange("p (r i) -> p i r", r=dil),
                                                              in0=u1.rearrange("p (i r) -> p i r", r=dil),
                                                              in1=tt_.rearrange("p (i r) -> p i r", r=dil), op=ALU.add),
                     reads=[Tu1, Ttt], writes=[Tdst])

        for c in range(6):
            rope_proj("w_dk", c, 1.0, kst2[c % 2], Tk2[c % 2], c)
            S.dma("sp", lambda e, c=c: e.dma_start(out=self.xbs[3 + c // 3].ap()[:, (c % 3) * 1024:(c % 3 + 1) * 1024], in_=kst2[c % 2]),
                  reads=[Tk2[c % 2]], writes=[self.Txbs[3 + c // 3]])
            if c % 3 == 2:
                self.late_gathers.append(3 + c // 3)
        S.barrier()
        bigd = [self.av(R0 + i * 8192, 8192, BF16) for i in range(2)]
        Tbd = self.Tbs
        vdst = [self.av(SC + 8192 + i * 8192, 8192, BF16, "p (t hp c) -> p t hp c", t=8, hp=2) for i in range(2)]
        Tvd = [T(), T()]
        for i in range(2):
            S.op("dve", lambda e, i=i: e.memset(vdst[i][:, :, :, 64:192], 1.0), writes=[Tvd[i]])
        DIL = (1, 4, 16)
        bi = 0
        for g in range(3):
            dil = DIL[g]
            for tp in range(8):
                b = 4 + bi % 2
                bi += 1

                def lhs(kc, tp=tp, dil=dil):
                    if dil == 1:
                        return hT[:, kc, tp * 128:(tp + 1) * 128]
                    if dil == 4:
                        return hT[:, kc, :].rearrange("p (i r) -> p r i", r=4)[:, tp // 2, (tp % 2) * 128:(tp % 2) * 128 + 128]
                    return hT[:, kc, :].rearrange("p (i r) -> p r i", r=8)[:, tp, :]
                fns = [lambda e, kc=kc, b=b, g=g, lhs=lhs: e.matmul(ps[b][:, 0:256], lhs(kc), bigd[g % 2][:, kc * 256:(kc + 1) * 256],
                                                                    start=(kc == 0), stop=(kc == 15)) for kc in range(16)]
                S.op("pe", fns, reads=ThT + [Tbd[g % 2]], writes=[Tps[b]])
                pv4 = ps[b][:, 0:256].rearrange("p (hp hh c) -> p hp hh c", hp=2, hh=2)
                for hh, c0 in ((0, 0), (1, 192)):
                    S.op("dve", lambda e, tp=tp, g=g, pv4=pv4, hh=hh, c0=c0: e.tensor_copy(
                        out=vdst[g % 2][:, tp, :, c0:c0 + 64], in_=pv4[:, :, hh:hh + 1, :].rearrange("p hp o c -> p hp (o c)")),
                        reads=[Tps[b]], writes=[Tvd[g % 2]])
            S.dma("sp", lambda e, g=g: e.dma_start(out=self.xbs[5 + g].ap(), in_=vdst[g % 2].rearrange("p t hp c -> p (t hp c)")),
                  reads=[Tvd[g % 2]], writes=[self.Txbs[5 + g]])
            self.late_gathers.append(5 + g)
            if g == 0:
                self.load_big(d["w_dv"][l, 2], bigA, self.Tbs[0], 4096, arena=False)
        S.barrier()
        if self.stop == 3:
            return
        if self.stop == 4:
            return
        QdT = self.av(QD, 12288, BF16, "p (a b) -> p a b", a=6)
        QfT = self.av(QF, 8192, BF16, "p (a b) -> p a b", a=4)
        TQd = [T() for _ in range(6)]
        TQf = [T() for _ in range(4)]
        for c in range(6):
            rope_proj("w_dq", c, 0.125, QdT[:, c, :], TQd[c], c)
        for c in range(4):
            w, Tw = self.load_w(d["w_q"][l, c], 2048)
            pr = self.next_pair()
            self.proj_b(lambda kc, w=w: w[:, kc * 128:(kc + 1) * 128], Tw, 16, hrhs, ThT, pr)
            for half in range(2):
                evac_act(QfT[:, c, half * 512:(half + 1) * 512], pr[half], scale=0.125, writes=[TQf[c]])
        self.tick_deferred(flush=True)
        S.barrier()
        if self.stop == 5:
            return
        self.fox_attention(l, QfT, TQf, o_a, To["a"], nfhl, Tnfhl, Tnfk, HT, SC, OB)
        if l == 0:
            self.tap("t_oa", o_a, [To["a"]])
        if self.stop == 6:
            return
        S.barrier()
        self.dil_attention(l, QdT, TQd, o_d, To["d"], HT, SC, QF)
        if l == 0:
            self.tap("t_od", o_d, [To["d"]])
        S.barrier()
        if self.stop == 7:
            return
        self.apply_norm(gc(0), hT, ThT)
        Tdt = T()
        S.dma("sp", lambda e: e.dma_start(out=self.dtmp[:, 0:24].rearrange("p (r c) -> p r c", r=3),
                                          in_=xfd[0:384, 64:72].rearrange("(r p) c -> p r c", p=128)), reads=[self.Txfd], writes=[Tdt])
        zh = self.dtmp[:, 24:32]
        S.op("dve", lambda e: e.tensor_scalar(out=zh, in0=self.dtmp[:, 0:8], scalar1=self.flags[:, 3:4], scalar2=None, op0=ALU.mult),
             reads=[Tdt, self.Tc], writes=[Tdt])
        for r in (1, 2):
            S.op("dve", lambda e, r=r: e.scalar_tensor_tensor(out=zh, in0=self.dtmp[:, 8 * r:8 * r + 8], scalar=self.flags[:, 3 + r:4 + r],
                                                             in1=zh, op0=ALU.mult, op1=ALU.add), reads=[Tdt, self.Tc], writes=[Tdt])
        S.op("dve", lambda e: e.tensor_copy(out=self.zp[:, :].rearrange("p (j k) -> p j k", k=4)[:, :, 0:2],
                                            in_=zh.rearrange("p (j k) -> p j k", k=2)), reads=[Tdt, Tzp], writes=[Tzp])
        o2 = self.dtmp[:, 32:40]
        for j in range(4):
            wc = lambda k, j=j: self.scw[:, l * 12 + k * 4 + j: l * 12 + k * 4 + j + 1]
            oj = o2[:, 2 * j:2 * j + 2]
            S.op("dve", lambda e, j=j, wc=wc, oj=oj: e.tensor_scalar(out=oj, in0=self.zp[:, 4 * j + 2:4 * j + 4], scalar1=wc(2), scalar2=None,
                                                                  op0=ALU.mult), reads=[Tzp, self.Tc], writes=[Tdt])
            S.op("dve", lambda e, j=j, wc=wc, oj=oj: e.scalar_tensor_tensor(out=oj, in0=self.zp[:, 4 * j + 1:4 * j + 3], scalar=wc(1), in1=oj,
                                                                         op0=ALU.mult, op1=ALU.add), reads=[Tzp, self.Tc, Tdt], writes=[Tdt])
            S.op("dve", lambda e, j=j, wc=wc, oj=oj: e.scalar_tensor_tensor(out=oj, in0=self.zp[:, 4 * j + 0:4 * j + 2], scalar=wc(0), in1=oj,
                                                                         op0=ALU.mult, op1=ALU.add), reads=[Tzp, self.Tc, Tdt], writes=[Tdt])
            S.op("dve", lambda e, j=j, oj=oj: e.tensor_tensor(out=o_b[:, j, 0:2], in0=oj, in1=self.gb2[:, 2 * j:2 * j + 2], op=ALU.mult),
                 reads=[Tdt, Tzp], writes=[To["b"]])
        if l == 0:
            self.tap("t_ob", o_b, [To["b"]])
        self.sgu(l, hT, ThT, o_c, To["c"], SC)
        if l == 0:
            self.tap("t_oc", o_c, [To["c"]])
        S.barrier()
        if self.stop == 8:
            return
        merged = self.av(R0, 32768, BF16, "p (a b) -> p a b", a=16)
        Tm = [T() for _ in range(16)]
        sg = [self.av(R0 + 32768 + i * 2048, 2048, F32) for i in range(2)]
        prod = [self.av(R0 + 36864 + i * 2048, 2048, F32) for i in range(2)]
        acc = self.av(R0 + 40960, 4096, F32)
        Tsg, Tpr, Tacc = [T(), T()], [T(), T()], T()
        branches = (("w_bf", 4, o_a, To["a"]), ("w_bc", 4, o_b, To["b"]), ("w_bs", 4, o_c, To["c"]), ("w_bd", 2, o_d, To["d"]))
        n = 0
        for j in range(16):
            for i, (wn, kcn, ot, Tot) in enumerate(branches):
                w, Tw = self.load_w(d["w_g"][l, i * 16 + j], 2048)
                pg = self.next_pair(4)
                self.proj_b(lambda kc, w=w: w[:, kc * 128:(kc + 1) * 128], Tw, 16, hrhs, ThT, pg)
                w2, Tw2 = self.load_w(d[wn][l, j], kcn * 128)
                pb = self.next_pair(4)
                self.proj_b(lambda kc, w2=w2: w2[:, kc * 128:(kc + 1) * 128], Tw2, kcn,
                            lambda kc, half, ot=ot: ot[:, kc, half * 512:(half + 1) * 512], [Tot], pb)
                for half in range(2):
                    q = n % 2
                    n += 1
                    hs = slice(half * 512, (half + 1) * 512)
                    S.op("act", lambda e, q=q, b=pg[half]: e.activation(out=sg[q], in_=ps[b][:, :], func=AF.Sigmoid),
                         reads=[Tps[pg[half]]], writes=[Tsg[q]])
                    if i == 0:
                        S.op("dve", lambda e, q=q, b=pb[half], hs=hs: e.tensor_tensor(out=acc[:, hs], in0=sg[q], in1=ps[b][:, :], op=ALU.mult),
                             reads=[Tsg[q], Tps[pb[half]]], writes=[Tacc])
                    else:
                        S.op("dve", lambda e, q=q, b=pb[half]: e.tensor_tensor(out=prod[q], in0=sg[q], in1=ps[b][:, :], op=ALU.mult),
                             reads=[Tsg[q], Tps[pb[half]]], writes=[Tpr[q]])
                        if i < 3:
                            S.op("dve", lambda e, q=q, hs=hs: e.tensor_tensor(out=acc[:, hs], in0=acc[:, hs], in1=prod[q], op=ALU.add),
                                 reads=[Tpr[q], Tacc], writes=[Tacc])
                        else:
                            S.op("dve", lambda e, q=q, hs=hs, j=j: e.tensor_tensor(out=merged[:, j, hs], in0=acc[:, hs], in1=prod[q], op=ALU.add),
                                 reads=[Tpr[q], Tacc], writes=[Tm[j]])
        for j in range(16):
            w, Tw = self.load_w(d["w_o"][l, j], 2048)
            pr = self.next_pair(4)
            self.proj_b(lambda kc, w=w: w[:, kc * 128:(kc + 1) * 128], Tw, 16,
                        lambda kc, half: merged[:, kc, half * 512:(half + 1) * 512], Tm, pr)
            for half in range(2):
                hs = slice(half * 512, (half + 1) * 512)
                S.op("dve", lambda e, j=j, hs=hs, b=pr[half]: e.tensor_tensor(out=self.xT[:, j, hs], in0=self.xT[:, j, hs], in1=ps[b][:, :], op=ALU.add),
                     reads=[Tps[pr[half]]], writes=[self.TxT[j]])
        if l == 0:
            self.tap("t_x1", self.xT[:, :, :], self.TxT)
        if self.stop == 9:
            return
        self.rmsnorm(gc(1), hT, ThT, R0 + 45056)
        self.ffn(l, hT, ThT, OB, R0)
        S.barrier()
        if l == 0:
            self.tap("t_x2", self.xT[:, :, :], self.TxT)
        if self.stop == 10:
            return
        self.rmsnorm(gc(2), hT, ThT, OB)
        pT = self.av(OB + 20544, 4096, BF16)
        TpT = T()
        self.load_big(d["pT"][l], pT, TpT, 2048)
        sg = [self.av(R0 + 28672 + i * 2048, 2048, F32) for i in range(2)]
        prod = [self.av(R0 + 32768 + i * 2048, 2048, F32) for i in range(2)]
        Tsg, Tpr = [T(), T()], [T(), T()]
        n = 0
        for j in range(16):
            w, Tw = self.load_w(d["w_pg"][l, j], 2048)
            pg = self.next_pair(4)
            self.proj_b(lambda kc, w=w: w[:, kc * 128:(kc + 1) * 128], Tw, 16, hrhs, ThT, pg)
            w2, Tw2 = self.load_w(d["w_pp"][l, j], 256)
            pb = self.next_pair(4)
            self.proj_b(lambda kc, w2=w2: w2[:, kc * 128:(kc + 1) * 128], Tw2, 2,
                        lambda kc, half: pT[:, kc * 1024 + half * 512: kc * 1024 + (half + 1) * 512], [TpT], pb)
            for half in range(2):
                q = n % 2
                n += 1
                hs = slice(half * 512, (half + 1) * 512)
                S.op("act", lambda e, q=q, b=pg[half]: e.activation(out=sg[q], in_=ps[b][:, :], func=AF.Sigmoid),
                     reads=[Tps[pg[half]]], writes=[Tsg[q]])
                S.op("dve", lambda e, q=q, b=pb[half]: e.tensor_tensor(out=prod[q], in0=sg[q], in1=ps[b][:, :], op=ALU.mult),
                     reads=[Tsg[q], Tps[pb[half]]], writes=[Tpr[q]])
                S.op("dve", lambda e, q=q, j=j, hs=hs: e.tensor_tensor(out=self.xT[:, j, hs], in0=self.xT[:, j, hs], in1=prod[q], op=ALU.add),
                     reads=[Tpr[q]], writes=[self.TxT[j]])

    def fox_attention(self, l, QfT, TQf, o_a, Toa, nfhl, Tnfhl, Tnfk, HT, SC, OB):
        S, ps, Tps, cmb = self.S, self.ps, self.Tps, self.cmb
        xfd = self.xf_dst.ap()
        Tb, Tdt = T(), T()
        S.dma("sp", lambda e: e.dma_start(out=self.biask[:, 0:192].rearrange("p (r c) -> p r c", r=3),
                                          in_=xfd[0:384, 0:64].rearrange("(r p) c -> p r c", p=128)), reads=[self.Txfd], writes=[Tb])
        for r in range(3):
            S.dma("sp", lambda e, r=r: e.dma_start(out=self.dtmp[:, 8 * r:8 * r + 8],
                                                   in_=xfd[r * 128 + 127:r * 128 + 128, 56:64].partition_broadcast(128).rearrange("p o c -> p (o c)")),
                  reads=[self.Txfd], writes=[Tdt])
        dt = self.dtmp
        S.op("dve", lambda e: e.tensor_tensor(out=dt[:, 0:24], in0=dt[:, 0:24], in1=self.flg2[:, 0:24], op=ALU.mult), reads=[Tdt, self.Tc], writes=[Tdt])
        S.op("dve", lambda e: e.tensor_tensor(out=dt[:, 8:16], in0=dt[:, 8:16], in1=dt[:, 16:24], op=ALU.add), reads=[Tdt], writes=[Tdt])
        S.op("dve", lambda e: e.tensor_tensor(out=dt[:, 0:8], in0=dt[:, 0:8], in1=dt[:, 8:16], op=ALU.add), reads=[Tdt], writes=[Tdt])
        S.op("dve", lambda e: e.tensor_tensor(out=dt[:, 24:48], in0=self.flg2[:, 24:48], in1=dt[:, 0:24], op=ALU.subtract), reads=[Tdt, self.Tc], writes=[Tdt])
        for r in range(3):
            for j in range(8):
                o = r * 64 + j * 8
                S.op("dve", lambda e, o=o, r=r: e.tensor_tensor(out=self.biask[:, o:o + 8], in0=self.biask[:, o:o + 8], in1=dt[:, 24 + 8 * r:32 + 8 * r],
                                                              op=ALU.add), reads=[Tb, Tdt], writes=[Tb])
        S.op("dve", lambda e: e.tensor_copy(out=self.biask[:, 192:256], in_=self.nfk[:, :]), reads=[Tnfk, Tb], writes=[Tb])
        for i in self.late_gathers:
            S.dma("pool", lambda e, i=i: e.collective_compute("AllGather", ALU.bypass, replica_groups=RG, dma_qos="P3",
                                                              ins=[self.xbs[i].ap().opt()], outs=[self.xbd[i].ap().opt()]),
                  reads=[self.Txbs[i]], writes=[self.Txbd[i]], dsem=self.cc_ds, inc=1, track=False)
        self.late_gathers = []
        vt = [[self.av(HT + (st * 4 + s) * 4096, 4096, BF16, "p (t c) -> p t c", t=8) for s in range(4)] for st in range(2)]
        ktp = [[self.av(SC + (hh * 4 + s) * 2048, 2048, BF16) for s in range(4)] for hh in range(2)]
        kall = self.av(SC, 16384, BF16)
        Tvt = [[T() for _ in range(4)] for _ in range(2)]
        Tkt = [[T() for _ in range(4)] for _ in range(2)]
        Pt = [self.av(SC + 16384 + i * 1024, 1024, BF16) for i in range(6)]
        TPt = [T() for _ in range(6)]
        rden = self.av(SC + 22528, 2048, F32)
        Trd = T()
        Qtmp = [self.av(OB + 16384 + 4096 + i * 2048, 2048, BF16) for i in range(2)]
        qall = self.av(OB + 16384 + 4096, 4096, BF16)
        TQt = [T(), T()]
        allk = [t for row in Tkt for t in row]
        S.op("dve", lambda e: e.memset(kall[64:128, :], 0.0), writes=allk)
        S.op("dve", lambda e: e.memset(kall[64:66, :], 1.0), writes=allk)
        S.op("dve", lambda e: e.memset(qall[64:128, :], 0.0), writes=TQt)
        LAG = 3

        def srcT(i, s):
            if s < 3:
                return self.xbd[i].ap()[s * 128:(s + 1) * 128, :], self.Txbd[i]
            return self.xbs[i].ap(), self.Txbs[i]

        def load_k(head):
            c, hh = head // 2, head % 2
            for s in range(4):
                ks, Tks = srcT(0, s)
                S.dma("sp", lambda e, ks=ks, c=c, hh=hh, s=s: e.dma_start(out=ktp[hh][s][0:64, :], in_=ks[64 * hh:64 * hh + 64, c * 1024:(c + 1) * 1024]),
                      reads=[Tks], writes=[Tkt[hh][s]])

        def load_v(c):
            st = c % 2
            for s in range(4):
                vs, Tvs = srcT(1 + c // 2, s)
                S.dma("sp", lambda e, vs=vs, c=c, st=st, s=s: e.dma_start(
                    out=vt[st][s], in_=vs[:, (c % 2) * 2048:(c % 2 + 1) * 2048].rearrange("p (t c) -> p t c", t=8)),
                    reads=[Tvs], writes=[Tvt[st][s]])

        load_k(0)
        load_v(0)
        for head in range(8):
            c, hh = head // 2, head % 2
            st = c % 2
            pb = 64 * hh
            ob = 64 - pb
            if head + 1 < 8:
                load_k(head + 1)
            if hh == 0 and c + 1 < 4:
                load_v(c + 1)
            qt = Qtmp[head % 2]
            S.op("act", lambda e, qt=qt, pb=pb, c=c: e.activation(out=qt[0:64, :], in_=QfT[pb:pb + 64, c, :], func=AF.Copy),
                 reads=[TQf[c]], writes=[TQt[head % 2]])
            S.dma("sp", lambda e, qt=qt, head=head: e.dma_start(out=qt[64:65, :], in_=nfhl[head:head + 1, 0, :]),
                  reads=[Tnfhl], writes=[TQt[head % 2]])
            S.dma("sp", lambda e, qt=qt, head=head: e.dma_start(out=qt[65:66, :], in_=nfhl[head:head + 1, 1, :]),
                  reads=[Tnfhl], writes=[TQt[head % 2]])
            for qh in range(2):
                steps = [(s, j, 0) for s in range(3) for j in range(8)] + [(3, j, max(0, j - 4 * qh) * 128) for j in range(4 * qh + 4)]
                bo = 4 + (self.bank_i % 4)
                self.bank_i += 1
                pend = []

                def pv(i, s, j, c0, bp, bo=bo, st=st, hh=hh, nsteps=len(steps)):
                    S.op("pe", lambda e: e.matmul(ps[bo][:, c0:512], vt[st][s][:, j, hh * 128:(hh + 1) * 128], Pt[bp][:, c0:512],
                                                  start=(i == 0), stop=(i == nsteps - 1)),
                         reads=[Tvt[st][s], TPt[bp]], writes=[Tps[bo]])
                for i, (s, j, c0) in enumerate(steps):
                    bs = i % 4
                    bp = i % 6
                    q0 = qh * 512 + c0
                    q1 = (qh + 1) * 512
                    S.op("pe", lambda e, bs=bs, c0=c0, s=s, j=j, q0=q0, q1=q1: e.matmul(
                        ps[bs][:, c0:512], ktp[hh][s][:, j * 128:(j + 1) * 128], qt[:, q0:q1], start=True, stop=True),
                        reads=[Tkt[hh][s], TQt[head % 2]], writes=[Tps[bs]])
                    bcol = s * 64 + j * 8 + head
                    S.op("act", lambda e, bs=bs, c0=c0, bcol=bcol: e.activation(out=Pt[bp][:, c0:512], in_=ps[bs][:, c0:512], func=AF.Exp,
                                                                               bias=self.biask[:, bcol:bcol + 1], scale=1.0),
                         reads=[Tps[bs], Tb], writes=[TPt[bp]])
                    if s == 3 and j >= 4 * qh:
                        S.op("dve", lambda e, bs=bs, c0=c0: e.tensor_tensor(out=Pt[bp][:, c0:c0 + 128], in0=Pt[bp][:, c0:c0 + 128],
                                                                           in1=cmb[:, 0:128], op=ALU.mult), reads=[TPt[bp], self.Tc], writes=[TPt[bp]])
                    pend.append((i, s, j, c0, bp))
                    if len(pend) > LAG:
                        pv(*pend.pop(0))
                while pend:
                    pv(*pend.pop(0))
                S.op("dve", lambda e, bo=bo: e.reciprocal(out=rden[pb:pb + 64, 0:512], in_=ps[bo][ob:ob + 64, :]), reads=[Tps[bo]], writes=[Trd])
                S.op("dve", lambda e, bo=bo, qh=qh: e.tensor_tensor(out=o_a[pb:pb + 64, c, qh * 512:(qh + 1) * 512], in0=ps[bo][pb:pb + 64, :],
                                                                   in1=rden[pb:pb + 64, 0:512], op=ALU.mult), reads=[Tps[bo], Trd], writes=[Toa])

    def dil_attention(self, l, QdT, TQd, o_d, Tod, HT, SC, QF):
        S, ps, Tps, cmb, mh = self.S, self.ps, self.Tps, self.cmb, self.mh
        kd = [self.av(SC + s * 4096, 4096, BF16, "p (a b) -> p a b", a=2) for s in range(4)]
        vd = [self.av(HT + s * 8192, 8192, BF16, "p (t hp c) -> p t hp c", t=8, hp=2) for s in range(4)]
        Tkd = [T() for _ in range(4)]
        Tvd = [T() for _ in range(4)]
        Uacc = self.av(SC + 16384, 8192, F32, "p (a b) -> p a b", a=2)
        Dacc = self.av(SC + 24576, 8192, F32, "p (a b) -> p a b", a=2)
        TU = T()
        Pt = [self.av(QF + i * 512, 512, BF16) for i in range(4)]
        TPt = [T() for _ in range(4)]
        Qz = [self.av(QF + 2048 + i * 2048, 2048, BF16) for i in range(2)]
        qzall = self.av(QF + 2048, 4096, BF16)
        TQz = [T(), T()]
        S.op("dve", lambda e: e.memset(qzall, 0.0), writes=TQz)
        DIL = (1, 4, 16)
        LAG = 2
        for g in range(3):
            dil = DIL[g]
            for s in range(4):
                def srcT(i, s=s):
                    if s < 3:
                        return self.xbd[i].ap()[s * 128:(s + 1) * 128, :], self.Txbd[i]
                    return self.xbs[i].ap(), self.Txbs[i]
                for cc_ in range(2):
                    c6 = 2 * g + cc_
                    ks, Tks = srcT(3 + c6 // 3)
                    S.dma("sp", lambda e, ks=ks, c6=c6, cc_=cc_, s=s: e.dma_start(out=kd[s][:, cc_, :], in_=ks[:, (c6 % 3) * 1024:(c6 % 3 + 1) * 1024]),
                          reads=[Tks], writes=[Tkd[s]])
                vs, Tvs = srcT(5 + g)
                vsrc = vs.rearrange("p (t hp c) -> p t hp c", t=8, hp=2)
                S.dma("sp", lambda e, vsrc=vsrc, s=s: e.dma_start(out=vd[s], in_=vsrc), reads=[Tvs], writes=[Tvd[s]])
            def head_blocks(j):
                cc, hh = j // 2, j % 2
                pb = 64 * hh
                chunk = 2 * g + cc
                qz = Qz[hh]
                S.op("dve", lambda e, qz=qz, pb=pb, chunk=chunk: e.tensor_copy(out=qz[pb:pb + 64, :], in_=QdT[pb:pb + 64, chunk, :]),
                     reads=[TQd[chunk]], writes=[TQz[hh]])
                blocks = {0: [], 1: []}
                if dil == 1:
                    for kc in range(8):
                        qb = kc // 4
                        if kc % 4 != 3:
                            blocks[qb].append((3, 128 * kc, kc, 128 * kc, 256, cmb[:, 0:256]))
                        else:
                            blocks[qb].append((3, 128 * kc, kc, 128 * kc, 128, cmb[:, 0:128]))
                            if kc < 7:
                                blocks[qb + 1].append((3, 128 * kc, kc, 128 * kc + 128, 128, cmb[:, 128:256]))
                    for r in range(3):
                        blocks[0].append((r, 896, 7, 0, 128, mh[:, r * 128:(r + 1) * 128]))
                elif dil == 4:
                    for r in range(4):
                        qb = r // 2
                        base = 256 * r
                        blocks[qb].append((3, base, 2 * r, base, 256, cmb[:, 0:256]))
                        blocks[qb].append((3, base + 128, 2 * r + 1, base + 128, 128, cmb[:, 0:128]))
                        for rr in range(3):
                            blocks[qb].append((rr, base + 128, 2 * r + 1, base, 128, mh[:, rr * 128:(rr + 1) * 128]))
                else:
                    for rp in range(8):
                        qb = rp // 4
                        base = 128 * rp
                        blocks[qb].append((3, base, rp, base, 128, cmb[:, 256:384]))
                        for rr in range(3):
                            blocks[qb].append((rr, base, rp, base, 128, mh[:, (3 + rr) * 128:(4 + rr) * 128]))
                return blocks

            def make_stream(j, qb, steps):
                cc, hh = j // 2, j % 2
                pb = 64 * hh
                ob = 64 - pb
                qz = Qz[hh]
                bo = 4 + (self.bank_i % 4)
                self.bank_i += 1
                st = {"i": 0, "n": len(steps), "pend": []}

                def pv(i, s, vtile, q0, nq, bs):
                    S.op("pe", lambda e: e.matmul(ps[bo][:, q0 - qb * 512:q0 - qb * 512 + nq], vd[s][:, vtile, j // 2, (j % 2) * 128:(j % 2) * 128 + 128],
                                                  Pt[bs][:, 0:nq], start=(i == 0), stop=(i == len(steps) - 1)),
                         reads=[Tvd[s], TPt[bs]], writes=[Tps[bo]])

                def step():
                    i = st["i"]
                    (s, k0, vtile, q0, nq, mask) = steps[i]
                    bs = self.dil_bs % 4
                    self.dil_bs += 1
                    S.op("pe", lambda e: e.matmul(ps[bs][:, 0:nq], kd[s][:, cc, k0:k0 + 128], qz[:, q0:q0 + nq], start=True, stop=True),
                         reads=[Tkd[s], TQz[hh]], writes=[Tps[bs]])
                    S.op("act", lambda e: e.activation(out=Pt[bs][:, 0:nq], in_=ps[bs][:, 0:nq], func=AF.Exp), reads=[Tps[bs]], writes=[TPt[bs]])
                    S.op("dve", lambda e: e.tensor_tensor(out=Pt[bs][:, 0:nq], in0=Pt[bs][:, 0:nq], in1=mask[:, 0:nq], op=ALU.mult),
                         reads=[TPt[bs], self.Tc], writes=[TPt[bs]])
                    st["pend"].append((i, s, vtile, q0, nq, bs))
                    if len(st["pend"]) > 1:
                        pv(*st["pend"].pop(0))
                    st["i"] += 1

                def finish():
                    while st["pend"]:
                        pv(*st["pend"].pop(0))
                    for (acc, rows) in ((Uacc, pb), (Dacc, ob)):
                        if dil == 1:
                            dv = acc[rows:rows + 64, cc, qb * 512:(qb + 1) * 512]
                            sv = ps[bo][rows:rows + 64, :]
                        elif dil == 4:
                            dv = acc[rows:rows + 64, cc, :].rearrange("p (i r) -> p r i", r=4)[:, 2 * qb:2 * qb + 2, :]
                            sv = ps[bo][rows:rows + 64, :].rearrange("p (r i) -> p r i", r=2)
                        else:
                            dv = acc[rows:rows + 64, cc, :].rearrange("p (i r) -> p r i", r=8)[:, 4 * qb:4 * qb + 4, :]
                            sv = ps[bo][rows:rows + 64, :].rearrange("p (r i) -> p r i", r=4)
                        if g == 0:
                            S.op("dve", lambda e, dv=dv, sv=sv: e.tensor_copy(out=dv, in_=sv), reads=[Tps[bo]], writes=[TU])
                        else:
                            S.op("dve", lambda e, dv=dv, sv=sv: e.tensor_tensor(out=dv, in0=dv, in1=sv, op=ALU.add), reads=[Tps[bo], TU], writes=[TU])
                st["step"], st["finish"] = step, finish
                return st

            for jp in (0, 2):
                blk = [head_blocks(jp), head_blocks(jp + 1)]
                for qb in range(2):
                    streams = [make_stream(jp + k, qb, blk[k][qb]) for k in range(2)]
                    active = list(streams)
                    while active:
                        for st in list(active):
                            if st["i"] < st["n"]:
                                st["step"]()
                            else:
                                st["finish"]()
                                active.remove(st)
        S.barrier()
        rtmp = self.av(QF + 4096, 4096, F32)
        Trt = T()
        for cc in range(2):
            for hh in range(2):
                pb = 64 * hh
                ob = 64 - pb
                S.op("dve", lambda e, cc=cc, pb=pb, ob=ob: e.reciprocal(out=rtmp[pb:pb + 64, :], in_=Dacc[ob:ob + 64, cc, :]), reads=[TU], writes=[Trt])
                S.op("dve", lambda e, cc=cc, pb=pb: e.tensor_tensor(out=o_d[pb:pb + 64, cc, :], in0=Uacc[pb:pb + 64, cc, :], in1=rtmp[pb:pb + 64, :], op=ALU.mult),
                     reads=[TU, Trt], writes=[Tod])

    def sgu(self, l, hT, ThT, o_c, Toc, SC):
        S, ps, Tps, cmb, d = self.S, self.ps, self.Tps, self.cmb, self.din
        S.barrier()
        big = [self.av(SC + i * 8192, 8192, BF16) for i in range(2)]
        Tbig = [T(), T()]
        vg = self.av(SC + 16384, 2048, F32)
        vn = self.av(SC + 18432, 4096, BF16, "p (a b) -> p a b", a=4)
        ut = self.av(SC + 22528, 2048, F32)
        wsT = self.av(SC + 24576, 2048, F32)
        wsb = self.av(SC + 26624, 1024, BF16)
        gt = self.av(SC + 27648, 2048, F32)
        sb_ = self.av(SC + 29696, 2048, F32)
        ssc = self.av(SC + 31744, 32, F32)
        Tvg, Tvn, Tut, Tw_, Tss = T(), T(), T(), T(), T()
        for pc in range(2):
            self.load_big(d["w_cv"][l, pc], big[pc], Tbig[pc], 4096)
        S.dma("sp", lambda e: e.dma_start(out=wsT, in_=d["sguwT"][l]), writes=[Tw_])
        S.dma("sp", lambda e: e.dma_start(out=gt, in_=d["sgug"][l]), writes=[Tw_])
        S.dma("sp", lambda e: e.dma_start(out=sb_[0:1, :], in_=d["sgub"][l]), writes=[Tw_])
        for g in range(4):
            S.op("dve", lambda e, g=g: e.tensor_tensor(out=wsb[:, g * 128:(g + 1) * 128], in0=wsT[:, g * 128:(g + 1) * 128], in1=cmb[:, 0:128], op=ALU.mult),
                 reads=[Tw_, self.Tc], writes=[Tw_])
        hrhs1 = lambda kc, half: hT[:, kc, half * 512:(half + 1) * 512]
        bi = 0
        for half in range(2):
            for t4 in range(4):
                tt = half * 4 + t4
                b = 4 + bi % 2
                bi += 1
                for pc in range(2):
                    fns = [lambda e, kc=kc, b=b, tt=tt, pc=pc: e.matmul(ps[b][:, pc * 256:(pc + 1) * 256], hT[:, kc, tt * 128:(tt + 1) * 128],
                                                                    big[pc][:, kc * 256:(kc + 1) * 256], start=(kc == 0), stop=(kc == 15))
                           for kc in range(16)]
                    S.op("pe", fns, reads=ThT + [Tbig[pc]], writes=[Tps[b]])
                S.op("act", lambda e, b=b: e.activation(out=vg, in_=ps[b][:, :], func=AF.Gelu_apprx_tanh), reads=[Tps[b]], writes=[Tvg])
                S.op("dve", lambda e, t4=t4: e.scalar_tensor_tensor(out=vn[:, t4, :], in0=vg, scalar=1.0, in1=vg, op0=ALU.mult, op1=ALU.mult,
                                                                   accum_out=ssc[:, 0:1]), reads=[Tvg], writes=[Tvn, Tss])
                S.op("act", lambda e: e.activation(out=ssc[:, 1:2], in_=ssc[:, 0:1], func=AF.Sqrt, bias=self.cst[:, 0:1], scale=1.0 / 512.0),
                     reads=[Tss, self.Tc], writes=[Tss])
                S.op("dve", lambda e: e.reciprocal(out=ssc[:, 1:2], in_=ssc[:, 1:2]), reads=[Tss], writes=[Tss])
                S.op("dve", lambda e, t4=t4: e.scalar_tensor_tensor(out=vn[:, t4, :], in0=vg, scalar=ssc[:, 1:2], in1=gt, op0=ALU.mult, op1=ALU.mult),
                     reads=[Tvg, Tss, Tw_], writes=[Tvn])
            for g in range(4):
                w, Tw = self.load_w(d["w_cu"][l, g], 2048)
                bu = self.next_pair()[0]
                self.proj_b(lambda kc, w=w: w[:, kc * 128:(kc + 1) * 128], Tw, 16, hrhs1, ThT, (bu,), halves=(half,))
                S.op("act", lambda e, bu=bu: e.activation(out=ut, in_=ps[bu][:, :], func=AF.Gelu_apprx_tanh), reads=[Tps[bu]], writes=[Tut])
                bm = 6 + g % 2
                for t4 in range(4):
                    S.op("pe", [lambda e, t4=t4, g=g, bm=bm: e.matmul(ps[bm][:, t4 * 128:(t4 + 1) * 128], vn[:, t4, g * 128:(g + 1) * 128],
                                                                   wsb[:, g * 128:(g + 1) * 128], start=True, stop=False),
                                lambda e, t4=t4, g=g, bm=bm: e.matmul(ps[bm][:, t4 * 128:(t4 + 1) * 128], self.onesf[0:1, 0:128],
                                                                   sb_[0:1, g * 128:(g + 1) * 128], start=False, stop=True)],
                         reads=[Tvn, Tw_, self.Tc], writes=[Tps[bm]])
                S.op("dve", lambda e, g=g, bm=bm, half=half: e.tensor_tensor(out=o_c[:, g, half * 512:(half + 1) * 512], in0=ut, in1=ps[bm][:, :], op=ALU.mult),
                     reads=[Tut, Tps[bm]], writes=[Toc])

    def ffn(self, l, hT, ThT, OB, R0):
        S, ps, Tps, d = self.S, self.ps, self.Tps, self.din
        xhs, xhd = self.xh_src.ap(), self.xh_dst.ap()
        S.dma("sp", lambda e: e.dma_start(out=xhs.rearrange("p (k t) -> p k t", t=2), in_=hT[:, :, 1022:1024]), reads=ThT, writes=[self.Txhs])
        S.dma("pool", lambda e: e.collective_compute("AllGather", ALU.bypass, replica_groups=RG, dma_qos="P3",
                                                     ins=[self.xh_src.ap().opt()], outs=[self.xh_dst.ap().opt()]),
              reads=[self.Txhs], writes=[self.Txhd], dsem=self.cc_ds, inc=1, track=False)
        Th = T()
        S.dma("sp", lambda e: e.dma_start(out=self.hcand[:, :].rearrange("p (r c) -> p r c", r=3),
                                          in_=xhd[0:384, :].rearrange("(r p) c -> p r c", p=128)), reads=[self.Txhd], writes=[Th])
        S.op("dve", lambda e: e.tensor_scalar(out=self.hhalo[:, :], in0=self.hcand[:, 0:32], scalar1=self.flags[:, 3:4], scalar2=None, op0=ALU.mult),
             reads=[Th, self.Tc], writes=[Th])
        for r in (1, 2):
            S.op("dve", lambda e, r=r: e.scalar_tensor_tensor(out=self.hhalo[:, :], in0=self.hcand[:, 32 * r:32 * r + 32], scalar=self.flags[:, 3 + r:4 + r],
                                                             in1=self.hhalo[:, :], op0=ALU.mult, op1=ALU.add), reads=[Th, self.Tc], writes=[Th])
        S.barrier()
        act = [self.av(R0 + i * 22528, 22528, BF16, "p (a b) -> p a b", a=11) for i in range(2)]
        Tact = [[T() for _ in range(11)] for _ in range(2)]
        t12 = [self.av(R0 + 45056 + i * 4096, 4096, F32) for i in range(2)]
        Tt12 = [T(), T()]
        bufs = [[self.av(OB + (k * 2 + i) * 4112, 4112, F32) for i in range(2)] for k in range(2)]
        Tbuf = [[T(), T()], [T(), T()]]
        sl = self.av(OB + 16448, 4096, F32)
        Tsl = T()
        hs_i = 0
        dn_i = 0
        hrhs = lambda kc, half: hT[:, kc, half * 512:(half + 1) * 512]
        for gk in range(4):
            for mm in range(11):
                m = gk * 11 + mm
                q = m % 2
                for k in range(2):
                    si = 2 * m + k
                    w, Tw = self.load_w(d["w_up"][l, si], 2048)
                    pr = self.next_pair()
                    self.proj_b(lambda kc, w=w: w[:, kc * 128:(kc + 1) * 128], Tw, 16, hrhs, ThT, pr)
                    hsl = (hs_i % 256) * 2
                    hs_i += 1
                    fns = [lambda e, kc=kc, w=w, hsl=hsl: e.matmul(ps[7][:, hsl:hsl + 2], w[:, kc * 128:(kc + 1) * 128], self.hhalo[:, 2 * kc:2 * kc + 2],
                                                                start=(kc == 0), stop=(kc == 15)) for kc in range(16)]
                    S.op("pe", fns, reads=[Tw, Th], writes=[Tps[7]])
                    bf = bufs[k][q]
                    for half in range(2):
                        S.op("act", lambda e, bf=bf, half=half, b=pr[half]: e.activation(out=bf[:, 2 + half * 512:2 + (half + 1) * 512], in_=ps[b][:, :], func=AF.Copy),
                             reads=[Tps[pr[half]]], writes=[Tbuf[k][q]])
                    S.op("dve", lambda e, bf=bf, hsl=hsl: e.tensor_copy(out=bf[:, 0:2], in_=ps[7][:, hsl:hsl + 2]), reads=[Tps[7]], writes=[Tbuf[k][q]])
                    wc = lambda tap, si=si: self.fcw[:, l * 264 + tap * 88 + si: l * 264 + tap * 88 + si + 1]
                    tk = t12[k]
                    S.op("dve", lambda e, bf=bf, tk=tk, wc=wc: e.tensor_scalar(out=tk, in0=bf[:, 2:1026], scalar1=wc(2), scalar2=None, op0=ALU.mult),
                         reads=[Tbuf[k][q], self.Tc], writes=[Tt12[k]])
                    S.op("dve", lambda e, bf=bf, tk=tk, wc=wc: e.scalar_tensor_tensor(out=tk, in0=bf[:, 1:1025], scalar=wc(1), in1=tk, op0=ALU.mult, op1=ALU.add),
                         reads=[Tbuf[k][q], self.Tc, Tt12[k]], writes=[Tt12[k]])
                    S.op("dve", lambda e, bf=bf, tk=tk, wc=wc: e.scalar_tensor_tensor(out=tk, in0=bf[:, 0:1024], scalar=wc(0), in1=tk, op0=ALU.mult, op1=ALU.add),
                         reads=[Tbuf[k][q], self.Tc, Tt12[k]], writes=[Tt12[k]])
                S.op("act", lambda e: e.activation(out=sl, in_=t12[0], func=AF.Silu), reads=[Tt12[0]], writes=[Tsl])
                S.op("dve", lambda e, gk=gk, mm=mm: e.tensor_tensor(out=act[gk % 2][:, mm, :], in0=sl, in1=t12[1], op=ALU.mult),
                     reads=[Tsl, Tt12[1]], writes=[Tact[gk % 2][mm]])
            for j in range(16):
                w, Tw = self.load_w(d["w_dn"][l, gk * 16 + j], 1408)
                for half in range(2):
                    b = 4 + dn_i % 3
                    dn_i += 1
                    fns = [lambda e, kc=kc, w=w, b=b, half=half, gk=gk: e.matmul(ps[b][:, :], w[:, kc * 128:(kc + 1) * 128],
                                                                              act[gk % 2][:, kc, half * 512:(half + 1) * 512],
                                                                              start=(kc == 0), stop=(kc == 10)) for kc in range(11)]
                    S.op("pe", fns, reads=[Tw] + Tact[gk % 2], writes=[Tps[b]])
                    hs = slice(half * 512, (half + 1) * 512)
                    S.op("dve", lambda e, j=j, hs=hs, b=b: e.tensor_tensor(out=self.xT[:, j, hs], in0=self.xT[:, j, hs], in1=ps[b][:, :], op=ALU.add),
                         reads=[Tps[b]], writes=[self.TxT[j]])

    def finish(self):
        S = self.S
        self.tick_deferred(flush=True)
        S.barrier()
        out = self.av(0, 65536, F32, "p (a b) -> p a b", a=16)
        Tout = [T() for _ in range(16)]
        self.rmsnorm(lambda kc: self.gfin[:, kc:kc + 1], None, None, 65536 + 8192, out_f32=out, Tout=Tout)
        tks = []
        for kc in range(16):
            tks.append(S.dma("sp", lambda e, kc=kc: e.dma_start(out=self.yT[:, kc, :], in_=out[:, kc, :]), reads=[Tout[kc]]))
        for tk in tks:
            S.wait_ticket("sp", tk)
        for tk in S.recent.values():
            S.wait_ticket("sp", tk)

    def build(self):
        self.setup()
        for l in range(self.L):
            self.layer(l)
        self.finish()
        self.st.close()
        return self.nc


TAPS = (("t_h", [128, 16, NTOK]), ("t_oa", [128, 4, NTOK]), ("t_od", [128, 2, NTOK]), ("t_ob", [128, 4, NTOK]),
        ("t_oc", [128, 4, NTOK]), ("t_x1", [128, 16, NTOK]), ("t_x2", [128, 16, NTOK]))


def run(inputs, depth=DEPTH, taps=(), n_cores=8, stop=99):
    maps = prep_inputs(inputs, depth)
    prog = Prog(depth, taps, stop)
    nc = prog.build()
    keep = set(input_specs(depth, stop).keys())
    maps = [{k: v for k, v in m.items() if k in keep} for m in maps]
    res = run_bass_kernel_spmd(nc, maps[:n_cores], core_ids=list(range(n_cores)))
    return res.results


def assemble(results, name="yT"):
    y = np.empty((2, 4096, results[0][name].shape[1] * 128), np.float32)
    for c in range(8):
        b, R = c // 4, c % 4
        t = results[c][name]
        y[b, R * NTOK:(R + 1) * NTOK, :] = t.transpose(2, 1, 0).reshape(NTOK, -1)
    return y


def kernel(**inputs):
    results = run(inputs, DEPTH)
    return assemble(results)
```

```python
import numpy as np
from contextlib import ExitStack
import concourse.bass as bass
import concourse.mybir as mybir
from concourse.bass_utils import run_bass_kernel_spmd

F32 = mybir.dt.float32
BF16 = mybir.dt.bfloat16
I32 = mybir.dt.int32
U8 = mybir.dt.uint8
AF = mybir.ActivationFunctionType
ALU = mybir.AluOpType

DEPTH = 4
NTOK = 1024
EPS = 1e-6
NEG = -30000.0
RG = [[0, 1, 2, 3], [4, 5, 6, 7]]
O_AQ, O_AK, O_AV, O_AF, O_XB, O_GB, O_GC, O_CU, O_CV, O_DQ, O_DK, O_DV, O_GT = (
    0, 512, 1024, 1536, 1544, 2056, 2568, 3080, 3592, 4104, 4872, 5640, 6408)
DFF = 5632
XB_KF, XB_VF, XB_KD, XB_VD, XBW = 0, 4096, 4096 + 8192, 4096 + 8192 + 6144, 4096 + 8192 + 6144 + 12288
XFW = 72
ENGS = ("pe", "act", "dve", "pool", "sp")


class T:
    __slots__ = ("w", "r")

    def __init__(self):
        self.w = None
        self.r = {}


class DSem:
    def __init__(self, sem, key):
        self.sem = sem
        self.key = key
        self.count = 0
        self.last = None


class Sched:
    def __init__(self, nc, stack, ndma=72):
        self.nc = nc
        self.eh = {"pe": nc.tensor, "act": nc.scalar, "dve": nc.vector, "pool": nc.gpsimd, "sp": nc.sync}
        self.sem = {e: stack.enter_context(nc.semaphore("s_" + e)) for e in ENGS}
        self.cnt = {e: 0 for e in ENGS}
        self.known = {e: {} for e in ENGS}
        self.snap = {}
        self.stack = stack
        self.nds = 0
        self.pool_ds = [self.dsem() for _ in range(ndma)]
        self.rr = 0
        self.recent = {}
        self.last_barrier = []
        self.ninstr = 0
        self.nwaits = 0

    def dsem(self):
        self.nds += 1
        return DSem(self.stack.enter_context(self.nc.semaphore("d%d" % self.nds)), "d%d" % self.nds)

    def _wait(self, eng, ticket):
        semobj, key, val = ticket
        kn = self.known[eng]
        if kn.get(key, 0) >= val:
            return
        self.nwaits += 1
        self.eh[eng].wait_ge(semobj, val)
        kn[key] = val
        sn = self.snap.get((key, val))
        if sn:
            for k, v in sn.items():
                if kn.get(k, 0) < v:
                    kn[k] = v

    def _deps(self, eng, reads, writes):
        for t in reads:
            if t.w is not None:
                self._wait(eng, t.w)
        for t in writes:
            if t.w is not None:
                self._wait(eng, t.w)
            for tk in t.r.values():
                self._wait(eng, tk)

    def _commit(self, ticket, reads, writes):
        key = ticket[1]
        for t in reads:
            t.r[key] = ticket
        for t in writes:
            t.w = ticket
            t.r = {}

    def op(self, eng, fns, reads=(), writes=(), arena=False):
        if not isinstance(fns, (list, tuple)):
            fns = [fns]
        if arena:
            for tk in self.last_barrier:
                self._wait(eng, tk)
        self._deps(eng, reads, writes)
        self.cnt[eng] += 1
        c = self.cnt[eng]
        s = self.sem[eng]
        if eng == "pe":
            self.known[eng][eng] = c
        ticket = (s, eng, c)
        self.snap[(eng, c)] = dict(self.known[eng])
        eh = self.eh[eng]
        for fn in fns[:-1]:
            fn(eh)
        fns[-1](eh).then_inc(s, 1)
        self.ninstr += len(fns)
        self._commit(ticket, reads, writes)
        return ticket

    def dma(self, eng, fn, reads=(), writes=(), dsem=None, arena=False, inc=16, track=True):
        if dsem is None:
            dsem = self.pool_ds[self.rr % len(self.pool_ds)]
            self.rr += 1
        if dsem.last is not None:
            self._wait(eng, dsem.last)
        if arena:
            for tk in self.last_barrier:
                self._wait(eng, tk)
        self._deps(eng, reads, writes)
        dsem.count += inc
        ticket = (dsem.sem, dsem.key, dsem.count)
        dsem.last = ticket
        self.snap[(dsem.key, dsem.count)] = dict(self.known[eng])
        if inc == 16:
            fn(self.eh[eng]).then_inc(dsem.sem, 16)
        else:
            fn(self.eh[eng]).then_inc(dsem.sem)
        self.ninstr += 1
        if track:
            self.recent[dsem.key] = ticket
        self._commit(ticket, reads, writes)
        return ticket

    def barrier(self):
        grp = ("pe", "act", "dve", "sp")
        tks = [(self.sem[e], e, self.cnt[e]) for e in grp if self.cnt[e] > 0]
        tks += list(self.recent.values())
        self.recent = {}
        for e in grp:
            for tk in tks:
                if tk[1] != e:
                    self._wait(e, tk)
        pk = (self.sem["pool"], "pool", self.cnt["pool"])
        if self.cnt["pool"] > 0:
            for e in grp:
                self._wait(e, pk)
        self.last_barrier = tks

    def wait_ticket(self, eng, ticket):
        self._wait(eng, ticket)

    def emit(self, block):
        m = {"pe": block.tensor, "act": block.scalar, "dve": block.vector, "pool": block.gpsimd, "sp": block.sync}
        for e in ENGS:
            lst = self.ops[e]
            if not lst:
                continue

            def body(eh, lst=lst):
                for f in lst:
                    f(eh)
            m[e](body)


def slab_b(w, kc, n):
    return np.ascontiguousarray(w.reshape(kc, 128, n, 128).transpose(2, 1, 0, 3)).reshape(n, 128, kc * 128)


def slab_a(w, kc, n):
    return np.ascontiguousarray(w.reshape(kc, 128, n).transpose(1, 0, 2)).reshape(128, kc * n)


def cols16(v):
    sh = v.shape
    k = sh[-1] // 128
    a = v.reshape(sh[:-1] + (k, 128))
    return np.ascontiguousarray(np.moveaxis(a, -1, 0))


STAGE_OF = {"w_k": 1, "w_v": 1, "w_f": 2, "w_cb": 3, "w_dv": 4, "w_dk": 4, "w_dq": 5, "w_q": 5, "w_cv": 8, "w_cu": 8,
            "sgug": 8, "sguwT": 8, "sgub": 8, "w_g": 9, "w_bf": 9, "w_bc": 9, "w_bs": 9, "w_bd": 9, "w_o": 9,
            "w_up": 10, "w_dn": 10, "w_pg": 11, "w_pp": 11, "pT": 11}


def input_specs(depth, stop=99):
    sp = input_specs_all(depth)
    return {k: v for k, v in sp.items() if STAGE_OF.get(k, 0) <= stop}


def input_specs_all(depth):
    L = depth
    sp = {
        "xT": ([128, 16, NTOK], F32), "pT": ([L, 128, 2 * NTOK], F32), "pos": ([1, NTOK], I32),
        "gcols": ([128, L * 3 * 16], F32), "gfin": ([128, 16], F32), "foxb": ([8, L], F32),
        "scw": ([128, L * 12], F32), "sgug": ([L, 128, 512], F32), "sguwT": ([L, 128, 512], F32),
        "sgub": ([L, 1, 512], F32), "fcw": ([128, L * 3 * 88], F32),
        "w_q": ([L, 4, 128, 2048], F32), "w_k": ([L, 4, 128, 2048], F32), "w_v": ([L, 2, 128, 4096], F32),
        "w_f": ([L, 128, 128], F32), "w_cb": ([L, 12, 128, 2048], F32), "w_cu": ([L, 4, 128, 2048], F32),
        "w_cv": ([L, 2, 128, 4096], F32), "w_dq": ([L, 6, 128, 2048], F32), "w_dk": ([L, 6, 128, 2048], F32),
        "w_dv": ([L, 3, 128, 4096], F32), "w_g": ([L, 64, 128, 2048], F32),
        "w_bf": ([L, 16, 128, 512], F32), "w_bc": ([L, 16, 128, 512], F32), "w_bs": ([L, 16, 128, 512], F32),
        "w_bd": ([L, 16, 128, 256], F32), "w_o": ([L, 16, 128, 2048], F32), "w_up": ([L, 88, 128, 2048], F32),
        "w_dn": ([L, 64, 128, 1408], F32), "w_pg": ([L, 16, 128, 2048], F32), "w_pp": ([L, 16, 128, 256], F32),
        "cm": ([128, 8 * 128], F32), "flags": ([128, 16], F32), "flg2": ([128, 48], F32), "ropec": ([128, 4], F32),
    }
    return sp


def host_consts():
    k = np.arange(128)[:, None]
    q = np.arange(128)[None, :]
    tri_le = (k <= q).astype(np.float32)
    tri_ge = (k >= q).astype(np.float32)
    same = ((k % 2) == (q % 2)).astype(np.float32)
    bd_le = same * ((k // 2) <= (q // 2))
    bd_ge = same * ((k // 2) >= (q // 2))
    ones = np.ones((128, 128), np.float32)
    ident = np.eye(128, dtype=np.float32)
    psw = np.zeros((128, 128), np.float32)
    for m in range(128):
        r = m % 64
        if r < 8:
            psw[m + 8, m] = 1.0
        elif r < 16:
            psw[m - 8, m] = 1.0
    cm = np.concatenate([tri_le, tri_ge, bd_le, same, bd_ge, ones, ident, psw], axis=1).astype(np.float32)
    ropec = np.zeros((128, 4), np.float32)
    half = 8
    inv = (500000.0 ** (-np.arange(half, dtype=np.float32) * (2.0 / 16))).astype(np.float32)
    for m in range(128):
        r = m % 64
        if r < 16:
            ropec[m, 0] = inv[r % 8]
            ropec[m, 1] = -1.0 if r < 8 else 1.0
    return cm, ropec


def prep_inputs(inp, depth):
    L = depth
    f = lambda a: np.asarray(a, dtype=np.float32)
    x = f(inp["x"])
    p = f(inp["p"])
    pos = np.asarray(inp["positions"]).astype(np.int32)
    w_in = f(inp["w_in"])
    sh = {}
    sh["gcols"] = np.ascontiguousarray(np.stack([cols16(f(inp[k])[:L]) for k in ("norm_mix_g", "norm_ffn_g", "norm_ple_g")], axis=2)
                                       ).reshape(128, L * 3 * 16)
    sh["gfin"] = cols16(f(inp["final_norm_g"]))
    sh["foxb"] = np.ascontiguousarray(f(inp["fox_forget_b"])[:L].T)
    sh["scw"] = np.ascontiguousarray(cols16(f(inp["shortconv_w"])[:L])).reshape(128, L * 12)
    sh["sgug"] = np.ascontiguousarray(np.broadcast_to(f(inp["sgu_norm_g"])[:L, None, :], (L, 128, 512)))
    sh["sguwT"] = np.ascontiguousarray(f(inp["sgu_w"])[:L].transpose(0, 3, 1, 2)).reshape(L, 128, 512)
    sh["sgub"] = np.ascontiguousarray(f(inp["sgu_b"])[:L]).reshape(L, 1, 512)
    fc = f(inp["ffn_conv_w"])[:L]
    fcc = cols16(fc)
    order = np.empty(88, np.int64)
    order[0::2] = np.arange(44)
    order[1::2] = 44 + np.arange(44)
    sh["fcw"] = np.ascontiguousarray(fcc[..., order]).reshape(128, L * 3 * 88)
    W = {k: [] for k in ("w_q", "w_k", "w_v", "w_f", "w_cb", "w_cu", "w_cv", "w_dq", "w_dk", "w_dv", "w_g", "w_bf", "w_bc",
                         "w_bs", "w_bd", "w_o", "w_up", "w_dn", "w_pg", "w_pp")}
    for l in range(L):
        wi = w_in[l]
        W["w_q"].append(slab_b(wi[:, O_AQ:O_AQ + 512], 16, 4))
        W["w_k"].append(slab_b(wi[:, O_AK:O_AK + 512], 16, 4))
        W["w_v"].append(np.stack([slab_a(wi[:, O_AV + i * 256:O_AV + (i + 1) * 256], 16, 256) for i in range(2)]))
        wf = np.zeros((128, 128), np.float32)
        wf[:, :] = slab_a(wi[:, O_AF:O_AF + 8], 16, 8)
        W["w_f"].append(wf)
        cb = []
        for j in range(4):
            for o in (O_XB, O_GC, O_GB):
                cb.append(slab_b(wi[:, o + j * 128:o + (j + 1) * 128], 16, 1)[0])
        W["w_cb"].append(np.stack(cb))
        W["w_cu"].append(slab_b(wi[:, O_CU:O_CU + 512], 16, 4))
        W["w_cv"].append(np.stack([slab_a(wi[:, O_CV + i * 256:O_CV + (i + 1) * 256], 16, 256) for i in range(2)]))
        W["w_dq"].append(slab_b(wi[:, O_DQ:O_DQ + 768], 16, 6))
        W["w_dk"].append(slab_b(wi[:, O_DK:O_DK + 768], 16, 6))
        W["w_dv"].append(np.stack([slab_a(wi[:, O_DV + g * 256:O_DV + (g + 1) * 256], 16, 256) for g in range(3)]))
        W["w_g"].append(slab_b(wi[:, O_GT:O_GT + 8192], 16, 64))
        W["w_bf"].append(slab_b(f(inp["w_br_fox"])[l], 4, 16))
        W["w_bc"].append(slab_b(f(inp["w_br_conv"])[l], 4, 16))
        W["w_bs"].append(slab_b(f(inp["w_br_sgu"])[l], 4, 16))
        W["w_bd"].append(slab_b(f(inp["w_br_dil"])[l], 2, 16))
        W["w_o"].append(slab_b(f(inp["w_out"])[l], 16, 16))
        wu = f(inp["w_up"])[l]
        su = slab_b(wu, 16, 88)
        W["w_up"].append(np.ascontiguousarray(su[order]))
        wd = f(inp["w_down"])[l]
        W["w_dn"].append(np.ascontiguousarray(wd.reshape(4, 11, 128, 16, 128).transpose(0, 3, 2, 1, 4)).reshape(64, 128, 1408))
        W["w_pg"].append(slab_b(f(inp["w_ple_gate"])[l], 16, 16))
        W["w_pp"].append(slab_b(f(inp["w_ple_proj"])[l], 2, 16))
    for k in W:
        sh[k] = np.stack(W[k])
    cm, ropec = host_consts()
    sh["cm"] = cm
    sh["ropec"] = ropec
    maps = []
    for c in range(8):
        b, R = c // 4, c % 4
        m = dict(sh)
        xs = x[b, R * NTOK:(R + 1) * NTOK, :]
        m["xT"] = np.ascontiguousarray(xs.T.reshape(16, 128, NTOK).transpose(1, 0, 2))
        ps = p[:L, b, R * NTOK:(R + 1) * NTOK, :]
        m["pT"] = np.ascontiguousarray(ps.transpose(0, 2, 1).reshape(L, 2, 128, NTOK).transpose(0, 2, 1, 3)).reshape(L, 128, 2 * NTOK)
        m["pos"] = np.ascontiguousarray(pos[b:b + 1, R * NTOK:(R + 1) * NTOK])
        fl = np.zeros((128, 16), np.float32)
        for r in range(3):
            fl[:, r] = 1.0 if r < R else 0.0
            fl[:, 3 + r] = 1.0 if r == R - 1 else 0.0
            fl[:, 6 + r] = 1.0 if r == R - 2 else 0.0
            fl[:, 9 + r] = 0.0 if r < R else NEG
        m["flags"] = fl
        f2 = np.zeros((128, 48), np.float32)
        for r in range(3):
            f2[:, r * 8:(r + 1) * 8] = fl[0, r]
            f2[:, 24 + r * 8:24 + (r + 1) * 8] = fl[0, 9 + r]
        m["flg2"] = f2
        maps.append(m)
    return maps


class Prog:
    def __init__(self, depth, taps=(), stop=99):
        self.L = depth
        self.taps = taps
        self.stop = stop
        self.nc = nc = bass.Bass("TRN2", target_bir_lowering=False)
        self.st = st = ExitStack()
        self.din = {}
        for name, (shape, dt) in input_specs(depth, stop).items():
            self.din[name] = nc.dram_tensor(name, shape, dt, kind="ExternalInput").ap()
        self.yT = nc.dram_tensor("yT", [128, 16, NTOK], F32, kind="ExternalOutput").ap()
        self.tap_out = {}
        for name, shape in taps:
            self.tap_out[name] = nc.dram_tensor(name, shape, F32, kind="ExternalOutput").ap()
        self.XW = [4096, 4096, 4096, 3072, 3072, 4096, 4096, 4096]
        self.xbs = [nc.dram_tensor("xbs%d" % i, [128, w], BF16) for i, w in enumerate(self.XW)]
        self.xbd = [nc.dram_tensor("xbd%d" % i, [512, w], BF16) for i, w in enumerate(self.XW)]
        self.rope_dram = nc.dram_tensor("rope_dram", [128, 2048], F32)
        self.Trope = T()
        self.Tbs = [T(), T()]
        self.xf_src = nc.dram_tensor("xf_src", [128, XFW], F32)
        self.xf_dst = nc.dram_tensor("xf_dst", [512, XFW], F32)
        self.xh_src = nc.dram_tensor("xh_src", [128, 32], BF16)
        self.xh_dst = nc.dram_tensor("xh_dst", [512, 32], BF16)
        sb = lambda n, s, d: st.enter_context(nc.sbuf_tensor(n, s, d))
        self.xT = sb("xT_sb", [128, 16, NTOK], F32)
        self.ring = sb("ring", [128, 4, 2048], BF16)
        self.rstd = sb("rstd", [128, NTOK], F32)
        self.cmb = sb("cmb", [128, 6 * 128], BF16)
        self.pswb = sb("pswb", [128, 128], BF16)
        self.ident = sb("ident", [128, 128], F32)
        self.mh = sb("mh", [128, 6 * 128], BF16)
        self.flags = sb("flags_sb", [128, 16], F32)
        self.flg2 = sb("flg2_sb", [128, 48], F32)
        self.onesf = sb("onesf", [128, 128], F32)
        self.gcols = sb("gcols_sb", [128, depth * 48], F32)
        self.gfin = sb("gfin_sb", [128, 16], F32)
        self.scw = sb("scw_sb", [128, depth * 12], F32)
        self.fcw = sb("fcw_sb", [128, depth * 264], F32)
        self.negb = sb("negb", [8, depth], F32)
        self.ropec = sb("ropec_sb", [128, 4], F32)
        self.cst = sb("cst", [128, 4], F32)
        self.nfk = sb("nfk", [128, 64], F32)
        self.biask = sb("biask", [128, 4 * 64], F32)
        self.dtmp = sb("dtmp", [128, 64], F32)
        self.zp = sb("zp", [128, 16], F32)
        self.gb2 = sb("gb2", [128, 8], F32)
        self.zl = sb("zl", [128, 8], F32)
        self.hcand = sb("hcand", [128, 96], BF16)
        self.hhalo = sb("hhalo", [128, 32], BF16)
        self.ARENA = 114944
        self.arena = sb("arena", [128, self.ARENA], U8)
        self.ps = [st.enter_context(nc.psum_tensor("ps%d" % i, [128, 512], F32)) for i in range(8)]
        self.Tps = [T() for _ in range(8)]
        self.S = Sched(nc, st)
        self.block = st.enter_context(nc.Block())
        self.ring_T = [T() for _ in range(4)]
        self.ring_ds = [self.S.dsem() for _ in range(4)]
        self.ring_i = 0
        self.deferred = []
        self.cc_ds = self.S.dsem()
        self.TxT = [T() for _ in range(16)]
        self.ThT = [T() for _ in range(16)]
        self.Tc = T()
        self.Trstd = T()
        self.pair_i = 0
        self.bank_i = 0
        self.dil_bs = 0
        self.Txbs = [T() for _ in range(8)]
        self.Txbd = [T() for _ in range(8)]
        self.Txfs = T()
        self.Txfd = T()
        self.Txhs = T()
        self.Txhd = T()

    def av(self, off, nbytes, dt, pat=None, **kw):
        assert off + nbytes <= self.ARENA, (off, nbytes)
        ap = self.arena[:, off:off + nbytes].bitcast(dt)
        if pat:
            ap = ap.rearrange(pat, **kw)
        return ap

    def load_w(self, src, n):
        i = self.ring_i % 4
        self.ring_i += 1
        dst = self.ring[:, i, 0:n]
        self.S.dma("pool", lambda e: e.dma_start(out=dst, in_=src), writes=[self.ring_T[i]], dsem=self.ring_ds[i], track=False)
        self.tick_deferred()
        return dst, self.ring_T[i]

    def defer(self, fn, n=3):
        self.deferred.append([n, fn])

    def tick_deferred(self, flush=False):
        keep = []
        for it in self.deferred:
            it[0] -= 1
            if it[0] <= 0 or flush:
                it[1]()
            else:
                keep.append(it)
        self.deferred = keep

    def load_big(self, src, dst, Tdst, n, arena=True):
        s3 = src.rearrange("p (a b) -> p a b", b=2048)
        d3 = dst.rearrange("p (a b) -> p a b", b=2048)
        self.S.dma("pool", lambda e: e.dma_start(out=d3, in_=s3), writes=[Tdst], arena=arena, track=arena)

    def next_pair(self, npairs=2):
        i = (self.pair_i % npairs) * 2
        self.pair_i += 1
        return i, i + 1

    def proj_b(self, lhs_fn, Tw, kc_n, rhs_fn, rhs_T, banks, m=128, halves=(0, 1)):
        S = self.S
        for hi, half in enumerate(halves):
            b = banks[hi]
            fns = []
            for kc in range(kc_n):
                fns.append(lambda e, b=b, kc=kc, half=half: e.matmul(
                    self.ps[b][0:m, :], lhs_fn(kc), rhs_fn(kc, half), start=(kc == 0), stop=(kc == kc_n - 1)))
            S.op("pe", fns, reads=[Tw] + list(rhs_T), writes=[self.Tps[b]])

    def rmsnorm(self, gcol_fn, hT, ThT, sq_off, out_f32=None, Tout=None):
        S = self.S
        sq = [self.av(sq_off + i * 2048, 2048, BF16) for i in range(2)]
        Tsq = [T(), T()]
        ones = self.cmb[:, 5 * 128:6 * 128]
        for kc in range(16):
            s = sq[kc % 2]
            if kc % 2 == 0:
                S.op("act", lambda e, s=s, kc=kc: e.activation(out=s, in_=self.xT[:, kc, :], func=AF.Square),
                     reads=[self.TxT[kc]], writes=[Tsq[kc % 2]])
            else:
                S.op("dve", lambda e, s=s, kc=kc: e.tensor_tensor(out=s, in0=self.xT[:, kc, :], in1=self.xT[:, kc, :], op=ALU.mult),
                     reads=[self.TxT[kc]], writes=[Tsq[kc % 2]])
            for half in range(2):
                S.op("pe", lambda e, s=s, kc=kc, half=half: e.matmul(
                    self.ps[6 + half][:, :], ones, s[:, half * 512:(half + 1) * 512], start=(kc == 0), stop=(kc == 15)),
                    reads=[Tsq[kc % 2], self.Tc], writes=[self.Tps[6 + half]])
        for half in range(2):
            r = self.rstd[:, half * 512:(half + 1) * 512]
            S.op("act", lambda e, r=r, half=half: e.activation(out=r, in_=self.ps[6 + half][:, :], func=AF.Sqrt,
                                                              bias=self.cst[:, 0:1], scale=1.0 / 2048.0),
                 reads=[self.Tps[6 + half], self.Tc], writes=[self.Trstd])
        S.op("dve", lambda e: e.reciprocal(out=self.rstd[:, :], in_=self.rstd[:, :]), reads=[self.Trstd], writes=[self.Trstd])
        self.apply_norm(gcol_fn, hT, ThT, out_f32, Tout)

    def apply_norm(self, gcol_fn, hT, ThT, out_f32=None, Tout=None, tmp_off=None):
        S = self.S
        tmp = [self.av(tmp_off + i * 4096, 4096, F32) for i in range(2)] if tmp_off is not None else None
        Ttmp = [T(), T()]
        for kc in range(16):
            if out_f32 is None and tmp is not None and kc % 2 == 1:
                q = (kc // 2) % 2
                S.op("pool", lambda e, kc=kc, q=q: e.tensor_tensor(out=tmp[q], in0=self.xT[:, kc, :], in1=self.rstd[:, :], op=ALU.mult),
                     reads=[self.TxT[kc], self.Trstd], writes=[Ttmp[q]], arena=True)
                S.op("act", lambda e, kc=kc, q=q: e.activation(out=hT[:, kc, :], in_=tmp[q], func=AF.Copy, scale=gcol_fn(kc)),
                     reads=[Ttmp[q], self.Tc], writes=[ThT[kc]])
                continue
            if out_f32 is None:
                S.op("dve", lambda e, kc=kc: e.scalar_tensor_tensor(out=hT[:, kc, :], in0=self.xT[:, kc, :], scalar=gcol_fn(kc),
                                                                   in1=self.rstd[:, :], op0=ALU.mult, op1=ALU.mult),
                     reads=[self.TxT[kc], self.Trstd, self.Tc], writes=[ThT[kc]])
            else:
                S.op("dve", lambda e, kc=kc: e.scalar_tensor_tensor(out=out_f32[:, kc, :], in0=self.xT[:, kc, :], scalar=gcol_fn(kc),
                                                                   in1=self.rstd[:, :], op0=ALU.mult, op1=ALU.mult),
                     reads=[self.TxT[kc], self.Trstd, self.Tc], writes=[Tout[kc]])

    def tap(self, name, src_ap, Ts, shape_pat=None):
        if name not in self.tap_out:
            return
        S = self.S
        dst = self.tap_out[name]
        if src_ap.dtype != F32:
            S.barrier()
            tt = T()
            tmp = self.av(self.ARENA - 4096, 4096, F32)
            for a in range(src_ap.shape[1]):
                S.op("dve", lambda e, a=a: e.tensor_copy(out=tmp, in_=src_ap[:, a, :]), reads=Ts, writes=[tt])
                S.dma("sp", lambda e, a=a: e.dma_start(out=dst[:, a, :], in_=tmp), reads=[tt])
            S.barrier()
        else:
            S.dma("sp", lambda e: e.dma_start(out=dst, in_=src_ap), reads=Ts)
            S.barrier()

    def setup(self):
        S, nc, L = self.S, self.nc, self.L
        d = self.din
        for kc in range(16):
            S.dma("sp", lambda e, kc=kc: e.dma_start(out=self.xT[:, kc, :], in_=d["xT"][:, kc, :]), writes=[self.TxT[kc]])
        Tc = self.Tc
        S.dma("pool", lambda e: e.dma_start(out=self.cmb[:, :], in_=d["cm"][:, 0:768]), writes=[Tc])
        S.dma("pool", lambda e: e.dma_start(out=self.pswb[:, :], in_=d["cm"][:, 896:1024]), writes=[Tc])
        S.dma("sp", lambda e: e.dma_start(out=self.ident[:, :], in_=d["cm"][:, 768:896]), writes=[Tc])
        S.dma("sp", lambda e: e.dma_start(out=self.onesf[:, :], in_=d["cm"][:, 640:768]), writes=[Tc])
        for nm, dst in (("flags", self.flags), ("flg2", self.flg2), ("gcols", self.gcols), ("gfin", self.gfin),
                        ("scw", self.scw), ("fcw", self.fcw), ("ropec", self.ropec)):
            S.dma("sp", lambda e, nm=nm, dst=dst: e.dma_start(out=dst[:, :], in_=d[nm]), writes=[Tc])
        S.dma("sp", lambda e: e.dma_start(out=self.negb[:, :], in_=d["foxb"]), writes=[Tc])
        S.op("dve", lambda e: e.tensor_scalar(out=self.negb[:, :], in0=self.negb[:, :], scalar1=-1.0, scalar2=None, op0=ALU.mult),
             reads=[Tc], writes=[Tc])
        S.op("dve", lambda e: e.memset(self.cst[:, 0:1], EPS), writes=[Tc])
        S.op("dve", lambda e: e.memset(self.cst[:, 1:2], 1.0), writes=[Tc])
        S.op("dve", lambda e: e.memset(self.cst[:, 2:3], 0.0), writes=[Tc])
        for r in range(3):
            S.op("dve", lambda e, r=r: e.tensor_scalar(out=self.mh[:, r * 128:(r + 1) * 128], in0=self.cmb[:, 128:256],
                                                      scalar1=self.flags[:, 3 + r:4 + r], scalar2=None, op0=ALU.mult),
                 reads=[Tc], writes=[Tc])
            S.op("dve", lambda e, r=r: e.tensor_scalar(out=self.mh[:, (3 + r) * 128:(4 + r) * 128], in0=self.cmb[:, 384:512],
                                                      scalar1=self.flags[:, 3 + r:4 + r], scalar2=None, op0=ALU.mult),
                 reads=[Tc], writes=[Tc])
            S.op("dve", lambda e, r=r: e.scalar_tensor_tensor(out=self.mh[:, (3 + r) * 128:(4 + r) * 128], in0=self.cmb[:, 512:640],
                                                             scalar=self.flags[:, 6 + r:7 + r], in1=self.mh[:, (3 + r) * 128:(4 + r) * 128],
                                                             op0=ALU.mult, op1=ALU.add),
                 reads=[Tc], writes=[Tc])
        SC_ = 61440 + 20480
        d = self.din
        Ct = self.av(SC_, 4096, F32)
        Sg = self.av(SC_ + 4096, 4096, F32)
        TC = T()
        posi = self.av(SC_ + 8192, 4096, I32)
        yv = self.av(SC_ + 12288, 4096, F32)
        kf = self.av(SC_ + 16384, 4096, F32)
        g1 = self.av(SC_ + 20480, 4096, F32)
        ki = self.av(SC_ + 24576, 4096, I32)
        Tr = T()
        S.dma("sp", lambda e: e.dma_start(out=posi, in_=d["pos"][0:1, :].partition_broadcast(128).rearrange("p o c -> p (o c)")), writes=[Tr])
        S.op("dve", lambda e: e.tensor_copy(out=yv, in_=posi), reads=[Tr], writes=[Tr])
        S.op("dve", lambda e: e.tensor_scalar(out=yv, in0=yv, scalar1=self.ropec[:, 0:1], scalar2=float(1.0 / (2.0 * np.pi)),
                                              op0=ALU.mult, op1=ALU.mult), reads=[Tr, self.Tc], writes=[Tr])
        for which, dst in ((0, Sg), (1, Ct)):
            if which == 1:
                S.op("dve", lambda e: e.tensor_scalar(out=yv, in0=yv, scalar1=0.25, scalar2=None, op0=ALU.add), reads=[Tr], writes=[Tr])
            S.op("dve", lambda e: e.tensor_copy(out=ki, in_=yv), reads=[Tr], writes=[Tr])
            S.op("dve", lambda e: e.tensor_copy(out=kf, in_=ki), reads=[Tr], writes=[Tr])
            S.op("dve", lambda e: e.tensor_tensor(out=kf, in0=yv, in1=kf, op=ALU.subtract), reads=[Tr], writes=[Tr])
            S.op("dve", lambda e: e.tensor_scalar(out=g1, in0=kf, scalar1=0.5, scalar2=None, op0=ALU.is_gt), reads=[Tr], writes=[Tr])
            S.op("dve", lambda e: e.tensor_tensor(out=kf, in0=kf, in1=g1, op=ALU.subtract), reads=[Tr], writes=[Tr])
            S.op("dve", lambda e: e.tensor_scalar(out=g1, in0=kf, scalar1=-0.5, scalar2=None, op0=ALU.is_lt), reads=[Tr], writes=[Tr])
            S.op("dve", lambda e: e.tensor_tensor(out=kf, in0=kf, in1=g1, op=ALU.add), reads=[Tr], writes=[Tr])
            S.op("act", lambda e, dst=dst: e.activation(out=dst, in_=kf, func=AF.Sin, scale=6.283185), reads=[Tr], writes=[TC])
        S.op("dve", lambda e: e.tensor_scalar(out=Sg, in0=Sg, scalar1=self.ropec[:, 1:2], scalar2=None, op0=ALU.mult),
             reads=[TC, self.Tc], writes=[TC])
        S.dma("sp", lambda e: e.dma_start(out=self.rope_dram.ap(), in_=self.av(SC_, 8192, F32)), reads=[TC], writes=[self.Trope])
        S.barrier()
        S.barrier()

    def layer(self, l):
        S, nc, L = self.S, self.nc, self.L
        d = self.din
        ps, Tps = self.ps, self.Tps
        cmb = self.cmb
        OB, HT, R0 = 0, 28672, 61440
        QF, QD, SC = R0, R0 + 8192, R0 + 20480
        o_a = self.av(OB, 8192, BF16, "p (a b) -> p a b", a=4)
        o_b = self.av(OB + 8192, 8192, BF16, "p (a b) -> p a b", a=4)
        o_c = self.av(OB + 16384, 8192, BF16, "p (a b) -> p a b", a=4)
        o_d = self.av(OB + 24576, 4096, BF16, "p (a b) -> p a b", a=2)
        To = {k: T() for k in ("a", "b", "c", "d")}
        hT = self.av(HT, 32768, BF16, "p (a b) -> p a b", a=16)
        ThT = self.ThT
        gc = lambda gi: (lambda kc: self.gcols[:, l * 48 + gi * 16 + kc: l * 48 + gi * 16 + kc + 1])
        hrhs = lambda kc, half: hT[:, kc, half * 512:(half + 1) * 512]
        xfs = self.xf_src.ap()
        xfd = self.xf_dst.ap()

        def evac_act(dst, b, scale=1.0, reads=(), writes=(), func=AF.Copy, m=128):
            return S.op("act", lambda e: e.activation(out=dst, in_=ps[b][0:m, :], func=func, scale=scale),
                        reads=[Tps[b]] + list(reads), writes=list(writes))

        if l == 0:
            S.barrier()
        self.rmsnorm(gc(0), hT, ThT, OB)
        if l == 0:
            self.tap("t_h", hT, ThT)

        self.late_gathers = []

        def gather(i, n=3):
            self.defer(lambda i=i: S.dma("pool", lambda e: e.collective_compute("AllGather", ALU.bypass, replica_groups=RG, dma_qos="P3",
                                                                                 ins=[self.xbs[i].ap().opt()], outs=[self.xbd[i].ap().opt()]),
                                         reads=[self.Txbs[i]], writes=[self.Txbd[i]], dsem=self.cc_ds, inc=1, track=False), n)
        bigA = self.av(R0, 8192, BF16)
        bigB = self.av(R0 + 8192, 8192, BF16)
        self.load_big(d["w_v"][l, 0], bigB, self.Tbs[1], 4096, arena=False)
        self.load_big(d["w_v"][l, 1], bigA, self.Tbs[0], 4096, arena=False)
        Ct = self.av(SC, 4096, F32)
        Sg = self.av(SC + 4096, 4096, F32)
        TC = T()
        S.dma("sp", lambda e: e.dma_start(out=self.av(SC, 8192, F32), in_=self.rope_dram.ap()), reads=[self.Trope], writes=[TC])
        S.barrier()
        kst = [self.av(SC + 8192 + i * 2048, 2048, BF16) for i in range(2)]
        Tkst = [T(), T()]
        for c in range(4):
            w, Tw = self.load_w(d["w_k"][l, c], 2048)
            pr = self.next_pair()
            self.proj_b(lambda kc, w=w: w[:, kc * 128:(kc + 1) * 128], Tw, 16, hrhs, ThT, pr)
            for half in range(2):
                evac_act(kst[c % 2][:, half * 512:(half + 1) * 512], pr[half], writes=[Tkst[c % 2]])
            S.dma("sp", lambda e, c=c: e.dma_start(out=self.xbs[0].ap()[:, c * 1024:(c + 1) * 1024], in_=kst[c % 2]),
                  reads=[Tkst[c % 2]], writes=[self.Txbs[0]])
        gather(0)
        big = [self.av(R0 + 8192 - i * 8192, 8192, BF16) for i in range(2)]
        Tbig = [self.Tbs[1], self.Tbs[0]]
        vst = self.av(SC + 12288, 16384, BF16, "p (hp t c) -> p hp t c", t=8, hp=4)
        Tvst = T()
        S.op("dve", lambda e: e.memset(vst[:, :, :, 64:192], 1.0), writes=[Tvst])
        bi = 0
        for tt in range(8):
            for pc in range(2):
                b = 4 + bi % 2
                bi += 1
                fns = [lambda e, kc=kc, b=b, tt=tt, pc=pc: e.matmul(ps[b][:, 0:256], hT[:, kc, tt * 128:(tt + 1) * 128],
                                                                big[pc][:, kc * 256:(kc + 1) * 256], start=(kc == 0), stop=(kc == 15))
                       for kc in range(16)]
                S.op("pe", fns, reads=ThT + [Tbig[pc]], writes=[Tps[b]])
                pv4 = ps[b][:, 0:256].rearrange("p (hp hh c) -> p hp hh c", hp=2, hh=2)
                for hh, c0 in ((0, 0), (1, 192)):
                    S.op("dve", lambda e, tt=tt, pc=pc, pv4=pv4, hh=hh, c0=c0: e.tensor_copy(
                        out=vst[:, 2 * pc:2 * pc + 2, tt, c0:c0 + 64], in_=pv4[:, :, hh:hh + 1, :].rearrange("p hp o c -> p hp (o c)")),
                        reads=[Tps[b]], writes=[Tvst])
        for th in range(2):
            S.dma("sp", lambda e, th=th: e.dma_start(out=self.xbs[1 + th].ap(), in_=vst[:, 2 * th:2 * th + 2, :, :].rearrange("p hp t c -> p (hp t c)")),
                  reads=[Tvst], writes=[self.Txbs[1 + th]])
            gather(1 + th, 3 + 8 * th)
        self.load_big(d["w_dv"][l, 0], bigA, self.Tbs[0], 4096, arena=False)
        self.load_big(d["w_dv"][l, 1], bigB, self.Tbs[1], 4096, arena=False)
        S.barrier()
        if self.stop == 1:
            return
        spb = self.av(SC + 8192, 4096, F32)
        NFT = self.av(SC + 12288, 4096, F32)
        onf = self.av(SC + 16384, 4096, F32)
        nfhl = self.av(OB + 16384, 4096, BF16, "p (a b) -> p a b", a=2)
        Tsp, Tnf, Tonf, Tnfhl = T(), T(), T(), T()
        w, Tw = self.load_w(d["w_f"][l], 128)
        pr = self.next_pair()
        self.proj_b(lambda kc, w=w: w[:, kc * 8:(kc + 1) * 8], Tw, 16, hrhs, ThT, pr, m=8)
        for half in range(2):
            S.op("act", lambda e, half=half: e.activation(out=spb[0:8, half * 512:(half + 1) * 512], in_=ps[pr[half]][0:8, :], func=AF.Exp,
                                                          bias=self.negb[0:8, l:l + 1], scale=-1.0),
                 reads=[Tps[pr[half]], self.Tc], writes=[Tsp])
        S.op("act", lambda e: e.activation(out=spb[0:8, :], in_=spb[0:8, :], func=AF.Ln, bias=self.cst[0:8, 1:2], scale=1.0),
             reads=[Tsp, self.Tc], writes=[Tsp])
        S.op("dve", lambda e: e.memset(onf[0:8, :], 1.0), writes=[Tonf])
        S.op("dve", lambda e: e.tensor_tensor_scan(out=NFT[0:8, :], data0=onf[0:8, :], data1=spb[0:8, :], initial=0.0,
                                                   op0=ALU.mult, op1=ALU.add), reads=[Tsp, Tonf], writes=[Tnf])
        S.op("dve", lambda e: e.tensor_scalar(out=nfhl[0:8, 0, :], in0=NFT[0:8, :], scalar1=-1.0, scalar2=None, op0=ALU.mult),
             reads=[Tnf], writes=[Tnfhl])
        S.op("dve", lambda e: e.scalar_tensor_tensor(out=nfhl[0:8, 1, :], in0=NFT[0:8, :], scalar=-1.0, in1=nfhl[0:8, 0, :],
                                                     op0=ALU.mult, op1=ALU.subtract), reads=[Tnf, Tnfhl], writes=[Tnfhl])
        for t in range(8):
            S.op("pe", lambda e, t=t: e.transpose(out=ps[4][:, t * 8:(t + 1) * 8], in_=NFT[0:8, t * 128:(t + 1) * 128],
                                                  identity=self.ident[0:8, 0:8]), reads=[Tnf, self.Tc], writes=[Tps[4]])
        Tnfk = T()
        S.op("dve", lambda e: e.tensor_copy(out=self.nfk[:, :], in_=ps[4][:, 0:64]), reads=[Tps[4]], writes=[Tnfk])
        S.dma("sp", lambda e: e.dma_start(out=xfs[:, 0:64], in_=self.nfk[:, :]), reads=[Tnfk], writes=[self.Txfs])
        S.barrier()
        if self.stop == 2:
            return
        tmpA = [self.av(SC + 8192, 4096, F32) for i in range(2)]
        zbuf = [self.av(SC + 20480 + i * 4112, 4112, F32) for i in range(2)]
        gbt = [self.av(SC + 12288 + i * 4096, 4096, F32) for i in range(2)]
        t1s = [self.av(SC + 28704, 4096, F32) for i in range(2)]
        TtA0, Tt10 = T(), T()
        TtA, Tz, Tg, Tt1 = [TtA0, TtA0], [T(), T()], [T(), T()], [Tt10, Tt10]
        Tzp = T()
        for j in range(4):
            q = j % 2
            w, Tw = self.load_w(d["w_cb"][l, 3 * j + 0], 2048)
            pr = self.next_pair()
            self.proj_b(lambda kc, w=w: w[:, kc * 128:(kc + 1) * 128], Tw, 16, hrhs, ThT, pr)
            for half in range(2):
                evac_act(tmpA[q][:, half * 512:(half + 1) * 512], pr[half], writes=[TtA[q]])
            w, Tw = self.load_w(d["w_cb"][l, 3 * j + 1], 2048)
            pr = self.next_pair()
            self.proj_b(lambda kc, w=w: w[:, kc * 128:(kc + 1) * 128], Tw, 16, hrhs, ThT, pr)
            for half in range(2):
                S.op("dve", lambda e, half=half, q=q, pr=pr: e.tensor_tensor(
                    out=zbuf[q][:, 2 + half * 512:2 + (half + 1) * 512], in0=tmpA[q][:, half * 512:(half + 1) * 512],
                    in1=ps[pr[half]][:, :], op=ALU.mult), reads=[TtA[q], Tps[pr[half]]], writes=[Tz[q]])
            w, Tw = self.load_w(d["w_cb"][l, 3 * j + 2], 2048)
            pr = self.next_pair()
            self.proj_b(lambda kc, w=w: w[:, kc * 128:(kc + 1) * 128], Tw, 16, hrhs, ThT, pr)
            for half in range(2):
                evac_act(gbt[q][:, half * 512:(half + 1) * 512], pr[half], writes=[Tg[q]])
            wc = lambda k, j=j: self.scw[:, l * 12 + k * 4 + j: l * 12 + k * 4 + j + 1]
            S.op("dve", lambda e, q=q, wc=wc: e.tensor_scalar(out=t1s[q][:, 2:1024], in0=zbuf[q][:, 4:1026], scalar1=wc(2), scalar2=None,
                                                           op0=ALU.mult), reads=[Tz[q], self.Tc], writes=[Tt1[q]])
            S.op("dve", lambda e, q=q, wc=wc: e.scalar_tensor_tensor(out=t1s[q][:, 2:1024], in0=zbuf[q][:, 3:1025], scalar=wc(1),
                                                                  in1=t1s[q][:, 2:1024], op0=ALU.mult, op1=ALU.add),
                 reads=[Tz[q], self.Tc, Tt1[q]], writes=[Tt1[q]])
            S.op("dve", lambda e, q=q, wc=wc: e.scalar_tensor_tensor(out=t1s[q][:, 2:1024], in0=zbuf[q][:, 2:1024], scalar=wc(0),
                                                                  in1=t1s[q][:, 2:1024], op0=ALU.mult, op1=ALU.add),
                 reads=[Tz[q], self.Tc, Tt1[q]], writes=[Tt1[q]])
            S.op("dve", lambda e, q=q, j=j: e.tensor_tensor(out=o_b[:, j, 2:1024], in0=t1s[q][:, 2:1024], in1=gbt[q][:, 2:1024], op=ALU.mult),
                 reads=[Tt1[q], Tg[q]], writes=[To["b"]])
            S.op("dve", lambda e, q=q, j=j: e.tensor_copy(out=self.zp[:, 4 * j + 2:4 * j + 4], in_=zbuf[q][:, 2:4]), reads=[Tz[q]], writes=[Tzp])
            S.op("dve", lambda e, q=q, j=j: e.tensor_copy(out=self.gb2[:, 2 * j:2 * j + 2], in_=gbt[q][:, 0:2]), reads=[Tg[q]], writes=[Tzp])
            S.op("dve", lambda e, q=q, j=j: e.tensor_copy(out=self.zl[:, 2 * j:2 * j + 2], in_=zbuf[q][:, 1024:1026]), reads=[Tz[q]], writes=[Tzp])
        S.dma("sp", lambda e: e.dma_start(out=xfs[:, 64:72], in_=self.zl[:, :]), reads=[Tzp], writes=[self.Txfs])
        self.defer(lambda: S.dma("pool", lambda e: e.collective_compute("AllGather", ALU.bypass, replica_groups=RG, dma_qos="P3",
                                                                        ins=[self.xf_src.ap().opt()], outs=[self.xf_dst.ap().opt()]),
                                 reads=[self.Txfs], writes=[self.Txfd], dsem=self.cc_ds, inc=1, track=False))
        S.barrier()
        qraw = [self.av(SC + 8192 + i * 2048, 2048, BF16) for i in range(2)]
        tt2 = [self.av(SC + 12288 + i * 4096, 4096, F32) for i in range(2)]
        u12 = [self.av(SC + 20480 + i * 4096, 4096, F32) for i in range(2)]
        kst2 = [self.av(SC + 28672 + i * 2048, 2048, BF16) for i in range(2)]
        Tq, Ttt2, Tu12, Tk2 = [T(), T()], [T(), T()], [T(), T()], [T(), T()]

        def rope_proj(wname, c, scale, dst, Tdst, cnt):
            dil = (1, 4, 8)[c // 2]
            q = cnt % 2
            tt_, u1, Ttt, Tu1 = tt2[q], u12[q], Ttt2[q], Tu12[q]
            w, Tw = self.load_w(d[wname][l, c], 2048)
            pr = self.next_pair()
            self.proj_b(lambda kc, w=w: w[:, kc * 128:(kc + 1) * 128], Tw, 16, hrhs, ThT, pr)
            for half in range(2):
                evac_act(qraw[q][:, half * 512:(half + 1) * 512], pr[half], scale=scale, writes=[Tq[q]])
            pr2 = self.next_pair()
            for half in range(2):
                S.op("pe", lambda e, half=half, pr2=pr2, q=q: e.matmul(ps[pr2[half]][:, :], self.pswb[:, :], qraw[q][:, half * 512:(half + 1) * 512],
                                                                  start=True, stop=True), reads=[Tq[q], self.Tc], writes=[Tps[pr2[half]]])
            S.op("dve", lambda e, q=q: e.tensor_tensor(out=tt_, in0=qraw[q], in1=Ct, op=ALU.mult), reads=[Tq[q], TC], writes=[Ttt])
            for half in range(2):
                S.op("dve", lambda e, half=half, pr2=pr2: e.tensor_tensor(out=u1[:, half * 512:(half + 1) * 512], in0=ps[pr2[half]][:, :],
                                                                         in1=Sg[:, half * 512:(half + 1) * 512], op=ALU.mult),
                     reads=[Tps[pr2[half]], TC], writes=[Tu1])
            if dil == 1:
                S.op("dve", lambda e: e.tensor_tensor(out=dst, in0=u1, in1=tt_, op=ALU.add), reads=[Tu1, Ttt], writes=[Tdst])
            else:
                S.op("dve", lambda e, dil=dil: e.tensor_tensor(out=dst.rearrange("p (r i) -> p i r", r=dil),
                                                              in0=u1.rearrange("p (i r) -> p i r", r=dil),
                                                              in1=tt_.rearrange("p (i r) -> p i r", r=dil), op=ALU.add),
                     reads=[Tu1, Ttt], writes=[Tdst])

        for c in range(6):
            rope_proj("w_dk", c, 1.0, kst2[c % 2], Tk2[c % 2], c)
            S.dma("sp", lambda e, c=c: e.dma_start(out=self.xbs[3 + c // 3].ap()[:, (c % 3) * 1024:(c % 3 + 1) * 1024], in_=kst2[c % 2]),
                  reads=[Tk2[c % 2]], writes=[self.Txbs[3 + c // 3]])
            if c % 3 == 2:
                self.late_gathers.append(3 + c // 3)
        S.barrier()
        bigd = [self.av(R0 + i * 8192, 8192, BF16) for i in range(2)]
        Tbd = self.Tbs
        vdst = [self.av(SC + 8192 + i * 8192, 8192, BF16, "p (t hp c) -> p t hp c", t=8, hp=2) for i in range(2)]
        Tvd = [T(), T()]
        for i in range(2):
            S.op("dve", lambda e, i=i: e.memset(vdst[i][:, :, :, 64:192], 1.0), writes=[Tvd[i]])
        DIL = (1, 4, 16)
        bi = 0
        for g in range(3):
            dil = DIL[g]
            for tp in range(8):
                b = 4 + bi % 2
                bi += 1

                def lhs(kc, tp=tp, dil=dil):
                    if dil == 1:
                        return hT[:, kc, tp * 128:(tp + 1) * 128]
                    if dil == 4:
                        return hT[:, kc, :].rearrange("p (i r) -> p r i", r=4)[:, tp // 2, (tp % 2) * 128:(tp % 2) * 128 + 128]
                    return hT[:, kc, :].rearrange("p (i r) -> p r i", r=8)[:, tp, :]
                fns = [lambda e, kc=kc, b=b, g=g, lhs=lhs: e.matmul(ps[b][:, 0:256], lhs(kc), bigd[g % 2][:, kc * 256:(kc + 1) * 256],
                                                                    start=(kc == 0), stop=(kc == 15)) for kc in range(16)]
                S.op("pe", fns, reads=ThT + [Tbd[g % 2]], writes=[Tps[b]])
                pv4 = ps[b][:, 0:256].rearrange("p (hp hh c) -> p hp hh c", hp=2, hh=2)
                for hh, c0 in ((0, 0), (1, 192)):
                    S.op("dve", lambda e, tp=tp, g=g, pv4=pv4, hh=hh, c0=c0: e.tensor_copy(
                        out=vdst[g % 2][:, tp, :, c0:c0 + 64], in_=pv4[:, :, hh:hh + 1, :].rearrange("p hp o c -> p hp (o c)")),
                        reads=[Tps[b]], writes=[Tvd[g % 2]])
            S.dma("sp", lambda e, g=g: e.dma_start(out=self.xbs[5 + g].ap(), in_=vdst[g % 2].rearrange("p t hp c -> p (t hp c)")),
                  reads=[Tvd[g % 2]], writes=[self.Txbs[5 + g]])
            self.late_gathers.append(5 + g)
            if g == 0:
                self.load_big(d["w_dv"][l, 2], bigA, self.Tbs[0], 4096, arena=False)
        S.barrier()
        if self.stop == 3:
            return
        if self.stop == 4:
            return
        QdT = self.av(QD, 12288, BF16, "p (a b) -> p a b", a=6)
        QfT = self.av(QF, 8192, BF16, "p (a b) -> p a b", a=4)
        TQd = [T() for _ in range(6)]
        TQf = [T() for _ in range(4)]
        for c in range(6):
            rope_proj("w_dq", c, 0.125, QdT[:, c, :], TQd[c], c)
        for c in range(4):
            w, Tw = self.load_w(d["w_q"][l, c], 2048)
            pr = self.next_pair()
            self.proj_b(lambda kc, w=w: w[:, kc * 128:(kc + 1) * 128], Tw, 16, hrhs, ThT, pr)
            for half in range(2):
                evac_act(QfT[:, c, half * 512:(half + 1) * 512], pr[half], scale=0.125, writes=[TQf[c]])
        self.tick_deferred(flush=True)
        S.barrier()
        if self.stop == 5:
            return
        self.fox_attention(l, QfT, TQf, o_a, To["a"], nfhl, Tnfhl, Tnfk, HT, SC, OB)
        if l == 0:
            self.tap("t_oa", o_a, [To["a"]])
        if self.stop == 6:
            return
        S.barrier()
        self.dil_attention(l, QdT, TQd, o_d, To["d"], HT, SC, QF)
        if l == 0:
            self.tap("t_od", o_d, [To["d"]])
        S.barrier()
        if self.stop == 7:
            return
        self.apply_norm(gc(0), hT, ThT)
        Tdt = T()
        S.dma("sp", lambda e: e.dma_start(out=self.dtmp[:, 0:24].rearrange("p (r c) -> p r c", r=3),
                                          in_=xfd[0:384, 64:72].rearrange("(r p) c -> p r c", p=128)), reads=[self.Txfd], writes=[Tdt])
        zh = self.dtmp[:, 24:32]
        S.op("dve", lambda e: e.tensor_scalar(out=zh, in0=self.dtmp[:, 0:8], scalar1=self.flags[:, 3:4], scalar2=None, op0=ALU.mult),
             reads=[Tdt, self.Tc], writes=[Tdt])
        for r in (1, 2):
            S.op("dve", lambda e, r=r: e.scalar_tensor_tensor(out=zh, in0=self.dtmp[:, 8 * r:8 * r + 8], scalar=self.flags[:, 3 + r:4 + r],
                                                             in1=zh, op0=ALU.mult, op1=ALU.add), reads=[Tdt, self.Tc], writes=[Tdt])
        S.op("dve", lambda e: e.tensor_copy(out=self.zp[:, :].rearrange("p (j k) -> p j k", k=4)[:, :, 0:2],
                                            in_=zh.rearrange("p (j k) -> p j k", k=2)), reads=[Tdt, Tzp], writes=[Tzp])
        o2 = self.dtmp[:, 32:40]
        for j in range(4):
            wc = lambda k, j=j: self.scw[:, l * 12 + k * 4 + j: l * 12 + k * 4 + j + 1]
            oj = o2[:, 2 * j:2 * j + 2]
            S.op("dve", lambda e, j=j, wc=wc, oj=oj: e.tensor_scalar(out=oj, in0=self.zp[:, 4 * j + 2:4 * j + 4], scalar1=wc(2), scalar2=None,
                                                                  op0=ALU.mult), reads=[Tzp, self.Tc], writes=[Tdt])
            S.op("dve", lambda e, j=j, wc=wc, oj=oj: e.scalar_tensor_tensor(out=oj, in0=self.zp[:, 4 * j + 1:4 * j + 3], scalar=wc(1), in1=oj,
                                                                         op0=ALU.mult, op1=ALU.add), reads=[Tzp, self.Tc, Tdt], writes=[Tdt])
            S.op("dve", lambda e, j=j, wc=wc, oj=oj: e.scalar_tensor_tensor(out=oj, in0=self.zp[:, 4 * j + 0:4 * j + 2], scalar=wc(0), in1=oj,
                                                                         op0=ALU.mult, op1=ALU.add), reads=[Tzp, self.Tc, Tdt], writes=[Tdt])
            S.op("dve", lambda e, j=j, oj=oj: e.tensor_tensor(out=o_b[:, j, 0:2], in0=oj, in1=self.gb2[:, 2 * j:2 * j + 2], op=ALU.mult),
                 reads=[Tdt, Tzp], writes=[To["b"]])
        if l == 0:
            self.tap("t_ob", o_b, [To["b"]])
        self.sgu(l, hT, ThT, o_c, To["c"], SC)
        if l == 0:
            self.tap("t_oc", o_c, [To["c"]])
        S.barrier()
        if self.stop == 8:
            return
        merged = self.av(R0, 32768, BF16, "p (a b) -> p a b", a=16)
        Tm = [T() for _ in range(16)]
        sg = [self.av(R0 + 32768 + i * 2048, 2048, F32) for i in range(2)]
        prod = [self.av(R0 + 36864 + i * 2048, 2048, F32) for i in range(2)]
        acc = self.av(R0 + 40960, 4096, F32)
        Tsg, Tpr, Tacc = [T(), T()], [T(), T()], T()
        branches = (("w_bf", 4, o_a, To["a"]), ("w_bc", 4, o_b, To["b"]), ("w_bs", 4, o_c, To["c"]), ("w_bd", 2, o_d, To["d"]))
        n = 0
        for j in range(16):
            for i, (wn, kcn, ot, Tot) in enumerate(branches):
                w, Tw = self.load_w(d["w_g"][l, i * 16 + j], 2048)
                pg = self.next_pair(4)
                self.proj_b(lambda kc, w=w: w[:, kc * 128:(kc + 1) * 128], Tw, 16, hrhs, ThT, pg)
                w2, Tw2 = self.load_w(d[wn][l, j], kcn * 128)
                pb = self.next_pair(4)
                self.proj_b(lambda kc, w2=w2: w2[:, kc * 128:(kc + 1) * 128], Tw2, kcn,
                            lambda kc, half, ot=ot: ot[:, kc, half * 512:(half + 1) * 512], [Tot], pb)
                for half in range(2):
                    q = n % 2
                    n += 1
                    hs = slice(half * 512, (half + 1) * 512)
                    S.op("act", lambda e, q=q, b=pg[half]: e.activation(out=sg[q], in_=ps[b][:, :], func=AF.Sigmoid),
                         reads=[Tps[pg[half]]], writes=[Tsg[q]])
                    if i == 0:
                        S.op("dve", lambda e, q=q, b=pb[half], hs=hs: e.tensor_tensor(out=acc[:, hs], in0=sg[q], in1=ps[b][:, :], op=ALU.mult),
                             reads=[Tsg[q], Tps[pb[half]]], writes=[Tacc])
                    else:
                        S.op("dve", lambda e, q=q, b=pb[half]: e.tensor_tensor(out=prod[q], in0=sg[q], in1=ps[b][:, :], op=ALU.mult),
                             reads=[Tsg[q], Tps[pb[half]]], writes=[Tpr[q]])
                        if i < 3:
                            S.op("dve", lambda e, q=q, hs=hs: e.tensor_tensor(out=acc[:, hs], in0=acc[:, hs], in1=prod[q], op=ALU.add),
                                 reads=[Tpr[q], Tacc], writes=[Tacc])
                        else:
                            S.op("dve", lambda e, q=q, hs=hs, j=j: e.tensor_tensor(out=merged[:, j, hs], in0=acc[:, hs], in1=prod[q], op=ALU.add),
                                 reads=[Tpr[q], Tacc], writes=[Tm[j]])
        for j in range(16):
            w, Tw = self.load_w(d["w_o"][l, j], 2048)
            pr = self.next_pair(4)
            self.proj_b(lambda kc, w=w: w[:, kc * 128:(kc + 1) * 128], Tw, 16,
                        lambda kc, half: merged[:, kc, half * 512:(half + 1) * 512], Tm, pr)
            for half in range(2):
                hs = slice(half * 512, (half + 1) * 512)
                S.op("dve", lambda e, j=j, hs=hs, b=pr[half]: e.tensor_tensor(out=self.xT[:, j, hs], in0=self.xT[:, j, hs], in1=ps[b][:, :], op=ALU.add),
                     reads=[Tps[pr[half]]], writes=[self.TxT[j]])
        if l == 0:
            self.tap("t_x1", self.xT[:, :, :], self.TxT)
        if self.stop == 9:
            return
        self.rmsnorm(gc(1), hT, ThT, R0 + 45056)
        self.ffn(l, hT, ThT, OB, R0)
        S.barrier()
        if l == 0:
            self.tap("t_x2", self.xT[:, :, :], self.TxT)
        if self.stop == 10:
            return
        self.rmsnorm(gc(2), hT, ThT, OB)
        pT = self.av(OB + 20544, 4096, BF16)
        TpT = T()
        self.load_big(d["pT"][l], pT, TpT, 2048)
        sg = [self.av(R0 + 28672 + i * 2048, 2048, F32) for i in range(2)]
        prod = [self.av(R0 + 32768 + i * 2048, 2048, F32) for i in range(2)]
        Tsg, Tpr = [T(), T()], [T(), T()]
        n = 0
        for j in range(16):
            w, Tw = self.load_w(d["w_pg"][l, j], 2048)
            pg = self.next_pair(4)
            self.proj_b(lambda kc, w=w: w[:, kc * 128:(kc + 1) * 128], Tw, 16, hrhs, ThT, pg)
            w2, Tw2 = self.load_w(d["w_pp"][l, j], 256)
            pb = self.next_pair(4)
            self.proj_b(lambda kc, w2=w2: w2[:, kc * 128:(kc + 1) * 128], Tw2, 2,
                        lambda kc, half: pT[:, kc * 1024 + half * 512: kc * 1024 + (half + 1) * 512], [TpT], pb)
            for half in range(2):
                q = n % 2
                n += 1
                hs = slice(half * 512, (half + 1) * 512)
                S.op("act", lambda e, q=q, b=pg[half]: e.activation(out=sg[q], in_=ps[b][:, :], func=AF.Sigmoid),
                     reads=[Tps[pg[half]]], writes=[Tsg[q]])
                S.op("dve", lambda e, q=q, b=pb[half]: e.tensor_tensor(out=prod[q], in0=sg[q], in1=ps[b][:, :], op=ALU.mult),
                     reads=[Tsg[q], Tps[pb[half]]], writes=[Tpr[q]])
                S.op("dve", lambda e, q=q, j=j, hs=hs: e.tensor_tensor(out=self.xT[:, j, hs], in0=self.xT[:, j, hs], in1=prod[q], op=ALU.add),
                     reads=[Tpr[q]], writes=[self.TxT[j]])

    def fox_attention(self, l, QfT, TQf, o_a, Toa, nfhl, Tnfhl, Tnfk, HT, SC, OB):
        S, ps, Tps, cmb = self.S, self.ps, self.Tps, self.cmb
        xfd = self.xf_dst.ap()
        Tb, Tdt = T(), T()
        S.dma("sp", lambda e: e.dma_start(out=self.biask[:, 0:192].rearrange("p (r c) -> p r c", r=3),
                                          in_=xfd[0:384, 0:64].rearrange("(r p) c -> p r c", p=128)), reads=[self.Txfd], writes=[Tb])
        for r in range(3):
            S.dma("sp", lambda e, r=r: e.dma_start(out=self.dtmp[:, 8 * r:8 * r + 8],
                                                   in_=xfd[r * 128 + 127:r * 128 + 128, 56:64].partition_broadcast(128).rearrange("p o c -> p (o c)")),
                  reads=[self.Txfd], writes=[Tdt])
        dt = self.dtmp
        S.op("dve", lambda e: e.tensor_tensor(out=dt[:, 0:24], in0=dt[:, 0:24], in1=self.flg2[:, 0:24], op=ALU.mult), reads=[Tdt, self.Tc], writes=[Tdt])
        S.op("dve", lambda e: e.tensor_tensor(out=dt[:, 8:16], in0=dt[:, 8:16], in1=dt[:, 16:24], op=ALU.add), reads=[Tdt], writes=[Tdt])
        S.op("dve", lambda e: e.tensor_tensor(out=dt[:, 0:8], in0=dt[:, 0:8], in1=dt[:, 8:16], op=ALU.add), reads=[Tdt], writes=[Tdt])
        S.op("dve", lambda e: e.tensor_tensor(out=dt[:, 24:48], in0=self.flg2[:, 24:48], in1=dt[:, 0:24], op=ALU.subtract), reads=[Tdt, self.Tc], writes=[Tdt])
        for r in range(3):
            for j in range(8):
                o = r * 64 + j * 8
                S.op("dve", lambda e, o=o, r=r: e.tensor_tensor(out=self.biask[:, o:o + 8], in0=self.biask[:, o:o + 8], in1=dt[:, 24 + 8 * r:32 + 8 * r],
                                                              op=ALU.add), reads=[Tb, Tdt], writes=[Tb])
        S.op("dve", lambda e: e.tensor_copy(out=self.biask[:, 192:256], in_=self.nfk[:, :]), reads=[Tnfk, Tb], writes=[Tb])
        for i in self.late_gathers:
            S.dma("pool", lambda e, i=i: e.collective_compute("AllGather", ALU.bypass, replica_groups=RG, dma_qos="P3",
                                                              ins=[self.xbs[i].ap().opt()], outs=[self.xbd[i].ap().opt()]),
                  reads=[self.Txbs[i]], writes=[self.Txbd[i]], dsem=self.cc_ds, inc=1, track=False)
        self.late_gathers = []
        vt = [[self.av(HT + (st * 4 + s) * 4096, 4096, BF16, "p (t c) -> p t c", t=8) for s in range(4)] for st in range(2)]
        ktp = [[self.av(SC + (hh * 4 + s) * 2048, 2048, BF16) for s in range(4)] for hh in range(2)]
        kall = self.av(SC, 16384, BF16)
        Tvt = [[T() for _ in range(4)] for _ in range(2)]
        Tkt = [[T() for _ in range(4)] for _ in range(2)]
        Pt = [self.av(SC + 16384 + i * 1024, 1024, BF16) for i in range(4)]
        TPt = [T() for _ in range(4)]
        rden = self.av(SC + 20480, 2048, F32)
        Trd = T()
        Qtmp = [self.av(OB + 16384 + 4096 + i * 2048, 2048, BF16) for i in range(2)]
        qall = self.av(OB + 16384 + 4096, 4096, BF16)
        TQt = [T(), T()]
        allk = [t for row in Tkt for t in row]
        S.op("dve", lambda e: e.memset(kall[64:128, :], 0.0), writes=allk)
        S.op("dve", lambda e: e.memset(kall[64:66, :], 1.0), writes=allk)
        S.op("dve", lambda e: e.memset(qall[64:128, :], 0.0), writes=TQt)
        LAG = 2

        def srcT(i, s):
            if s < 3:
                return self.xbd[i].ap()[s * 128:(s + 1) * 128, :], self.Txbd[i]
            return self.xbs[i].ap(), self.Txbs[i]

        def load_k(head):
            c, hh = head // 2, head % 2
            for s in range(4):
                ks, Tks = srcT(0, s)
                S.dma("sp", lambda e, ks=ks, c=c, hh=hh, s=s: e.dma_start(out=ktp[hh][s][0:64, :], in_=ks[64 * hh:64 * hh + 64, c * 1024:(c + 1) * 1024]),
                      reads=[Tks], writes=[Tkt[hh][s]])

        def load_v(c):
            st = c % 2
            for s in range(4):
                vs, Tvs = srcT(1 + c // 2, s)
                S.dma("sp", lambda e, vs=vs, c=c, st=st, s=s: e.dma_start(
                    out=vt[st][s], in_=vs[:, (c % 2) * 2048:(c % 2 + 1) * 2048].rearrange("p (t c) -> p t c", t=8)),
                    reads=[Tvs], writes=[Tvt[st][s]])

        load_k(0)
        load_v(0)
        for head in range(8):
            c, hh = head // 2, head % 2
            st = c % 2
            pb = 64 * hh
            ob = 64 - pb
            if head + 1 < 8:
                load_k(head + 1)
            if hh == 0 and c + 1 < 4:
                load_v(c + 1)
            qt = Qtmp[head % 2]
            S.op("act", lambda e, qt=qt, pb=pb, c=c: e.activation(out=qt[0:64, :], in_=QfT[pb:pb + 64, c, :], func=AF.Copy),
                 reads=[TQf[c]], writes=[TQt[head % 2]])
            S.dma("sp", lambda e, qt=qt, head=head: e.dma_start(out=qt[64:65, :], in_=nfhl[head:head + 1, 0, :]),
                  reads=[Tnfhl], writes=[TQt[head % 2]])
            S.dma("sp", lambda e, qt=qt, head=head: e.dma_start(out=qt[65:66, :], in_=nfhl[head:head + 1, 1, :]),
                  reads=[Tnfhl], writes=[TQt[head % 2]])
            for qh in range(2):
                steps = [(s, j, 0) for s in range(3) for j in range(8)] + [(3, j, max(0, j - 4 * qh) * 128) for j in range(4 * qh + 4)]
                bo = 4 + (self.bank_i % 4)
                self.bank_i += 1
                pend = []

                def pv(i, s, j, c0, bs, bo=bo, st=st, hh=hh, nsteps=len(steps)):
                    S.op("pe", lambda e: e.matmul(ps[bo][:, c0:512], vt[st][s][:, j, hh * 128:(hh + 1) * 128], Pt[bs][:, c0:512],
                                                  start=(i == 0), stop=(i == nsteps - 1)),
                         reads=[Tvt[st][s], TPt[bs]], writes=[Tps[bo]])
                for i, (s, j, c0) in enumerate(steps):
                    bs = i % 4
                    q0 = qh * 512 + c0
                    q1 = (qh + 1) * 512
                    S.op("pe", lambda e, bs=bs, c0=c0, s=s, j=j, q0=q0, q1=q1: e.matmul(
                        ps[bs][:, c0:512], ktp[hh][s][:, j * 128:(j + 1) * 128], qt[:, q0:q1], start=True, stop=True),
                        reads=[Tkt[hh][s], TQt[head % 2]], writes=[Tps[bs]])
                    bcol = s * 64 + j * 8 + head
                    S.op("act", lambda e, bs=bs, c0=c0, bcol=bcol: e.activation(out=Pt[bs][:, c0:512], in_=ps[bs][:, c0:512], func=AF.Exp,
                                                                               bias=self.biask[:, bcol:bcol + 1], scale=1.0),
                         reads=[Tps[bs], Tb], writes=[TPt[bs]])
                    if s == 3 and j >= 4 * qh:
                        S.op("dve", lambda e, bs=bs, c0=c0: e.tensor_tensor(out=Pt[bs][:, c0:c0 + 128], in0=Pt[bs][:, c0:c0 + 128],
                                                                           in1=cmb[:, 0:128], op=ALU.mult), reads=[TPt[bs], self.Tc], writes=[TPt[bs]])
                    pend.append((i, s, j, c0, bs))
                    if len(pend) > LAG:
                        pv(*pend.pop(0))
                while pend:
                    pv(*pend.pop(0))
                S.op("dve", lambda e, bo=bo: e.reciprocal(out=rden[pb:pb + 64, 0:512], in_=ps[bo][ob:ob + 64, :]), reads=[Tps[bo]], writes=[Trd])
                S.op("dve", lambda e, bo=bo, qh=qh: e.tensor_tensor(out=o_a[pb:pb + 64, c, qh * 512:(qh + 1) * 512], in0=ps[bo][pb:pb + 64, :],
                                                                   in1=rden[pb:pb + 64, 0:512], op=ALU.mult), reads=[Tps[bo], Trd], writes=[Toa])

    def dil_attention(self, l, QdT, TQd, o_d, Tod, HT, SC, QF):
        S, ps, Tps, cmb, mh = self.S, self.ps, self.Tps, self.cmb, self.mh
        kd = [self.av(SC + s * 4096, 4096, BF16, "p (a b) -> p a b", a=2) for s in range(4)]
        vd = [self.av(HT + s * 8192, 8192, BF16, "p (t hp c) -> p t hp c", t=8, hp=2) for s in range(4)]
        Tkd = [T() for _ in range(4)]
        Tvd = [T() for _ in range(4)]
        Uacc = self.av(SC + 16384, 8192, F32, "p (a b) -> p a b", a=2)
        Dacc = self.av(SC + 24576, 8192, F32, "p (a b) -> p a b", a=2)
        TU = T()
        Pt = [self.av(QF + i * 512, 512, BF16) for i in range(4)]
        TPt = [T() for _ in range(4)]
        Qz = [self.av(QF + 2048 + i * 2048, 2048, BF16) for i in range(2)]
        qzall = self.av(QF + 2048, 4096, BF16)
        TQz = [T(), T()]
        S.op("dve", lambda e: e.memset(qzall, 0.0), writes=TQz)
        DIL = (1, 4, 16)
        LAG = 2
        for g in range(3):
            dil = DIL[g]
            for s in range(4):
                def srcT(i, s=s):
                    if s < 3:
                        return self.xbd[i].ap()[s * 128:(s + 1) * 128, :], self.Txbd[i]
                    return self.xbs[i].ap(), self.Txbs[i]
                for cc_ in range(2):
                    c6 = 2 * g + cc_
                    ks, Tks = srcT(3 + c6 // 3)
                    S.dma("sp", lambda e, ks=ks, c6=c6, cc_=cc_, s=s: e.dma_start(out=kd[s][:, cc_, :], in_=ks[:, (c6 % 3) * 1024:(c6 % 3 + 1) * 1024]),
                          reads=[Tks], writes=[Tkd[s]])
                vs, Tvs = srcT(5 + g)
                vsrc = vs.rearrange("p (t hp c) -> p t hp c", t=8, hp=2)
                S.dma("sp", lambda e, vsrc=vsrc, s=s: e.dma_start(out=vd[s], in_=vsrc), reads=[Tvs], writes=[Tvd[s]])
            if dil != 16:
                fl = self.flags
                if dil == 1:
                    ksel = [kd[s_][:, :, 896:1024] for s_ in range(3)]
                    vsel = [vd[s_][:, 7:8, :, :].rearrange("p o hp c -> p (o hp c)") for s_ in range(3)]
                else:
                    ksel = [self.av(SC + s_ * 4096, 4096, BF16) for s_ in range(3)]
                    vsel = [vd[s_].rearrange("p (t two) hp c -> p t two (hp c)", two=2)[:, :, 1:2, :].rearrange("p t o c -> p t (o c)") for s_ in range(3)]
                for sel, Tt in ((ksel, Tkd), (vsel, Tvd)):
                    S.op("dve", lambda e, sel=sel: e.tensor_scalar(out=sel[2], in0=sel[2], scalar1=fl[:, 5:6], scalar2=None, op0=ALU.mult),
                         reads=[Tt[2], self.Tc], writes=[Tt[2]])
                    S.op("dve", lambda e, sel=sel: e.scalar_tensor_tensor(out=sel[2], in0=sel[1], scalar=fl[:, 4:5], in1=sel[2], op0=ALU.mult, op1=ALU.add),
                         reads=[Tt[1], Tt[2], self.Tc], writes=[Tt[2]])
                    S.op("dve", lambda e, sel=sel: e.scalar_tensor_tensor(out=sel[2], in0=sel[0], scalar=fl[:, 3:4], in1=sel[2], op0=ALU.mult, op1=ALU.add),
                         reads=[Tt[0], Tt[2], self.Tc], writes=[Tt[2]])

            def head_blocks(j):
                cc, hh = j // 2, j % 2
                pb = 64 * hh
                chunk = 2 * g + cc
                qz = Qz[hh]
                S.op("dve", lambda e, qz=qz, pb=pb, chunk=chunk: e.tensor_copy(out=qz[pb:pb + 64, :], in_=QdT[pb:pb + 64, chunk, :]),
                     reads=[TQd[chunk]], writes=[TQz[hh]])
                blocks = {0: [], 1: []}
                if dil == 1:
                    for kc in range(8):
                        qb = kc // 4
                        if kc % 4 != 3:
                            blocks[qb].append((3, 128 * kc, kc, 128 * kc, 256, cmb[:, 0:256]))
                        else:
                            blocks[qb].append((3, 128 * kc, kc, 128 * kc, 128, cmb[:, 0:128]))
                            if kc < 7:
                                blocks[qb + 1].append((3, 128 * kc, kc, 128 * kc + 128, 128, cmb[:, 128:256]))
                    blocks[0].append((2, 896, 7, 0, 128, cmb[:, 128:256]))
                elif dil == 4:
                    for r in range(4):
                        qb = r // 2
                        base = 256 * r
                        blocks[qb].append((3, base, 2 * r, base, 256, cmb[:, 0:256]))
                        blocks[qb].append((3, base + 128, 2 * r + 1, base + 128, 128, cmb[:, 0:128]))
                        blocks[qb].append((2, base + 128, 2 * r + 1, base, 128, cmb[:, 128:256]))
                else:
                    for rp in range(8):
                        qb = rp // 4
                        base = 128 * rp
                        blocks[qb].append((3, base, rp, base, 128, cmb[:, 256:384]))
                        for rr in range(3):
                            blocks[qb].append((rr, base, rp, base, 128, mh[:, (3 + rr) * 128:(4 + rr) * 128]))
                return blocks

            def make_stream(j, qb, steps):
                cc, hh = j // 2, j % 2
                pb = 64 * hh
                ob = 64 - pb
                qz = Qz[hh]
                bo = 4 + (self.bank_i % 4)
                self.bank_i += 1
                st = {"i": 0, "n": len(steps), "pend": []}

                def pv(i, s, vtile, q0, nq, bs):
                    S.op("pe", lambda e: e.matmul(ps[bo][:, q0 - qb * 512:q0 - qb * 512 + nq], vd[s][:, vtile, j // 2, (j % 2) * 128:(j % 2) * 128 + 128],
                                                  Pt[bs][:, 0:nq], start=(i == 0), stop=(i == len(steps) - 1)),
                         reads=[Tvd[s], TPt[bs]], writes=[Tps[bo]])

                def step():
                    i = st["i"]
                    (s, k0, vtile, q0, nq, mask) = steps[i]
                    bs = self.dil_bs % 4
                    self.dil_bs += 1
                    S.op("pe", lambda e: e.matmul(ps[bs][:, 0:nq], kd[s][:, cc, k0:k0 + 128], qz[:, q0:q0 + nq], start=True, stop=True),
                         reads=[Tkd[s], TQz[hh]], writes=[Tps[bs]])
                    S.op("act", lambda e: e.activation(out=Pt[bs][:, 0:nq], in_=ps[bs][:, 0:nq], func=AF.Exp), reads=[Tps[bs]], writes=[TPt[bs]])
                    S.op("dve", lambda e: e.tensor_tensor(out=Pt[bs][:, 0:nq], in0=Pt[bs][:, 0:nq], in1=mask[:, 0:nq], op=ALU.mult),
                         reads=[TPt[bs], self.Tc], writes=[TPt[bs]])
                    st["pend"].append((i, s, vtile, q0, nq, bs))
                    if len(st["pend"]) > 1:
                        pv(*st["pend"].pop(0))
                    st["i"] += 1

                def finish():
                    while st["pend"]:
                        pv(*st["pend"].pop(0))
                    for (acc, rows) in ((Uacc, pb), (Dacc, ob)):
                        if dil == 1:
                            dv = acc[rows:rows + 64, cc, qb * 512:(qb + 1) * 512]
                            sv = ps[bo][rows:rows + 64, :]
                        elif dil == 4:
                            dv = acc[rows:rows + 64, cc, :].rearrange("p (i r) -> p r i", r=4)[:, 2 * qb:2 * qb + 2, :]
                            sv = ps[bo][rows:rows + 64, :].rearrange("p (r i) -> p r i", r=2)
                        else:
                            dv = acc[rows:rows + 64, cc, :].rearrange("p (i r) -> p r i", r=8)[:, 4 * qb:4 * qb + 4, :]
                            sv = ps[bo][rows:rows + 64, :].rearrange("p (r i) -> p r i", r=4)
                        if g == 0:
                            S.op("dve", lambda e, dv=dv, sv=sv: e.tensor_copy(out=dv, in_=sv), reads=[Tps[bo]], writes=[TU])
                        else:
                            S.op("dve", lambda e, dv=dv, sv=sv: e.tensor_tensor(out=dv, in0=dv, in1=sv, op=ALU.add), reads=[Tps[bo], TU], writes=[TU])
                st["step"], st["finish"] = step, finish
                return st

            for jp in (0, 2):
                blk = [head_blocks(jp), head_blocks(jp + 1)]
                for qb in range(2):
                    streams = [make_stream(jp + k, qb, blk[k][qb]) for k in range(2)]
                    active = list(streams)
                    while active:
                        for st in list(active):
                            if st["i"] < st["n"]:
                                st["step"]()
                            else:
                                st["finish"]()
                                active.remove(st)
        S.barrier()
        rtmp = self.av(QF + 4096, 4096, F32)
        Trt = T()
        for cc in range(2):
            for hh in range(2):
                pb = 64 * hh
                ob = 64 - pb
                S.op("dve", lambda e, cc=cc, pb=pb, ob=ob: e.reciprocal(out=rtmp[pb:pb + 64, :], in_=Dacc[ob:ob + 64, cc, :]), reads=[TU], writes=[Trt])
                S.op("dve", lambda e, cc=cc, pb=pb: e.tensor_tensor(out=o_d[pb:pb + 64, cc, :], in0=Uacc[pb:pb + 64, cc, :], in1=rtmp[pb:pb + 64, :], op=ALU.mult),
                     reads=[TU, Trt], writes=[Tod])

    def sgu(self, l, hT, ThT, o_c, Toc, SC):
        S, ps, Tps, cmb, d = self.S, self.ps, self.Tps, self.cmb, self.din
        S.barrier()
        big = [self.av(SC + i * 8192, 8192, BF16) for i in range(2)]
        Tbig = [T(), T()]
        vg = self.av(SC + 16384, 2048, F32)
        vn = self.av(SC + 18432, 4096, BF16, "p (a b) -> p a b", a=4)
        ut = self.av(SC + 22528, 2048, F32)
        wsT = self.av(SC + 24576, 2048, F32)
        wsb = self.av(SC + 26624, 1024, BF16)
        gt = self.av(SC + 27648, 2048, F32)
        sb_ = self.av(SC + 29696, 2048, F32)
        ssc = self.av(SC + 31744, 32, F32)
        Tvg, Tvn, Tut, Tw_, Tss = T(), T(), T(), T(), T()
        for pc in range(2):
            self.load_big(d["w_cv"][l, pc], big[pc], Tbig[pc], 4096)
        S.dma("sp", lambda e: e.dma_start(out=wsT, in_=d["sguwT"][l]), writes=[Tw_])
        S.dma("sp", lambda e: e.dma_start(out=gt, in_=d["sgug"][l]), writes=[Tw_])
        S.dma("sp", lambda e: e.dma_start(out=sb_[0:1, :], in_=d["sgub"][l]), writes=[Tw_])
        for g in range(4):
            S.op("dve", lambda e, g=g: e.tensor_tensor(out=wsb[:, g * 128:(g + 1) * 128], in0=wsT[:, g * 128:(g + 1) * 128], in1=cmb[:, 0:128], op=ALU.mult),
                 reads=[Tw_, self.Tc], writes=[Tw_])
        hrhs1 = lambda kc, half: hT[:, kc, half * 512:(half + 1) * 512]
        bi = 0
        for half in range(2):
            for t4 in range(4):
                tt = half * 4 + t4
                b = 4 + bi % 2
                bi += 1
                for pc in range(2):
                    fns = [lambda e, kc=kc, b=b, tt=tt, pc=pc: e.matmul(ps[b][:, pc * 256:(pc + 1) * 256], hT[:, kc, tt * 128:(tt + 1) * 128],
                                                                    big[pc][:, kc * 256:(kc + 1) * 256], start=(kc == 0), stop=(kc == 15))
                           for kc in range(16)]
                    S.op("pe", fns, reads=ThT + [Tbig[pc]], writes=[Tps[b]])
                S.op("act", lambda e, b=b: e.activation(out=vg, in_=ps[b][:, :], func=AF.Gelu_apprx_tanh), reads=[Tps[b]], writes=[Tvg])
                S.op("dve", lambda e, t4=t4: e.scalar_tensor_tensor(out=vn[:, t4, :], in0=vg, scalar=1.0, in1=vg, op0=ALU.mult, op1=ALU.mult,
                                                                   accum_out=ssc[:, 0:1]), reads=[Tvg], writes=[Tvn, Tss])
                S.op("act", lambda e: e.activation(out=ssc[:, 1:2], in_=ssc[:, 0:1], func=AF.Sqrt, bias=self.cst[:, 0:1], scale=1.0 / 512.0),
                     reads=[Tss, self.Tc], writes=[Tss])
                S.op("dve", lambda e: e.reciprocal(out=ssc[:, 1:2], in_=ssc[:, 1:2]), reads=[Tss], writes=[Tss])
                S.op("dve", lambda e, t4=t4: e.scalar_tensor_tensor(out=vn[:, t4, :], in0=vg, scalar=ssc[:, 1:2], in1=gt, op0=ALU.mult, op1=ALU.mult),
                     reads=[Tvg, Tss, Tw_], writes=[Tvn])
            for g in range(4):
                w, Tw = self.load_w(d["w_cu"][l, g], 2048)
                bu = self.next_pair()[0]
                self.proj_b(lambda kc, w=w: w[:, kc * 128:(kc + 1) * 128], Tw, 16, hrhs1, ThT, (bu,), halves=(half,))
                S.op("act", lambda e, bu=bu: e.activation(out=ut, in_=ps[bu][:, :], func=AF.Gelu_apprx_tanh), reads=[Tps[bu]], writes=[Tut])
                bm = 6 + g % 2
                for t4 in range(4):
                    S.op("pe", [lambda e, t4=t4, g=g, bm=bm: e.matmul(ps[bm][:, t4 * 128:(t4 + 1) * 128], vn[:, t4, g * 128:(g + 1) * 128],
                                                                   wsb[:, g * 128:(g + 1) * 128], start=True, stop=False),
                                lambda e, t4=t4, g=g, bm=bm: e.matmul(ps[bm][:, t4 * 128:(t4 + 1) * 128], self.onesf[0:1, 0:128],
                                                                   sb_[0:1, g * 128:(g + 1) * 128], start=False, stop=True)],
                         reads=[Tvn, Tw_, self.Tc], writes=[Tps[bm]])
                S.op("dve", lambda e, g=g, bm=bm, half=half: e.tensor_tensor(out=o_c[:, g, half * 512:(half + 1) * 512], in0=ut, in1=ps[bm][:, :], op=ALU.mult),
                     reads=[Tut, Tps[bm]], writes=[Toc])

    def ffn(self, l, hT, ThT, OB, R0):
        S, ps, Tps, d = self.S, self.ps, self.Tps, self.din
        xhs, xhd = self.xh_src.ap(), self.xh_dst.ap()
        S.dma("sp", lambda e: e.dma_start(out=xhs.rearrange("p (k t) -> p k t", t=2), in_=hT[:, :, 1022:1024]), reads=ThT, writes=[self.Txhs])
        S.dma("pool", lambda e: e.collective_compute("AllGather", ALU.bypass, replica_groups=RG, dma_qos="P3",
                                                     ins=[self.xh_src.ap().opt()], outs=[self.xh_dst.ap().opt()]),
              reads=[self.Txhs], writes=[self.Txhd], dsem=self.cc_ds, inc=1, track=False)
        Th = T()
        S.dma("sp", lambda e: e.dma_start(out=self.hcand[:, :].rearrange("p (r c) -> p r c", r=3),
                                          in_=xhd[0:384, :].rearrange("(r p) c -> p r c", p=128)), reads=[self.Txhd], writes=[Th])
        S.op("dve", lambda e: e.tensor_scalar(out=self.hhalo[:, :], in0=self.hcand[:, 0:32], scalar1=self.flags[:, 3:4], scalar2=None, op0=ALU.mult),
             reads=[Th, self.Tc], writes=[Th])
        for r in (1, 2):
            S.op("dve", lambda e, r=r: e.scalar_tensor_tensor(out=self.hhalo[:, :], in0=self.hcand[:, 32 * r:32 * r + 32], scalar=self.flags[:, 3 + r:4 + r],
                                                             in1=self.hhalo[:, :], op0=ALU.mult, op1=ALU.add), reads=[Th, self.Tc], writes=[Th])
        S.barrier()
        act = [self.av(R0 + i * 22528, 22528, BF16, "p (a b) -> p a b", a=11) for i in range(2)]
        Tact = [[T() for _ in range(11)] for _ in range(2)]
        t12 = [self.av(R0 + 45056 + i * 4096, 4096, F32) for i in range(2)]
        Tt12 = [T(), T()]
        bufs = [[self.av(OB + (k * 2 + i) * 4112, 4112, F32) for i in range(2)] for k in range(2)]
        Tbuf = [[T(), T()], [T(), T()]]
        sl = self.av(OB + 16448, 4096, F32)
        Tsl = T()
        hs_i = 0
        dn_i = 0
        hrhs = lambda kc, half: hT[:, kc, half * 512:(half + 1) * 512]
        for gk in range(4):
            for mm in range(11):
                m = gk * 11 + mm
                q = m % 2
                for k in range(2):
                    si = 2 * m + k
                    w, Tw = self.load_w(d["w_up"][l, si], 2048)
                    pr = self.next_pair()
                    self.proj_b(lambda kc, w=w: w[:, kc * 128:(kc + 1) * 128], Tw, 16, hrhs, ThT, pr)
                    hsl = (hs_i % 256) * 2
                    hs_i += 1
                    fns = [lambda e, kc=kc, w=w, hsl=hsl: e.matmul(ps[7][:, hsl:hsl + 2], w[:, kc * 128:(kc + 1) * 128], self.hhalo[:, 2 * kc:2 * kc + 2],
                                                                start=(kc == 0), stop=(kc == 15)) for kc in range(16)]
                    S.op("pe", fns, reads=[Tw, Th], writes=[Tps[7]])
                    bf = bufs[k][q]
                    for half in range(2):
                        S.op("act", lambda e, bf=bf, half=half, b=pr[half]: e.activation(out=bf[:, 2 + half * 512:2 + (half + 1) * 512], in_=ps[b][:, :], func=AF.Copy),
                             reads=[Tps[pr[half]]], writes=[Tbuf[k][q]])
                    S.op("dve", lambda e, bf=bf, hsl=hsl: e.tensor_copy(out=bf[:, 0:2], in_=ps[7][:, hsl:hsl + 2]), reads=[Tps[7]], writes=[Tbuf[k][q]])
                    wc = lambda tap, si=si: self.fcw[:, l * 264 + tap * 88 + si: l * 264 + tap * 88 + si + 1]
                    tk = t12[k]
                    S.op("dve", lambda e, bf=bf, tk=tk, wc=wc: e.tensor_scalar(out=tk, in0=bf[:, 2:1026], scalar1=wc(2), scalar2=None, op0=ALU.mult),
                         reads=[Tbuf[k][q], self.Tc], writes=[Tt12[k]])
                    S.op("dve", lambda e, bf=bf, tk=tk, wc=wc: e.scalar_tensor_tensor(out=tk, in0=bf[:, 1:1025], scalar=wc(1), in1=tk, op0=ALU.mult, op1=ALU.add),
                         reads=[Tbuf[k][q], self.Tc, Tt12[k]], writes=[Tt12[k]])
                    S.op("dve", lambda e, bf=bf, tk=tk, wc=wc: e.scalar_tensor_tensor(out=tk, in0=bf[:, 0:1024], scalar=wc(0), in1=tk, op0=ALU.mult, op1=ALU.add),
                         reads=[Tbuf[k][q], self.Tc, Tt12[k]], writes=[Tt12[k]])
                S.op("act", lambda e: e.activation(out=sl, in_=t12[0], func=AF.Silu), reads=[Tt12[0]], writes=[Tsl])
                S.op("dve", lambda e, gk=gk, mm=mm: e.tensor_tensor(out=act[gk % 2][:, mm, :], in0=sl, in1=t12[1], op=ALU.mult),
                     reads=[Tsl, Tt12[1]], writes=[Tact[gk % 2][mm]])
            for j in range(16):
                w, Tw = self.load_w(d["w_dn"][l, gk * 16 + j], 1408)
                for half in range(2):
                    b = 4 + dn_i % 3
                    dn_i += 1
                    fns = [lambda e, kc=kc, w=w, b=b, half=half, gk=gk: e.matmul(ps[b][:, :], w[:, kc * 128:(kc + 1) * 128],
                                                                              act[gk % 2][:, kc, half * 512:(half + 1) * 512],
                                                                              start=(kc == 0), stop=(kc == 10)) for kc in range(11)]
                    S.op("pe", fns, reads=[Tw] + Tact[gk % 2], writes=[Tps[b]])
                    hs = slice(half * 512, (half + 1) * 512)
                    S.op("dve", lambda e, j=j, hs=hs, b=b: e.tensor_tensor(out=self.xT[:, j, hs], in0=self.xT[:, j, hs], in1=ps[b][:, :], op=ALU.add),
                         reads=[Tps[b]], writes=[self.TxT[j]])

    def finish(self):
        S = self.S
        self.tick_deferred(flush=True)
        S.barrier()
        out = self.av(0, 65536, F32, "p (a b) -> p a b", a=16)
        Tout = [T() for _ in range(16)]
        self.rmsnorm(lambda kc: self.gfin[:, kc:kc + 1], None, None, 65536 + 8192, out_f32=out, Tout=Tout)
        tks = []
        for kc in range(16):
            tks.append(S.dma("sp", lambda e, kc=kc: e.dma_start(out=self.yT[:, kc, :], in_=out[:, kc, :]), reads=[Tout[kc]]))
        for tk in tks:
            S.wait_ticket("sp", tk)
        for tk in S.recent.values():
            S.wait_ticket("sp", tk)

    def build(self):
        self.setup()
        for l in range(self.L):
            self.layer(l)
        self.finish()
        self.st.close()
        return self.nc


TAPS = (("t_h", [128, 16, NTOK]), ("t_oa", [128, 4, NTOK]), ("t_od", [128, 2, NTOK]), ("t_ob", [128, 4, NTOK]),
        ("t_oc", [128, 4, NTOK]), ("t_x1", [128, 16, NTOK]), ("t_x2", [128, 16, NTOK]))


def run(inputs, depth=DEPTH, taps=(), n_cores=8, stop=99):
    maps = prep_inputs(inputs, depth)
    prog = Prog(depth, taps, stop)
    nc = prog.build()
    keep = set(input_specs(depth, stop).keys())
    maps = [{k: v for k, v in m.items() if k in keep} for m in maps]
    res = run_bass_kernel_spmd(nc, maps[:n_cores], core_ids=list(range(n_cores)))
    return res.results


def assemble(results, name="yT"):
    y = np.empty((2, 4096, results[0][name].shape[1] * 128), np.float32)
    for c in range(8):
        b, R = c // 4, c % 4
        t = results[c][name]
        y[b, R * NTOK:(R + 1) * NTOK, :] = t.transpose(2, 1, 0).reshape(NTOK, -1)
    return y


def kernel(**inputs):
    results = run(inputs, DEPTH)
    return assemble(results)
```

```python
import numpy as np
from contextlib import ExitStack
import concourse.bass as bass
import concourse.mybir as mybir
from concourse.bass_utils import run_bass_kernel_spmd

F32 = mybir.dt.float32
BF16 = mybir.dt.bfloat16
I32 = mybir.dt.int32
U8 = mybir.dt.uint8
AF = mybir.ActivationFunctionType
ALU = mybir.AluOpType

DEPTH = 4
NTOK = 1024
EPS = 1e-6
NEG = -30000.0
RG = [[0, 1, 2, 3], [4, 5, 6, 7]]
O_AQ, O_AK, O_AV, O_AF, O_XB, O_GB, O_GC, O_CU, O_CV, O_DQ, O_DK, O_DV, O_GT = (
    0, 512, 1024, 1536, 1544, 2056, 2568, 3080, 3592, 4104, 4872, 5640, 6408)
DFF = 5632
XB_KF, XB_VF, XB_KD, XB_VD, XBW = 0, 4096, 4096 + 8192, 4096 + 8192 + 6144, 4096 + 8192 + 6144 + 12288
XFW = 72
ENGS = ("pe", "act", "dve", "pool", "sp")


class T:
    __slots__ = ("w", "r")

    def __init__(self):
        self.w = None
        self.r = {}


class DSem:
    def __init__(self, sem, key):
        self.sem = sem
        self.key = key
        self.count = 0
        self.last = None


class Sched:
    def __init__(self, nc, stack, ndma=72):
        self.nc = nc
        self.eh = {"pe": nc.tensor, "act": nc.scalar, "dve": nc.vector, "pool": nc.gpsimd, "sp": nc.sync}
        self.sem = {e: stack.enter_context(nc.semaphore("s_" + e)) for e in ENGS}
        self.cnt = {e: 0 for e in ENGS}
        self.known = {e: {} for e in ENGS}
        self.snap = {}
        self.stack = stack
        self.nds = 0
        self.pool_ds = [self.dsem() for _ in range(ndma)]
        self.rr = 0
        self.recent = {}
        self.last_barrier = []
        self.ninstr = 0
        self.nwaits = 0

    def dsem(self):
        self.nds += 1
        return DSem(self.stack.enter_context(self.nc.semaphore("d%d" % self.nds)), "d%d" % self.nds)

    def _wait(self, eng, ticket):
        semobj, key, val = ticket
        kn = self.known[eng]
        if kn.get(key, 0) >= val:
            return
        self.nwaits += 1
        self.eh[eng].wait_ge(semobj, val)
        kn[key] = val
        sn = self.snap.get((key, val))
        if sn:
            for k, v in sn.items():
                if kn.get(k, 0) < v:
                    kn[k] = v

    def _deps(self, eng, reads, writes):
        for t in reads:
            if t.w is not None:
                self._wait(eng, t.w)
        for t in writes:
            if t.w is not None:
                self._wait(eng, t.w)
            for tk in t.r.values():
                self._wait(eng, tk)

    def _commit(self, ticket, reads, writes):
        key = ticket[1]
        for t in reads:
            t.r[key] = ticket
        for t in writes:
            t.w = ticket
            t.r = {}

    def op(self, eng, fns, reads=(), writes=(), arena=False):
        if not isinstance(fns, (list, tuple)):
            fns = [fns]
        if arena:
            for tk in self.last_barrier:
                self._wait(eng, tk)
        self._deps(eng, reads, writes)
        self.cnt[eng] += 1
        c = self.cnt[eng]
        s = self.sem[eng]
        if eng == "pe":
            self.known[eng][eng] = c
        ticket = (s, eng, c)
        self.snap[(eng, c)] = dict(self.known[eng])
        eh = self.eh[eng]
        for fn in fns[:-1]:
            fn(eh)
        fns[-1](eh).then_inc(s, 1)
        self.ninstr += len(fns)
        self._commit(ticket, reads, writes)
        return ticket

    def dma(self, eng, fn, reads=(), writes=(), dsem=None, arena=False, inc=16, track=True):
        if dsem is None:
            dsem = self.pool_ds[self.rr % len(self.pool_ds)]
            self.rr += 1
        if dsem.last is not None:
            self._wait(eng, dsem.last)
        if arena:
            for tk in self.last_barrier:
                self._wait(eng, tk)
        self._deps(eng, reads, writes)
        dsem.count += inc
        ticket = (dsem.sem, dsem.key, dsem.count)
        dsem.last = ticket
        self.snap[(dsem.key, dsem.count)] = dict(self.known[eng])
        if inc == 16:
            fn(self.eh[eng]).then_inc(dsem.sem, 16)
        else:
            fn(self.eh[eng]).then_inc(dsem.sem)
        self.ninstr += 1
        if track:
            self.recent[dsem.key] = ticket
        self._commit(ticket, reads, writes)
        return ticket

    def barrier(self):
        grp = ("pe", "act", "dve", "sp")
        tks = [(self.sem[e], e, self.cnt[e]) for e in grp if self.cnt[e] > 0]
        tks += list(self.recent.values())
        self.recent = {}
        for e in grp:
            for tk in tks:
                if tk[1] != e:
                    self._wait(e, tk)
        pk = (self.sem["pool"], "pool", self.cnt["pool"])
        if self.cnt["pool"] > 0:
            for e in grp:
                self._wait(e, pk)
        self.last_barrier = tks

    def wait_ticket(self, eng, ticket):
        self._wait(eng, ticket)

    def emit(self, block):
        m = {"pe": block.tensor, "act": block.scalar, "dve": block.vector, "pool": block.gpsimd, "sp": block.sync}
        for e in ENGS:
            lst = self.ops[e]
            if not lst:
                continue

            def body(eh, lst=lst):
                for f in lst:
                    f(eh)
            m[e](body)


def slab_b(w, kc, n):
    return np.ascontiguousarray(w.reshape(kc, 128, n, 128).transpose(2, 1, 0, 3)).reshape(n, 128, kc * 128)


def slab_a(w, kc, n):
    return np.ascontiguousarray(w.reshape(kc, 128, n).transpose(1, 0, 2)).reshape(128, kc * n)


def cols16(v):
    sh = v.shape
    k = sh[-1] // 128
    a = v.reshape(sh[:-1] + (k, 128))
    return np.ascontiguousarray(np.moveaxis(a, -1, 0))


STAGE_OF = {"w_k": 1, "w_v": 1, "w_f": 2, "w_cb": 3, "w_dv": 4, "w_dk": 4, "w_dq": 5, "w_q": 5, "w_cv": 8, "w_cu": 8,
            "sgug": 8, "sguwT": 8, "sgub": 8, "w_g": 9, "w_bf": 9, "w_bc": 9, "w_bs": 9, "w_bd": 9, "w_o": 9,
            "w_up": 10, "w_dn": 10, "w_pg": 11, "w_pp": 11, "pT": 11}


def input_specs(depth, stop=99):
    sp = input_specs_all(depth)
    return {k: v for k, v in sp.items() if STAGE_OF.get(k, 0) <= stop}


def input_specs_all(depth):
    L = depth
    sp = {
        "xT": ([128, 16, NTOK], F32), "pT": ([L, 128, 2 * NTOK], F32), "pos": ([1, NTOK], I32),
        "gcols": ([128, L * 3 * 16], F32), "gfin": ([128, 16], F32), "foxb": ([8, L], F32),
        "scw": ([128, L * 12], F32), "sgug": ([L, 128, 512], F32), "sguwT": ([L, 128, 512], F32),
        "sgub": ([L, 1, 512], F32), "fcw": ([128, L * 3 * 88], F32),
        "w_q": ([L, 4, 128, 2048], F32), "w_k": ([L, 4, 128, 2048], F32), "w_v": ([L, 2, 128, 4096], F32),
        "w_f": ([L, 128, 128], F32), "w_cb": ([L, 12, 128, 2048], F32), "w_cu": ([L, 4, 128, 2048], F32),
        "w_cv": ([L, 2, 128, 4096], F32), "w_dq": ([L, 6, 128, 2048], F32), "w_dk": ([L, 6, 128, 2048], F32),
        "w_dv": ([L, 3, 128, 4096], F32), "w_g": ([L, 64, 128, 2048], F32),
        "w_bf": ([L, 16, 128, 512], F32), "w_bc": ([L, 16, 128, 512], F32), "w_bs": ([L, 16, 128, 512], F32),
        "w_bd": ([L, 16, 128, 256], F32), "w_o": ([L, 16, 128, 2048], F32), "w_up": ([L, 88, 128, 2048], F32),
        "w_dn": ([L, 64, 128, 1408], F32), "w_pg": ([L, 16, 128, 2048], F32), "w_pp": ([L, 16, 128, 256], F32),
        "cm": ([128, 8 * 128], F32), "flags": ([128, 16], F32), "flg2": ([128, 48], F32), "ropec": ([128, 4], F32),
    }
    return sp


def host_consts():
    k = np.arange(128)[:, None]
    q = np.arange(128)[None, :]
    tri_le = (k <= q).astype(np.float32)
    tri_ge = (k >= q).astype(np.float32)
    same = ((k % 2) == (q % 2)).astype(np.float32)
    bd_le = same * ((k // 2) <= (q // 2))
    bd_ge = same * ((k // 2) >= (q // 2))
    ones = np.ones((128, 128), np.float32)
    ident = np.eye(128, dtype=np.float32)
    psw = np.zeros((128, 128), np.float32)
    for m in range(128):
        r = m % 64
        if r < 8:
            psw[m + 8, m] = 1.0
        elif r < 16:
            psw[m - 8, m] = 1.0
    cm = np.concatenate([tri_le, tri_ge, bd_le, same, bd_ge, ones, ident, psw], axis=1).astype(np.float32)
    ropec = np.zeros((128, 4), np.float32)
    half = 8
    inv = (500000.0 ** (-np.arange(half, dtype=np.float32) * (2.0 / 16))).astype(np.float32)
    for m in range(128):
        r = m % 64
        if r < 16:
            ropec[m, 0] = inv[r % 8]
            ropec[m, 1] = -1.0 if r < 8 else 1.0
    return cm, ropec


def prep_inputs(inp, depth):
    L = depth
    f = lambda a: np.asarray(a, dtype=np.float32)
    x = f(inp["x"])
    p = f(inp["p"])
    pos = np.asarray(inp["positions"]).astype(np.int32)
    w_in = f(inp["w_in"])
    sh = {}
    sh["gcols"] = np.ascontiguousarray(np.stack([cols16(f(inp[k])[:L]) for k in ("norm_mix_g", "norm_ffn_g", "norm_ple_g")], axis=2)
                                       ).reshape(128, L * 3 * 16)
    sh["gfin"] = cols16(f(inp["final_norm_g"]))
    sh["foxb"] = np.ascontiguousarray(f(inp["fox_forget_b"])[:L].T)
    sh["scw"] = np.ascontiguousarray(cols16(f(inp["shortconv_w"])[:L])).reshape(128, L * 12)
    sh["sgug"] = np.ascontiguousarray(np.broadcast_to(f(inp["sgu_norm_g"])[:L, None, :], (L, 128, 512)))
    sh["sguwT"] = np.ascontiguousarray(f(inp["sgu_w"])[:L].transpose(0, 3, 1, 2)).reshape(L, 128, 512)
    sh["sgub"] = np.ascontiguousarray(f(inp["sgu_b"])[:L]).reshape(L, 1, 512)
    fc = f(inp["ffn_conv_w"])[:L]
    fcc = cols16(fc)
    order = np.empty(88, np.int64)
    order[0::2] = np.arange(44)
    order[1::2] = 44 + np.arange(44)
    sh["fcw"] = np.ascontiguousarray(fcc[..., order]).reshape(128, L * 3 * 88)
    W = {k: [] for k in ("w_q", "w_k", "w_v", "w_f", "w_cb", "w_cu", "w_cv", "w_dq", "w_dk", "w_dv", "w_g", "w_bf", "w_bc",
                         "w_bs", "w_bd", "w_o", "w_up", "w_dn", "w_pg", "w_pp")}
    for l in range(L):
        wi = w_in[l]
        W["w_q"].append(slab_b(wi[:, O_AQ:O_AQ + 512], 16, 4))
        W["w_k"].append(slab_b(wi[:, O_AK:O_AK + 512], 16, 4))
        W["w_v"].append(np.stack([slab_a(wi[:, O_AV + i * 256:O_AV + (i + 1) * 256], 16, 256) for i in range(2)]))
        wf = np.zeros((128, 128), np.float32)
        wf[:, :] = slab_a(wi[:, O_AF:O_AF + 8], 16, 8)
        W["w_f"].append(wf)
        cb = []
        for j in range(4):
            for o in (O_XB, O_GC, O_GB):
                cb.append(slab_b(wi[:, o + j * 128:o + (j + 1) * 128], 16, 1)[0])
        W["w_cb"].append(np.stack(cb))
        W["w_cu"].append(slab_b(wi[:, O_CU:O_CU + 512], 16, 4))
        W["w_cv"].append(np.stack([slab_a(wi[:, O_CV + i * 256:O_CV + (i + 1) * 256], 16, 256) for i in range(2)]))
        W["w_dq"].append(slab_b(wi[:, O_DQ:O_DQ + 768], 16, 6))
        W["w_dk"].append(slab_b(wi[:, O_DK:O_DK + 768], 16, 6))
        W["w_dv"].append(np.stack([slab_a(wi[:, O_DV + g * 256:O_DV + (g + 1) * 256], 16, 256) for g in range(3)]))
        W["w_g"].append(slab_b(wi[:, O_GT:O_GT + 8192], 16, 64))
        W["w_bf"].append(slab_b(f(inp["w_br_fox"])[l], 4, 16))
        W["w_bc"].append(slab_b(f(inp["w_br_conv"])[l], 4, 16))
        W["w_bs"].append(slab_b(f(inp["w_br_sgu"])[l], 4, 16))
        W["w_bd"].append(slab_b(f(inp["w_br_dil"])[l], 2, 16))
        W["w_o"].append(slab_b(f(inp["w_out"])[l], 16, 16))
        wu = f(inp["w_up"])[l]
        su = slab_b(wu, 16, 88)
        W["w_up"].append(np.ascontiguousarray(su[order]))
        wd = f(inp["w_down"])[l]
        W["w_dn"].append(np.ascontiguousarray(wd.reshape(4, 11, 128, 16, 128).transpose(0, 3, 2, 1, 4)).reshape(64, 128, 1408))
        W["w_pg"].append(slab_b(f(inp["w_ple_gate"])[l], 16, 16))
        W["w_pp"].append(slab_b(f(inp["w_ple_proj"])[l], 2, 16))
    for k in W:
        sh[k] = np.stack(W[k])
    cm, ropec = host_consts()
    sh["cm"] = cm
    sh["ropec"] = ropec
    maps = []
    for c in range(8):
        b, R = c // 4, c % 4
        m = dict(sh)
        xs = x[b, R * NTOK:(R + 1) * NTOK, :]
        m["xT"] = np.ascontiguousarray(xs.T.reshape(16, 128, NTOK).transpose(1, 0, 2))
        ps = p[:L, b, R * NTOK:(R + 1) * NTOK, :]
        m["pT"] = np.ascontiguousarray(ps.transpose(0, 2, 1).reshape(L, 2, 128, NTOK).transpose(0, 2, 1, 3)).reshape(L, 128, 2 * NTOK)
        m["pos"] = np.ascontiguousarray(pos[b:b + 1, R * NTOK:(R + 1) * NTOK])
        fl = np.zeros((128, 16), np.float32)
        for r in range(3):
            fl[:, r] = 1.0 if r < R else 0.0
            fl[:, 3 + r] = 1.0 if r == R - 1 else 0.0
            fl[:, 6 + r] = 1.0 if r == R - 2 else 0.0
            fl[:, 9 + r] = 0.0 if r < R else NEG
        m["flags"] = fl
        f2 = np.zeros((128, 48), np.float32)
        for r in range(3):
            f2[:, r * 8:(r + 1) * 8] = fl[0, r]
            f2[:, 24 + r * 8:24 + (r + 1) * 8] = fl[0, 9 + r]
        m["flg2"] = f2
        maps.append(m)
    return maps


class Prog:
    def __init__(self, depth, taps=(), stop=99):
        self.L = depth
        self.taps = taps
        self.stop = stop
        self.nc = nc = bass.Bass("TRN2", target_bir_lowering=False)
        self.st = st = ExitStack()
        self.din = {}
        for name, (shape, dt) in input_specs(depth, stop).items():
            self.din[name] = nc.dram_tensor(name, shape, dt, kind="ExternalInput").ap()
        self.yT = nc.dram_tensor("yT", [128, 16, NTOK], F32, kind="ExternalOutput").ap()
        self.tap_out = {}
        for name, shape in taps:
            self.tap_out[name] = nc.dram_tensor(name, shape, F32, kind="ExternalOutput").ap()
        self.XW = [4096, 4096, 4096, 3072, 3072, 4096, 4096, 4096]
        self.xbs = [nc.dram_tensor("xbs%d" % i, [128, w], BF16) for i, w in enumerate(self.XW)]
        self.xbd = [nc.dram_tensor("xbd%d" % i, [512, w], BF16) for i, w in enumerate(self.XW)]
        self.rope_dram = nc.dram_tensor("rope_dram", [128, 2048], F32)
        self.Trope = T()
        self.Tbs = [T(), T()]
        self.xf_src = nc.dram_tensor("xf_src", [128, XFW], F32)
        self.xf_dst = nc.dram_tensor("xf_dst", [512, XFW], F32)
        self.xh_src = nc.dram_tensor("xh_src", [128, 32], BF16)
        self.xh_dst = nc.dram_tensor("xh_dst", [512, 32], BF16)
        sb = lambda n, s, d: st.enter_context(nc.sbuf_tensor(n, s, d))
        self.xT = sb("xT_sb", [128, 16, NTOK], F32)
        self.ring = sb("ring", [128, 4, 2048], BF16)
        self.rstd = sb("rstd", [128, NTOK], F32)
        self.cmb = sb("cmb", [128, 6 * 128], BF16)
        self.pswb = sb("pswb", [128, 128], BF16)
        self.ident = sb("ident", [128, 128], F32)
        self.mh = sb("mh", [128, 6 * 128], BF16)
        self.flags = sb("flags_sb", [128, 16], F32)
        self.flg2 = sb("flg2_sb", [128, 48], F32)
        self.onesf = sb("onesf", [128, 128], F32)
        self.gcols = sb("gcols_sb", [128, depth * 48], F32)
        self.gfin = sb("gfin_sb", [128, 16], F32)
        self.scw = sb("scw_sb", [128, depth * 12], F32)
        self.fcw = sb("fcw_sb", [128, depth * 264], F32)
        self.negb = sb("negb", [8, depth], F32)
        self.ropec = sb("ropec_sb", [128, 4], F32)
        self.cst = sb("cst", [128, 4], F32)
        self.nfk = sb("nfk", [128, 64], F32)
        self.biask = sb("biask", [128, 4 * 64], F32)
        self.dtmp = sb("dtmp", [128, 64], F32)
        self.zp = sb("zp", [128, 16], F32)
        self.gb2 = sb("gb2", [128, 8], F32)
        self.zl = sb("zl", [128, 8], F32)
        self.hcand = sb("hcand", [128, 96], BF16)
        self.hhalo = sb("hhalo", [128, 32], BF16)
        self.ARENA = 114944
        self.arena = sb("arena", [128, self.ARENA], U8)
        self.ps = [st.enter_context(nc.psum_tensor("ps%d" % i, [128, 512], F32)) for i in range(8)]
        self.Tps = [T() for _ in range(8)]
        self.S = Sched(nc, st)
        self.block = st.enter_context(nc.Block())
        self.ring_T = [T() for _ in range(4)]
        self.ring_ds = [self.S.dsem() for _ in range(4)]
        self.ring_i = 0
        self.deferred = []
        self.cc_ds = self.S.dsem()
        self.TxT = [T() for _ in range(16)]
        self.ThT = [T() for _ in range(16)]
        self.Tc = T()
        self.Trstd = T()
        self.pair_i = 0
        self.bank_i = 0
        self.dil_bs = 0
        self.Txbs = [T() for _ in range(8)]
        self.Txbd = [T() for _ in range(8)]
        self.Txfs = T()
        self.Txfd = T()
        self.Txhs = T()
        self.Txhd = T()

    def av(self, off, nbytes, dt, pat=None, **kw):
        assert off + nbytes <= self.ARENA, (off, nbytes)
        ap = self.arena[:, off:off + nbytes].bitcast(dt)
        if pat:
            ap = ap.rearrange(pat, **kw)
        return ap

    def load_w(self, src, n):
        i = self.ring_i % 4
        self.ring_i += 1
        dst = self.ring[:, i, 0:n]
        self.S.dma("pool", lambda e: e.dma_start(out=dst, in_=src), writes=[self.ring_T[i]], dsem=self.ring_ds[i], track=False)
        self.tick_deferred()
        return dst, self.ring_T[i]

    def defer(self, fn, n=3):
        self.deferred.append([n, fn])

    def tick_deferred(self, flush=False):
        keep = []
        for it in self.deferred:
            it[0] -= 1
            if it[0] <= 0 or flush:
                it[1]()
            else:
                keep.append(it)
        self.deferred = keep

    def load_big(self, src, dst, Tdst, n, arena=True):
        s3 = src.rearrange("p (a b) -> p a b", b=2048)
        d3 = dst.rearrange("p (a b) -> p a b", b=2048)
        self.S.dma("pool", lambda e: e.dma_start(out=d3, in_=s3), writes=[Tdst], arena=arena, track=arena)

    def next_pair(self, npairs=2):
        i = (self.pair_i % npairs) * 2
        self.pair_i += 1
        return i, i + 1

    def proj_b(self, lhs_fn, Tw, kc_n, rhs_fn, rhs_T, banks, m=128, halves=(0, 1)):
        S = self.S
        for hi, half in enumerate(halves):
            b = banks[hi]
            fns = []
            for kc in range(kc_n):
                fns.append(lambda e, b=b, kc=kc, half=half: e.matmul(
                    self.ps[b][0:m, :], lhs_fn(kc), rhs_fn(kc, half), start=(kc == 0), stop=(kc == kc_n - 1)))
            S.op("pe", fns, reads=[Tw] + list(rhs_T), writes=[self.Tps[b]])

    def rmsnorm(self, gcol_fn, hT, ThT, sq_off, out_f32=None, Tout=None):
        S = self.S
        sq = [self.av(sq_off + i * 2048, 2048, BF16) for i in range(2)]
        Tsq = [T(), T()]
        ones = self.cmb[:, 5 * 128:6 * 128]
        for kc in range(16):
            s = sq[kc % 2]
            if kc % 2 == 0:
                S.op("act", lambda e, s=s, kc=kc: e.activation(out=s, in_=self.xT[:, kc, :], func=AF.Square),
                     reads=[self.TxT[kc]], writes=[Tsq[kc % 2]])
            else:
                S.op("dve", lambda e, s=s, kc=kc: e.tensor_tensor(out=s, in0=self.xT[:, kc, :], in1=self.xT[:, kc, :], op=ALU.mult),
                     reads=[self.TxT[kc]], writes=[Tsq[kc % 2]])
            for half in range(2):
                S.op("pe", lambda e, s=s, kc=kc, half=half: e.matmul(
                    self.ps[6 + half][:, :], ones, s[:, half * 512:(half + 1) * 512], start=(kc == 0), stop=(kc == 15)),
                    reads=[Tsq[kc % 2], self.Tc], writes=[self.Tps[6 + half]])
        for half in range(2):
            r = self.rstd[:, half * 512:(half + 1) * 512]
            S.op("act", lambda e, r=r, half=half: e.activation(out=r, in_=self.ps[6 + half][:, :], func=AF.Sqrt,
                                                              bias=self.cst[:, 0:1], scale=1.0 / 2048.0),
                 reads=[self.Tps[6 + half], self.Tc], writes=[self.Trstd])
        S.op("dve", lambda e: e.reciprocal(out=self.rstd[:, :], in_=self.rstd[:, :]), reads=[self.Trstd], writes=[self.Trstd])
        self.apply_norm(gcol_fn, hT, ThT, out_f32, Tout)

    def apply_norm(self, gcol_fn, hT, ThT, out_f32=None, Tout=None, tmp_off=None):
        S = self.S
        tmp = [self.av(tmp_off + i * 4096, 4096, F32) for i in range(2)] if tmp_off is not None else None
        Ttmp = [T(), T()]
        for kc in range(16):
            if out_f32 is None and tmp is not None and kc % 2 == 1:
                q = (kc // 2) % 2
                S.op("pool", lambda e, kc=kc, q=q: e.tensor_tensor(out=tmp[q], in0=self.xT[:, kc, :], in1=self.rstd[:, :], op=ALU.mult),
                     reads=[self.TxT[kc], self.Trstd], writes=[Ttmp[q]], arena=True)
                S.op("act", lambda e, kc=kc, q=q: e.activation(out=hT[:, kc, :], in_=tmp[q], func=AF.Copy, scale=gcol_fn(kc)),
                     reads=[Ttmp[q], self.Tc], writes=[ThT[kc]])
                continue
            if out_f32 is None:
                S.op("dve", lambda e, kc=kc: e.scalar_tensor_tensor(out=hT[:, kc, :], in0=self.xT[:, kc, :], scalar=gcol_fn(kc),
                                                                   in1=self.rstd[:, :], op0=ALU.mult, op1=ALU.mult),
                     reads=[self.TxT[kc], self.Trstd, self.Tc], writes=[ThT[kc]])
            else:
                S.op("dve", lambda e, kc=kc: e.scalar_tensor_tensor(out=out_f32[:, kc, :], in0=self.xT[:, kc, :], scalar=gcol_fn(kc),
                                                                   in1=self.rstd[:, :], op0=ALU.mult, op1=ALU.mult),
                     reads=[self.TxT[kc], self.Trstd, self.Tc], writes=[Tout[kc]])

    def tap(self, name, src_ap, Ts, shape_pat=None):
        if name not in self.tap_out:
            return
        S = self.S
        dst = self.tap_out[name]
        if src_ap.dtype != F32:
            S.barrier()
            tt = T()
            tmp = self.av(self.ARENA - 4096, 4096, F32)
            for a in range(src_ap.shape[1]):
                S.op("dve", lambda e, a=a: e.tensor_copy(out=tmp, in_=src_ap[:, a, :]), reads=Ts, writes=[tt])
                S.dma("sp", lambda e, a=a: e.dma_start(out=dst[:, a, :], in_=tmp), reads=[tt])
            S.barrier()
        else:
            S.dma("sp", lambda e: e.dma_start(out=dst, in_=src_ap), reads=Ts)
            S.barrier()

    def setup(self):
        S, nc, L = self.S, self.nc, self.L
        d = self.din
        for kc in range(16):
            S.dma("sp", lambda e, kc=kc: e.dma_start(out=self.xT[:, kc, :], in_=d["xT"][:, kc, :]), writes=[self.TxT[kc]])
        Tc = self.Tc
        S.dma("pool", lambda e: e.dma_start(out=self.cmb[:, :], in_=d["cm"][:, 0:768]), writes=[Tc])
        S.dma("pool", lambda e: e.dma_start(out=self.pswb[:, :], in_=d["cm"][:, 896:1024]), writes=[Tc])
        S.dma("sp", lambda e: e.dma_start(out=self.ident[:, :], in_=d["cm"][:, 768:896]), writes=[Tc])
        S.dma("sp", lambda e: e.dma_start(out=self.onesf[:, :], in_=d["cm"][:, 640:768]), writes=[Tc])
        for nm, dst in (("flags", self.flags), ("flg2", self.flg2), ("gcols", self.gcols), ("gfin", self.gfin),
                        ("scw", self.scw), ("fcw", self.fcw), ("ropec", self.ropec)):
            S.dma("sp", lambda e, nm=nm, dst=dst: e.dma_start(out=dst[:, :], in_=d[nm]), writes=[Tc])
        S.dma("sp", lambda e: e.dma_start(out=self.negb[:, :], in_=d["foxb"]), writes=[Tc])
        S.op("dve", lambda e: e.tensor_scalar(out=self.negb[:, :], in0=self.negb[:, :], scalar1=-1.0, scalar2=None, op0=ALU.mult),
             reads=[Tc], writes=[Tc])
        S.op("dve", lambda e: e.memset(self.cst[:, 0:1], EPS), writes=[Tc])
        S.op("dve", lambda e: e.memset(self.cst[:, 1:2], 1.0), writes=[Tc])
        S.op("dve", lambda e: e.memset(self.cst[:, 2:3], 0.0), writes=[Tc])
        for r in range(3):
            S.op("dve", lambda e, r=r: e.tensor_scalar(out=self.mh[:, r * 128:(r + 1) * 128], in0=self.cmb[:, 128:256],
                                                      scalar1=self.flags[:, 3 + r:4 + r], scalar2=None, op0=ALU.mult),
                 reads=[Tc], writes=[Tc])
            S.op("dve", lambda e, r=r: e.tensor_scalar(out=self.mh[:, (3 + r) * 128:(4 + r) * 128], in0=self.cmb[:, 384:512],
                                                      scalar1=self.flags[:, 3 + r:4 + r], scalar2=None, op0=ALU.mult),
                 reads=[Tc], writes=[Tc])
            S.op("dve", lambda e, r=r: e.scalar_tensor_tensor(out=self.mh[:, (3 + r) * 128:(4 + r) * 128], in0=self.cmb[:, 512:640],
                                                             scalar=self.flags[:, 6 + r:7 + r], in1=self.mh[:, (3 + r) * 128:(4 + r) * 128],
                                                             op0=ALU.mult, op1=ALU.add),
                 reads=[Tc], writes=[Tc])
        SC_ = 61440 + 20480
        d = self.din
        Ct = self.av(SC_, 4096, F32)
        Sg = self.av(SC_ + 4096, 4096, F32)
        TC = T()
        posi = self.av(SC_ + 8192, 4096, I32)
        yv = self.av(SC_ + 12288, 4096, F32)
        kf = self.av(SC_ + 16384, 4096, F32)
        g1 = self.av(SC_ + 20480, 4096, F32)
        ki = self.av(SC_ + 24576, 4096, I32)
        Tr = T()
        S.dma("sp", lambda e: e.dma_start(out=posi, in_=d["pos"][0:1, :].partition_broadcast(128).rearrange("p o c -> p (o c)")), writes=[Tr])
        S.op("dve", lambda e: e.tensor_copy(out=yv, in_=posi), reads=[Tr], writes=[Tr])
        S.op("dve", lambda e: e.tensor_scalar(out=yv, in0=yv, scalar1=self.ropec[:, 0:1], scalar2=float(1.0 / (2.0 * np.pi)),
                                              op0=ALU.mult, op1=ALU.mult), reads=[Tr, self.Tc], writes=[Tr])
        for which, dst in ((0, Sg), (1, Ct)):
            if which == 1:
                S.op("dve", lambda e: e.tensor_scalar(out=yv, in0=yv, scalar1=0.25, scalar2=None, op0=ALU.add), reads=[Tr], writes=[Tr])
            S.op("dve", lambda e: e.tensor_copy(out=ki, in_=yv), reads=[Tr], writes=[Tr])
            S.op("dve", lambda e: e.tensor_copy(out=kf, in_=ki), reads=[Tr], writes=[Tr])
            S.op("dve", lambda e: e.tensor_tensor(out=kf, in0=yv, in1=kf, op=ALU.subtract), reads=[Tr], writes=[Tr])
            S.op("dve", lambda e: e.tensor_scalar(out=g1, in0=kf, scalar1=0.5, scalar2=None, op0=ALU.is_gt), reads=[Tr], writes=[Tr])
            S.op("dve", lambda e: e.tensor_tensor(out=kf, in0=kf, in1=g1, op=ALU.subtract), reads=[Tr], writes=[Tr])
            S.op("dve", lambda e: e.tensor_scalar(out=g1, in0=kf, scalar1=-0.5, scalar2=None, op0=ALU.is_lt), reads=[Tr], writes=[Tr])
            S.op("dve", lambda e: e.tensor_tensor(out=kf, in0=kf, in1=g1, op=ALU.add), reads=[Tr], writes=[Tr])
            S.op("act", lambda e, dst=dst: e.activation(out=dst, in_=kf, func=AF.Sin, scale=6.283185), reads=[Tr], writes=[TC])
        S.op("dve", lambda e: e.tensor_scalar(out=Sg, in0=Sg, scalar1=self.ropec[:, 1:2], scalar2=None, op0=ALU.mult),
             reads=[TC, self.Tc], writes=[TC])
        S.dma("sp", lambda e: e.dma_start(out=self.rope_dram.ap(), in_=self.av(SC_, 8192, F32)), reads=[TC], writes=[self.Trope])
        S.barrier()
        S.barrier()

    def layer(self, l):
        S, nc, L = self.S, self.nc, self.L
        d = self.din
        ps, Tps = self.ps, self.Tps
        cmb = self.cmb
        OB, HT, R0 = 0, 28672, 61440
        QF, QD, SC = R0, R0 + 8192, R0 + 20480
        o_a = self.av(OB, 8192, BF16, "p (a b) -> p a b", a=4)
        o_b = self.av(OB + 8192, 8192, BF16, "p (a b) -> p a b", a=4)
        o_c = self.av(OB + 16384, 8192, BF16, "p (a b) -> p a b", a=4)
        o_d = self.av(OB + 24576, 4096, BF16, "p (a b) -> p a b", a=2)
        To = {k: T() for k in ("a", "b", "c", "d")}
        hT = self.av(HT, 32768, BF16, "p (a b) -> p a b", a=16)
        ThT = self.ThT
        gc = lambda gi: (lambda kc: self.gcols[:, l * 48 + gi * 16 + kc: l * 48 + gi * 16 + kc + 1])
        hrhs = lambda kc, half: hT[:, kc, half * 512:(half + 1) * 512]
        xfs = self.xf_src.ap()
        xfd = self.xf_dst.ap()

        def evac_act(dst, b, scale=1.0, reads=(), writes=(), func=AF.Copy, m=128):
            return S.op("act", lambda e: e.activation(out=dst, in_=ps[b][0:m, :], func=func, scale=scale),
                        reads=[Tps[b]] + list(reads), writes=list(writes))

        if l == 0:
            S.barrier()
        self.rmsnorm(gc(0), hT, ThT, OB)
        if l == 0:
            self.tap("t_h", hT, ThT)

        self.late_gathers = []

        def gather(i, n=3):
            self.defer(lambda i=i: S.dma("pool", lambda e: e.collective_compute("AllGather", ALU.bypass, replica_groups=RG, dma_qos="P3",
                                                                                 ins=[self.xbs[i].ap().opt()], outs=[self.xbd[i].ap().opt()]),
                                         reads=[self.Txbs[i]], writes=[self.Txbd[i]], dsem=self.cc_ds, inc=1, track=False), n)
        bigA = self.av(R0, 8192, BF16)
        bigB = self.av(R0 + 8192, 8192, BF16)
        self.load_big(d["w_v"][l, 0], bigB, self.Tbs[1], 4096, arena=False)
        self.load_big(d["w_v"][l, 1], bigA, self.Tbs[0], 4096, arena=False)
        Ct = self.av(SC, 4096, F32)
        Sg = self.av(SC + 4096, 4096, F32)
        TC = T()
        S.dma("sp", lambda e: e.dma_start(out=self.av(SC, 8192, F32), in_=self.rope_dram.ap()), reads=[self.Trope], writes=[TC])
        S.barrier()
        kst = [self.av(SC + 8192 + i * 2048, 2048, BF16) for i in range(2)]
        Tkst = [T(), T()]
        for c in range(4):
            w, Tw = self.load_w(d["w_k"][l, c], 2048)
            pr = self.next_pair()
            self.proj_b(lambda kc, w=w: w[:, kc * 128:(kc + 1) * 128], Tw, 16, hrhs, ThT, pr)
            for half in range(2):
                evac_act(kst[c % 2][:, half * 512:(half + 1) * 512], pr[half], writes=[Tkst[c % 2]])
            S.dma("sp", lambda e, c=c: e.dma_start(out=self.xbs[0].ap()[:, c * 1024:(c + 1) * 1024], in_=kst[c % 2]),
                  reads=[Tkst[c % 2]], writes=[self.Txbs[0]])
        gather(0)
        big = [self.av(R0 + 8192 - i * 8192, 8192, BF16) for i in range(2)]
        Tbig = [self.Tbs[1], self.Tbs[0]]
        vst = self.av(SC + 12288, 16384, BF16, "p (hp t c) -> p hp t c", t=8, hp=4)
        Tvst = T()
        S.op("dve", lambda e: e.memset(vst[:, :, :, 64:192], 1.0), writes=[Tvst])
        bi = 0
        for tt in range(8):
            for pc in range(2):
                b = 4 + bi % 2
                bi += 1
                fns = [lambda e, kc=kc, b=b, tt=tt, pc=pc: e.matmul(ps[b][:, 0:256], hT[:, kc, tt * 128:(tt + 1) * 128],
                                                                big[pc][:, kc * 256:(kc + 1) * 256], start=(kc == 0), stop=(kc == 15))
                       for kc in range(16)]
                S.op("pe", fns, reads=ThT + [Tbig[pc]], writes=[Tps[b]])
                pv4 = ps[b][:, 0:256].rearrange("p (hp hh c) -> p hp hh c", hp=2, hh=2)
                for hh, c0 in ((0, 0), (1, 192)):
                    S.op("dve", lambda e, tt=tt, pc=pc, pv4=pv4, hh=hh, c0=c0: e.tensor_copy(
                        out=vst[:, 2 * pc:2 * pc + 2, tt, c0:c0 + 64], in_=pv4[:, :, hh:hh + 1, :].rearrange("p hp o c -> p hp (o c)")),
                        reads=[Tps[b]], writes=[Tvst])
        for th in range(2):
            S.dma("sp", lambda e, th=th: e.dma_start(out=self.xbs[1 + th].ap(), in_=vst[:, 2 * th:2 * th + 2, :, :].rearrange("p hp t c -> p (hp t c)")),
                  reads=[Tvst], writes=[self.Txbs[1 + th]])
            gather(1 + th, 3 + 8 * th)
        self.load_big(d["w_dv"][l, 0], bigA, self.Tbs[0], 4096, arena=False)
        self.load_big(d["w_dv"][l, 1], bigB, self.Tbs[1], 4096, arena=False)
        S.barrier()
        if self.stop == 1:
            return
        spb = self.av(SC + 8192, 4096, F32)
        NFT = self.av(SC + 12288, 4096, F32)
        onf = self.av(SC + 16384, 4096, F32)
        nfhl = self.av(OB + 16384, 4096, BF16, "p (a b) -> p a b", a=2)
        Tsp, Tnf, Tonf, Tnfhl = T(), T(), T(), T()
        w, Tw = self.load_w(d["w_f"][l], 128)
        pr = self.next_pair()
        self.proj_b(lambda kc, w=w: w[:, kc * 8:(kc + 1) * 8], Tw, 16, hrhs, ThT, pr, m=8)
        for half in range(2):
            S.op("act", lambda e, half=half: e.activation(out=spb[0:8, half * 512:(half + 1) * 512], in_=ps[pr[half]][0:8, :], func=AF.Exp,
                                                          bias=self.negb[0:8, l:l + 1], scale=-1.0),
                 reads=[Tps[pr[half]], self.Tc], writes=[Tsp])
        S.op("act", lambda e: e.activation(out=spb[0:8, :], in_=spb[0:8, :], func=AF.Ln, bias=self.cst[0:8, 1:2], scale=1.0),
             reads=[Tsp, self.Tc], writes=[Tsp])
        S.op("dve", lambda e: e.memset(onf[0:8, :], 1.0), writes=[Tonf])
        S.op("dve", lambda e: e.tensor_tensor_scan(out=NFT[0:8, :], data0=onf[0:8, :], data1=spb[0:8, :], initial=0.0,
                                                   op0=ALU.mult, op1=ALU.add), reads=[Tsp, Tonf], writes=[Tnf])
        S.op("dve", lambda e: e.tensor_scalar(out=nfhl[0:8, 0, :], in0=NFT[0:8, :], scalar1=-1.0, scalar2=None, op0=ALU.mult),
             reads=[Tnf], writes=[Tnfhl])
        S.op("dve", lambda e: e.scalar_tensor_tensor(out=nfhl[0:8, 1, :], in0=NFT[0:8, :], scalar=-1.0, in1=nfhl[0:8, 0, :],
                                                     op0=ALU.mult, op1=ALU.subtract), reads=[Tnf, Tnfhl], writes=[Tnfhl])
        for t in range(8):
            S.op("pe", lambda e, t=t: e.transpose(out=ps[4][:, t * 8:(t + 1) * 8], in_=NFT[0:8, t * 128:(t + 1) * 128],
                                                  identity=self.ident[0:8, 0:8]), reads=[Tnf, self.Tc], writes=[Tps[4]])
        Tnfk = T()
        S.op("dve", lambda e: e.tensor_copy(out=self.nfk[:, :], in_=ps[4][:, 0:64]), reads=[Tps[4]], writes=[Tnfk])
        S.dma("sp", lambda e: e.dma_start(out=xfs[:, 0:64], in_=self.nfk[:, :]), reads=[Tnfk], writes=[self.Txfs])
        S.barrier()
        if self.stop == 2:
            return
        tmpA = [self.av(SC + 8192, 4096, F32) for i in range(2)]
        zbuf = [self.av(SC + 20480 + i * 4112, 4112, F32) for i in range(2)]
        gbt = [self.av(SC + 12288 + i * 4096, 4096, F32) for i in range(2)]
        t1s = [self.av(SC + 28704, 4096, F32) for i in range(2)]
        TtA0, Tt10 = T(), T()
        TtA, Tz, Tg, Tt1 = [TtA0, TtA0], [T(), T()], [T(), T()], [Tt10, Tt10]
        Tzp = T()
        for j in range(4):
            q = j % 2
            w, Tw = self.load_w(d["w_cb"][l, 3 * j + 0], 2048)
            pr = self.next_pair()
            self.proj_b(lambda kc, w=w: w[:, kc * 128:(kc + 1) * 128], Tw, 16, hrhs, ThT, pr)
            for half in range(2):
                evac_act(tmpA[q][:, half * 512:(half + 1) * 512], pr[half], writes=[TtA[q]])
            w, Tw = self.load_w(d["w_cb"][l, 3 * j + 1], 2048)
            pr = self.next_pair()
            self.proj_b(lambda kc, w=w: w[:, kc * 128:(kc + 1) * 128], Tw, 16, hrhs, ThT, pr)
            for half in range(2):
                S.op("dve", lambda e, half=half, q=q, pr=pr: e.tensor_tensor(
                    out=zbuf[q][:, 2 + half * 512:2 + (half + 1) * 512], in0=tmpA[q][:, half * 512:(half + 1) * 512],
                    in1=ps[pr[half]][:, :], op=ALU.mult), reads=[TtA[q], Tps[pr[half]]], writes=[Tz[q]])
            w, Tw = self.load_w(d["w_cb"][l, 3 * j + 2], 2048)
            pr = self.next_pair()
            self.proj_b(lambda kc, w=w: w[:, kc * 128:(kc + 1) * 128], Tw, 16, hrhs, ThT, pr)
            for half in range(2):
                evac_act(gbt[q][:, half * 512:(half + 1) * 512], pr[half], writes=[Tg[q]])
            wc = lambda k, j=j: self.scw[:, l * 12 + k * 4 + j: l * 12 + k * 4 + j + 1]
            S.op("dve", lambda e, q=q, wc=wc: e.tensor_scalar(out=t1s[q][:, 2:1024], in0=zbuf[q][:, 4:1026], scalar1=wc(2), scalar2=None,
                                                           op0=ALU.mult), reads=[Tz[q], self.Tc], writes=[Tt1[q]])
            S.op("dve", lambda e, q=q, wc=wc: e.scalar_tensor_tensor(out=t1s[q][:, 2:1024], in0=zbuf[q][:, 3:1025], scalar=wc(1),
                                                                  in1=t1s[q][:, 2:1024], op0=ALU.mult, op1=ALU.add),
                 reads=[Tz[q], self.Tc, Tt1[q]], writes=[Tt1[q]])
            S.op("dve", lambda e, q=q, wc=wc: e.scalar_tensor_tensor(out=t1s[q][:, 2:1024], in0=zbuf[q][:, 2:1024], scalar=wc(0),
                                                                  in1=t1s[q][:, 2:1024], op0=ALU.mult, op1=ALU.add),
                 reads=[Tz[q], self.Tc, Tt1[q]], writes=[Tt1[q]])
            S.op("dve", lambda e, q=q, j=j: e.tensor_tensor(out=o_b[:, j, 2:1024], in0=t1s[q][:, 2:1024], in1=gbt[q][:, 2:1024], op=ALU.mult),
                 reads=[Tt1[q], Tg[q]], writes=[To["b"]])
            S.op("dve", lambda e, q=q, j=j: e.tensor_copy(out=self.zp[:, 4 * j + 2:4 * j + 4], in_=zbuf[q][:, 2:4]), reads=[Tz[q]], writes=[Tzp])
            S.op("dve", lambda e, q=q, j=j: e.tensor_copy(out=self.gb2[:, 2 * j:2 * j + 2], in_=gbt[q][:, 0:2]), reads=[Tg[q]], writes=[Tzp])
            S.op("dve", lambda e, q=q, j=j: e.tensor_copy(out=self.zl[:, 2 * j:2 * j + 2], in_=zbuf[q][:, 1024:1026]), reads=[Tz[q]], writes=[Tzp])
        S.dma("sp", lambda e: e.dma_start(out=xfs[:, 64:72], in_=self.zl[:, :]), reads=[Tzp], writes=[self.Txfs])
        self.defer(lambda: S.dma("pool", lambda e: e.collective_compute("AllGather", ALU.bypass, replica_groups=RG, dma_qos="P3",
                                                                        ins=[self.xf_src.ap().opt()], outs=[self.xf_dst.ap().opt()]),
                                 reads=[self.Txfs], writes=[self.Txfd], dsem=self.cc_ds, inc=1, track=False))
        S.barrier()
        qraw = [self.av(SC + 8192 + i * 2048, 2048, BF16) for i in range(2)]
        tt2 = [self.av(SC + 12288 + i * 4096, 4096, F32) for i in range(2)]
        u12 = [self.av(SC + 20480 + i * 4096, 4096, F32) for i in range(2)]
        kst2 = [self.av(SC + 28672 + i * 2048, 2048, BF16) for i in range(2)]
        Tq, Ttt2, Tu12, Tk2 = [T(), T()], [T(), T()], [T(), T()], [T(), T()]

        def rope_proj(wname, c, scale, dst, Tdst, cnt):
            dil = (1, 4, 8)[c // 2]
            q = cnt % 2
            tt_, u1, Ttt, Tu1 = tt2[q], u12[q], Ttt2[q], Tu12[q]
            w, Tw = self.load_w(d[wname][l, c], 2048)
            pr = self.next_pair()
            self.proj_b(lambda kc, w=w: w[:, kc * 128:(kc + 1) * 128], Tw, 16, hrhs, ThT, pr)
            for half in range(2):
                evac_act(qraw[q][:, half * 512:(half + 1) * 512], pr[half], scale=scale, writes=[Tq[q]])
            pr2 = self.next_pair()
            for half in range(2):
                S.op("pe", lambda e, half=half, pr2=pr2, q=q: e.matmul(ps[pr2[half]][:, :], self.pswb[:, :], qraw[q][:, half * 512:(half + 1) * 512],
                                                                  start=True, stop=True), reads=[Tq[q], self.Tc], writes=[Tps[pr2[half]]])
            S.op("dve", lambda e, q=q: e.tensor_tensor(out=tt_, in0=qraw[q], in1=Ct, op=ALU.mult), reads=[Tq[q], TC], writes=[Ttt])
            for half in range(2):
                S.op("dve", lambda e, half=half, pr2=pr2: e.tensor_tensor(out=u1[:, half * 512:(half + 1) * 512], in0=ps[pr2[half]][:, :],
                                                                         in1=Sg[:, half * 512:(half + 1) * 512], op=ALU.mult),
                     reads=[Tps[pr2[half]], TC], writes=[Tu1])
            if dil == 1:
                S.op("dve", lambda e: e.tensor_tensor(out=dst, in0=u1, in1=tt_, op=ALU.add), reads=[Tu1, Ttt], writes=[Tdst])
            else:
                S.op("dve", lambda e, dil=dil: e.tensor_tensor(out=dst.rearrange("p (r i) -> p i r", r=dil),
                                                              in0=u1.rearrange("p (i r) -> p i r", r=dil),
                                                              in1=tt_.rearrange("p (i r) -> p i r", r=dil), op=ALU.add),
                     reads=[Tu1, Ttt], writes=[Tdst])

        for c in range(6):
            rope_proj("w_dk", c, 1.0, kst2[c % 2], Tk2[c % 2], c)
            S.dma("sp", lambda e, c=c: e.dma_start(out=self.xbs[3 + c // 3].ap()[:, (c % 3) * 1024:(c % 3 + 1) * 1024], in_=kst2[c % 2]),
                  reads=[Tk2[c % 2]], writes=[self.Txbs[3 + c // 3]])
            if c % 3 == 2:
                self.late_gathers.append(3 + c // 3)
        S.barrier()
        bigd = [self.av(R0 + i * 8192, 8192, BF16) for i in range(2)]
        Tbd = self.Tbs
        vdst = [self.av(SC + 8192 + i * 8192, 8192, BF16, "p (t hp c) -> p t hp c", t=8, hp=2) for i in range(2)]
        Tvd = [T(), T()]
        for i in range(2):
            S.op("dve", lambda e, i=i: e.memset(vdst[i][:, :, :, 64:192], 1.0), writes=[Tvd[i]])
        DIL = (1, 4, 16)
        bi = 0
        for g in range(3):
            dil = DIL[g]
            for tp in range(8):
                b = 4 + bi % 2
                bi += 1

                def lhs(kc, tp=tp, dil=dil):
                    if dil == 1:
                        return hT[:, kc, tp * 128:(tp + 1) * 128]
                    if dil == 4:
                        return hT[:, kc, :].rearrange("p (i r) -> p r i", r=4)[:, tp // 2, (tp % 2) * 128:(tp % 2) * 128 + 128]
                    return hT[:, kc, :].rearrange("p (i r) -> p r i", r=8)[:, tp, :]
                fns = [lambda e, kc=kc, b=b, g=g, lhs=lhs: e.matmul(ps[b][:, 0:256], lhs(kc), bigd[g % 2][:, kc * 256:(kc + 1) * 256],
                                                                    start=(kc == 0), stop=(kc == 15)) for kc in range(16)]
                S.op("pe", fns, reads=ThT + [Tbd[g % 2]], writes=[Tps[b]])
                pv4 = ps[b][:, 0:256].rearrange("p (hp hh c) -> p hp hh c", hp=2, hh=2)
                for hh, c0 in ((0, 0), (1, 192)):
                    S.op("dve", lambda e, tp=tp, g=g, pv4=pv4, hh=hh, c0=c0: e.tensor_copy(
                        out=vdst[g % 2][:, tp, :, c0:c0 + 64], in_=pv4[:, :, hh:hh + 1, :].rearrange("p hp o c -> p hp (o c)")),
                        reads=[Tps[b]], writes=[Tvd[g % 2]])
            S.dma("sp", lambda e, g=g: e.dma_start(out=self.xbs[5 + g].ap(), in_=vdst[g % 2].rearrange("p t hp c -> p (t hp c)")),
                  reads=[Tvd[g % 2]], writes=[self.Txbs[5 + g]])
            self.late_gathers.append(5 + g)
            if g == 0:
                self.load_big(d["w_dv"][l, 2], bigA, self.Tbs[0], 4096, arena=False)
        S.barrier()
        if self.stop == 3:
            return
        if self.stop == 4:
            return
        QdT = self.av(QD, 12288, BF16, "p (a b) -> p a b", a=6)
        QfT = self.av(QF, 8192, BF16, "p (a b) -> p a b", a=4)
        TQd = [T() for _ in range(6)]
        TQf = [T() for _ in range(4)]
        for c in range(6):
            rope_proj("w_dq", c, 0.125, QdT[:, c, :], TQd[c], c)
        for c in range(4):
            w, Tw = self.load_w(d["w_q"][l, c], 2048)
            pr = self.next_pair()
            self.proj_b(lambda kc, w=w: w[:, kc * 128:(kc + 1) * 128], Tw, 16, hrhs, ThT, pr)
            for half in range(2):
                evac_act(QfT[:, c, half * 512:(half + 1) * 512], pr[half], scale=0.125, writes=[TQf[c]])
        self.tick_deferred(flush=True)
        S.barrier()
        if self.stop == 5:
            return
        self.fox_attention(l, QfT, TQf, o_a, To["a"], nfhl, Tnfhl, Tnfk, HT, SC, OB)
        if l == 0:
            self.tap("t_oa", o_a, [To["a"]])
        if self.stop == 6:
            return
        S.barrier()
        self.dil_attention(l, QdT, TQd, o_d, To["d"], HT, SC, QF)
        if l == 0:
            self.tap("t_od", o_d, [To["d"]])
        S.barrier()
        if self.stop == 7:
            return
        self.apply_norm(gc(0), hT, ThT)
        Tdt = T()
        S.dma("sp", lambda e: e.dma_start(out=self.dtmp[:, 0:24].rearrange("p (r c) -> p r c", r=3),
                                          in_=xfd[0:384, 64:72].rearrange("(r p) c -> p r c", p=128)), reads=[self.Txfd], writes=[Tdt])
        zh = self.dtmp[:, 24:32]
        S.op("dve", lambda e: e.tensor_scalar(out=zh, in0=self.dtmp[:, 0:8], scalar1=self.flags[:, 3:4], scalar2=None, op0=ALU.mult),
             reads=[Tdt, self.Tc], writes=[Tdt])
        for r in (1, 2):
            S.op("dve", lambda e, r=r: e.scalar_tensor_tensor(out=zh, in0=self.dtmp[:, 8 * r:8 * r + 8], scalar=self.flags[:, 3 + r:4 + r],
                                                             in1=zh, op0=ALU.mult, op1=ALU.add), reads=[Tdt, self.Tc], writes=[Tdt])
        S.op("dve", lambda e: e.tensor_copy(out=self.zp[:, :].rearrange("p (j k) -> p j k", k=4)[:, :, 0:2],
                                            in_=zh.rearrange("p (j k) -> p j k", k=2)), reads=[Tdt, Tzp], writes=[Tzp])
        o2 = self.dtmp[:, 32:40]
        for j in range(4):
            wc = lambda k, j=j: self.scw[:, l * 12 + k * 4 + j: l * 12 + k * 4 + j + 1]
            oj = o2[:, 2 * j:2 * j + 2]
            S.op("dve", lambda e, j=j, wc=wc, oj=oj: e.tensor_scalar(out=oj, in0=self.zp[:, 4 * j + 2:4 * j + 4], scalar1=wc(2), scalar2=None,
                                                                  op0=ALU.mult), reads=[Tzp, self.Tc], writes=[Tdt])
            S.op("dve", lambda e, j=j, wc=wc, oj=oj: e.scalar_tensor_tensor(out=oj, in0=self.zp[:, 4 * j + 1:4 * j + 3], scalar=wc(1), in1=oj,
                                                                         op0=ALU.mult, op1=ALU.add), reads=[Tzp, self.Tc, Tdt], writes=[Tdt])
            S.op("dve", lambda e, j=j, wc=wc, oj=oj: e.scalar_tensor_tensor(out=oj, in0=self.zp[:, 4 * j + 0:4 * j + 2], scalar=wc(0), in1=oj,
                                                                         op0=ALU.mult, op1=ALU.add), reads=[Tzp, self.Tc, Tdt], writes=[Tdt])
            S.op("dve", lambda e, j=j, oj=oj: e.tensor_tensor(out=o_b[:, j, 0:2], in0=oj, in1=self.gb2[:, 2 * j:2 * j + 2], op=ALU.mult),
                 reads=[Tdt, Tzp], writes=[To["b"]])
        if l == 0:
            self.tap("t_ob", o_b, [To["b"]])
        self.sgu(l, hT, ThT, o_c, To["c"], SC)
        if l == 0:
            self.tap("t_oc", o_c, [To["c"]])
        S.barrier()
        if self.stop == 8:
            return
        merged = self.av(R0, 32768, BF16, "p (a b) -> p a b", a=16)
        Tm = [T() for _ in range(16)]
        sg = [self.av(R0 + 32768 + i * 2048, 2048, F32) for i in range(2)]
        prod = [self.av(R0 + 36864 + i * 2048, 2048, F32) for i in range(2)]
        acc = self.av(R0 + 40960, 4096, F32)
        Tsg, Tpr, Tacc = [T(), T()], [T(), T()], T()
        branches = (("w_bf", 4, o_a, To["a"]), ("w_bc", 4, o_b, To["b"]), ("w_bs", 4, o_c, To["c"]), ("w_bd", 2, o_d, To["d"]))
        n = 0
        for j in range(16):
            for i, (wn, kcn, ot, Tot) in enumerate(branches):
                w, Tw = self.load_w(d["w_g"][l, i * 16 + j], 2048)
                pg = self.next_pair(4)
                self.proj_b(lambda kc, w=w: w[:, kc * 128:(kc + 1) * 128], Tw, 16, hrhs, ThT, pg)
                w2, Tw2 = self.load_w(d[wn][l, j], kcn * 128)
                pb = self.next_pair(4)
                self.proj_b(lambda kc, w2=w2: w2[:, kc * 128:(kc + 1) * 128], Tw2, kcn,
                            lambda kc, half, ot=ot: ot[:, kc, half * 512:(half + 1) * 512], [Tot], pb)
                for half in range(2):
                    q = n % 2
                    n += 1
                    hs = slice(half * 512, (half + 1) * 512)
                    S.op("act", lambda e, q=q, b=pg[half]: e.activation(out=sg[q], in_=ps[b][:, :], func=AF.Sigmoid),
                         reads=[Tps[pg[half]]], writes=[Tsg[q]])
                    if i == 0:
                        S.op("dve", lambda e, q=q, b=pb[half], hs=hs: e.tensor_tensor(out=acc[:, hs], in0=sg[q], in1=ps[b][:, :], op=ALU.mult),
                             reads=[Tsg[q], Tps[pb[half]]], writes=[Tacc])
                    else:
                        S.op("dve", lambda e, q=q, b=pb[half]: e.tensor_tensor(out=prod[q], in0=sg[q], in1=ps[b][:, :], op=ALU.mult),
                             reads=[Tsg[q], Tps[pb[half]]], writes=[Tpr[q]])
                        if i < 3:
                            S.op("dve", lambda e, q=q, hs=hs: e.tensor_tensor(out=acc[:, hs], in0=acc[:, hs], in1=prod[q], op=ALU.add),
                                 reads=[Tpr[q], Tacc], writes=[Tacc])
                        else:
                            S.op("dve", lambda e, q=q, hs=hs, j=j: e.tensor_tensor(out=merged[:, j, hs], in0=acc[:, hs], in1=prod[q], op=ALU.add),
                                 reads=[Tpr[q], Tacc], writes=[Tm[j]])
        for j in range(16):
            w, Tw = self.load_w(d["w_o"][l, j], 2048)
            pr = self.next_pair(4)
            self.proj_b(lambda kc, w=w: w[:, kc * 128:(kc + 1) * 128], Tw, 16,
                        lambda kc, half: merged[:, kc, half * 512:(half + 1) * 512], Tm, pr)
            for half in range(2):
                hs = slice(half * 512, (half + 1) * 512)
                S.op("dve", lambda e, j=j, hs=hs, b=pr[half]: e.tensor_tensor(out=self.xT[:, j, hs], in0=self.xT[:, j, hs], in1=ps[b][:, :], op=ALU.add),
                     reads=[Tps[pr[half]]], writes=[self.TxT[j]])
        if l == 0:
            self.tap("t_x1", self.xT[:, :, :], self.TxT)
        if self.stop == 9:
            return
        self.rmsnorm(gc(1), hT, ThT, R0 + 45056)
        self.ffn(l, hT, ThT, OB, R0)
        S.barrier()
        if l == 0:
            self.tap("t_x2", self.xT[:, :, :], self.TxT)
        if self.stop == 10:
            return
        self.rmsnorm(gc(2), hT, ThT, OB)
        pT = self.av(OB + 20544, 4096, BF16)
        TpT = T()
        self.load_big(d["pT"][l], pT, TpT, 2048)
        sg = [self.av(R0 + 28672 + i * 2048, 2048, F32) for i in range(2)]
        prod = [self.av(R0 + 32768 + i * 2048, 2048, F32) for i in range(2)]
        Tsg, Tpr = [T(), T()], [T(), T()]
        n = 0
        for j in range(16):
            w, Tw = self.load_w(d["w_pg"][l, j], 2048)
            pg = self.next_pair(4)
            self.proj_b(lambda kc, w=w: w[:, kc * 128:(kc + 1) * 128], Tw, 16, hrhs, ThT, pg)
            w2, Tw2 = self.load_w(d["w_pp"][l, j], 256)
            pb = self.next_pair(4)
            self.proj_b(lambda kc, w2=w2: w2[:, kc * 128:(kc + 1) * 128], Tw2, 2,
                        lambda kc, half: pT[:, kc * 1024 + half * 512: kc * 1024 + (half + 1) * 512], [TpT], pb)
            for half in range(2):
                q = n % 2
                n += 1
                hs = slice(half * 512, (half + 1) * 512)
                S.op("act", lambda e, q=q, b=pg[half]: e.activation(out=sg[q], in_=ps[b][:, :], func=AF.Sigmoid),
                     reads=[Tps[pg[half]]], writes=[Tsg[q]])
                S.op("dve", lambda e, q=q, b=pb[half]: e.tensor_tensor(out=prod[q], in0=sg[q], in1=ps[b][:, :], op=ALU.mult),
                     reads=[Tsg[q], Tps[pb[half]]], writes=[Tpr[q]])
                S.op("dve", lambda e, q=q, j=j, hs=hs: e.tensor_tensor(out=self.xT[:, j, hs], in0=self.xT[:, j, hs], in1=prod[q], op=ALU.add),
                     reads=[Tpr[q]], writes=[self.TxT[j]])

    def fox_attention(self, l, QfT, TQf, o_a, Toa, nfhl, Tnfhl, Tnfk, HT, SC, OB):
        S, ps, Tps, cmb = self.S, self.ps, self.Tps, self.cmb
        xfd = self.xf_dst.ap()
        Tb, Tdt = T(), T()
        S.dma("sp", lambda e: e.dma_start(out=self.biask[:, 0:192].rearrange("p (r c) -> p r c", r=3),
                                          in_=xfd[0:384, 0:64].rearrange("(r p) c -> p r c", p=128)), reads=[self.Txfd], writes=[Tb])
        for r in range(3):
            S.dma("sp", lambda e, r=r: e.dma_start(out=self.dtmp[:, 8 * r:8 * r + 8],
                                                   in_=xfd[r * 128 + 127:r * 128 + 128, 56:64].partition_broadcast(128).rearrange("p o c -> p (o c)")),
                  reads=[self.Txfd], writes=[Tdt])
        dt = self.dtmp
        S.op("dve", lambda e: e.tensor_tensor(out=dt[:, 0:24], in0=dt[:, 0:24], in1=self.flg2[:, 0:24], op=ALU.mult), reads=[Tdt, self.Tc], writes=[Tdt])
        S.op("dve", lambda e: e.tensor_tensor(out=dt[:, 8:16], in0=dt[:, 8:16], in1=dt[:, 16:24], op=ALU.add), reads=[Tdt], writes=[Tdt])
        S.op("dve", lambda e: e.tensor_tensor(out=dt[:, 0:8], in0=dt[:, 0:8], in1=dt[:, 8:16], op=ALU.add), reads=[Tdt], writes=[Tdt])
        S.op("dve", lambda e: e.tensor_tensor(out=dt[:, 24:48], in0=self.flg2[:, 24:48], in1=dt[:, 0:24], op=ALU.subtract), reads=[Tdt, self.Tc], writes=[Tdt])
        for r in range(3):
            for j in range(8):
                o = r * 64 + j * 8
                S.op("dve", lambda e, o=o, r=r: e.tensor_tensor(out=self.biask[:, o:o + 8], in0=self.biask[:, o:o + 8], in1=dt[:, 24 + 8 * r:32 + 8 * r],
                                                              op=ALU.add), reads=[Tb, Tdt], writes=[Tb])
        S.op("dve", lambda e: e.tensor_copy(out=self.biask[:, 192:256], in_=self.nfk[:, :]), reads=[Tnfk, Tb], writes=[Tb])
        for i in self.late_gathers:
            S.dma("pool", lambda e, i=i: e.collective_compute("AllGather", ALU.bypass, replica_groups=RG, dma_qos="P3",
                                                              ins=[self.xbs[i].ap().opt()], outs=[self.xbd[i].ap().opt()]),
                  reads=[self.Txbs[i]], writes=[self.Txbd[i]], dsem=self.cc_ds, inc=1, track=False)
        self.late_gathers = []
        vt = [[self.av(HT + (st * 4 + s) * 4096, 4096, BF16, "p (t c) -> p t c", t=8) for s in range(4)] for st in range(2)]
        ktp = [[self.av(SC + (hh * 4 + s) * 2048, 2048, BF16) for s in range(4)] for hh in range(2)]
        kall = self.av(SC, 16384, BF16)
        Tvt = [[T() for _ in range(4)] for _ in range(2)]
        Tkt = [[T() for _ in range(4)] for _ in range(2)]
        Pt = [self.av(SC + 16384 + i * 1024, 1024, BF16) for i in range(4)]
        TPt = [T() for _ in range(4)]
        rden = self.av(SC + 20480, 2048, F32)
        Trd = T()
        Qtmp = [self.av(OB + 16384 + 4096 + i * 2048, 2048, BF16) for i in range(2)]
        qall = self.av(OB + 16384 + 4096, 4096, BF16)
        TQt = [T(), T()]
        allk = [t for row in Tkt for t in row]
        S.op("dve", lambda e: e.memset(kall[64:128, :], 0.0), writes=allk)
        S.op("dve", lambda e: e.memset(kall[64:66, :], 1.0), writes=allk)
        S.op("dve", lambda e: e.memset(qall[64:128, :], 0.0), writes=TQt)
        LAG = 2

        def srcT(i, s):
            if s < 3:
                return self.xbd[i].ap()[s * 128:(s + 1) * 128, :], self.Txbd[i]
            return self.xbs[i].ap(), self.Txbs[i]

        def load_k(head):
            c, hh = head // 2, head % 2
            for s in range(4):
                ks, Tks = srcT(0, s)
                S.dma("sp", lambda e, ks=ks, c=c, hh=hh, s=s: e.dma_start(out=ktp[hh][s][0:64, :], in_=ks[64 * hh:64 * hh + 64, c * 1024:(c + 1) * 1024]),
                      reads=[Tks], writes=[Tkt[hh][s]])

        def load_v(c):
            st = c % 2
            for s in range(4):
                vs, Tvs = srcT(1 + c // 2, s)
                S.dma("sp", lambda e, vs=vs, c=c, st=st, s=s: e.dma_start(
                    out=vt[st][s], in_=vs[:, (c % 2) * 2048:(c % 2 + 1) * 2048].rearrange("p (t c) -> p t c", t=8)),
                    reads=[Tvs], writes=[Tvt[st][s]])

        load_k(0)
        load_v(0)
        for head in range(8):
            c, hh = head // 2, head % 2
            st = c % 2
            pb = 64 * hh
            ob = 64 - pb
            if head + 1 < 8:
                load_k(head + 1)
            if hh == 0 and c + 1 < 4:
                load_v(c + 1)
            qt = Qtmp[head % 2]
            S.op("act", lambda e, qt=qt, pb=pb, c=c: e.activation(out=qt[0:64, :], in_=QfT[pb:pb + 64, c, :], func=AF.Copy),
                 reads=[TQf[c]], writes=[TQt[head % 2]])
            S.dma("sp", lambda e, qt=qt, head=head: e.dma_start(out=qt[64:65, :], in_=nfhl[head:head + 1, 0, :]),
                  reads=[Tnfhl], writes=[TQt[head % 2]])
            S.dma("sp", lambda e, qt=qt, head=head: e.dma_start(out=qt[65:66, :], in_=nfhl[head:head + 1, 1, :]),
                  reads=[Tnfhl], writes=[TQt[head % 2]])
            for qh in range(2):
                steps = [(s, j, 0) for s in range(3) for j in range(8)] + [(3, j, max(0, j - 4 * qh) * 128) for j in range(4 * qh + 4)]
                bo = 4 + (self.bank_i % 4)
                self.bank_i += 1
                pend = []

                def pv(i, s, j, c0, bs, bo=bo, st=st, hh=hh, nsteps=len(steps)):
                    S.op("pe", lambda e: e.matmul(ps[bo][:, c0:512], vt[st][s][:, j, hh * 128:(hh + 1) * 128], Pt[bs][:, c0:512],
                                                  start=(i == 0), stop=(i == nsteps - 1)),
                         reads=[Tvt[st][s], TPt[bs]], writes=[Tps[bo]])
                for i, (s, j, c0) in enumerate(steps):
                    bs = i % 4
                    q0 = qh * 512 + c0
                    q1 = (qh + 1) * 512
                    S.op("pe", lambda e, bs=bs, c0=c0, s=s, j=j, q0=q0, q1=q1: e.matmul(
                        ps[bs][:, c0:512], ktp[hh][s][:, j * 128:(j + 1) * 128], qt[:, q0:q1], start=True, stop=True),
                        reads=[Tkt[hh][s], TQt[head % 2]], writes=[Tps[bs]])
                    bcol = s * 64 + j * 8 + head
                    S.op("act", lambda e, bs=bs, c0=c0, bcol=bcol: e.activation(out=Pt[bs][:, c0:512], in_=ps[bs][:, c0:512], func=AF.Exp,
                                                                               bias=self.biask[:, bcol:bcol + 1], scale=1.0),
                         reads=[Tps[bs], Tb], writes=[TPt[bs]])
                    if s == 3 and j >= 4 * qh:
                        S.op("dve", lambda e, bs=bs, c0=c0: e.tensor_tensor(out=Pt[bs][:, c0:c0 + 128], in0=Pt[bs][:, c0:c0 + 128],
                                                                           in1=cmb[:, 0:128], op=ALU.mult), reads=[TPt[bs], self.Tc], writes=[TPt[bs]])
                    pend.append((i, s, j, c0, bs))
                    if len(pend) > LAG:
                        pv(*pend.pop(0))
                while pend:
                    pv(*pend.pop(0))
                S.op("dve", lambda e, bo=bo: e.reciprocal(out=rden[pb:pb + 64, 0:512], in_=ps[bo][ob:ob + 64, :]), reads=[Tps[bo]], writes=[Trd])
                S.op("dve", lambda e, bo=bo, qh=qh: e.tensor_tensor(out=o_a[pb:pb + 64, c, qh * 512:(qh + 1) * 512], in0=ps[bo][pb:pb + 64, :],
                                                                   in1=rden[pb:pb + 64, 0:512], op=ALU.mult), reads=[Tps[bo], Trd], writes=[Toa])

    def dil_attention(self, l, QdT, TQd, o_d, Tod, HT, SC, QF):
        S, ps, Tps, cmb, mh = self.S, self.ps, self.Tps, self.cmb, self.mh
        kd = [self.av(SC + s * 4096, 4096, BF16, "p (a b) -> p a b", a=2) for s in range(4)]
        vd = [self.av(HT + s * 8192, 8192, BF16, "p (t hp c) -> p t hp c", t=8, hp=2) for s in range(4)]
        Tkd = [T() for _ in range(4)]
        Tvd = [T() for _ in range(4)]
        Uacc = self.av(SC + 16384, 8192, F32, "p (a b) -> p a b", a=2)
        Dacc = self.av(SC + 24576, 8192, F32, "p (a b) -> p a b", a=2)
        TU = T()
        Pt = [self.av(QF + i * 512, 512, BF16) for i in range(8)]
        TPt = [T() for _ in range(8)]
        Qz = [self.av(QF + 4096 + i * 2048, 2048, BF16) for i in range(2)]
        qzall = self.av(QF + 4096, 4096, BF16)
        TQz = [T(), T()]
        S.op("dve", lambda e: e.memset(qzall, 0.0), writes=TQz)
        DIL = (1, 4, 16)
        LAG = 2
        for g in range(3):
            dil = DIL[g]
            for s in range(4):
                def srcT(i, s=s):
                    if s < 3:
                        return self.xbd[i].ap()[s * 128:(s + 1) * 128, :], self.Txbd[i]
                    return self.xbs[i].ap(), self.Txbs[i]
                for cc_ in range(2):
                    c6 = 2 * g + cc_
                    ks, Tks = srcT(3 + c6 // 3)
                    S.dma("sp", lambda e, ks=ks, c6=c6, cc_=cc_, s=s: e.dma_start(out=kd[s][:, cc_, :], in_=ks[:, (c6 % 3) * 1024:(c6 % 3 + 1) * 1024]),
                          reads=[Tks], writes=[Tkd[s]])
                vs, Tvs = srcT(5 + g)
                vsrc = vs.rearrange("p (t hp c) -> p t hp c", t=8, hp=2)
                S.dma("sp", lambda e, vsrc=vsrc, s=s: e.dma_start(out=vd[s], in_=vsrc), reads=[Tvs], writes=[Tvd[s]])
            if dil != 16:
                fl = self.flags
                if dil == 1:
                    ksel = [kd[s_][:, :, 896:1024] for s_ in range(3)]
                    vsel = [vd[s_][:, 7:8, :, :].rearrange("p o hp c -> p (o hp c)") for s_ in range(3)]
                else:
                    ksel = [self.av(SC + s_ * 4096, 4096, BF16) for s_ in range(3)]
                    vsel = [vd[s_].rearrange("p (t two) hp c -> p t two (hp c)", two=2)[:, :, 1:2, :].rearrange("p t o c -> p t (o c)") for s_ in range(3)]
                for sel, Tt in ((ksel, Tkd), (vsel, Tvd)):
                    S.op("dve", lambda e, sel=sel: e.tensor_scalar(out=sel[2], in0=sel[2], scalar1=fl[:, 5:6], scalar2=None, op0=ALU.mult),
                         reads=[Tt[2], self.Tc], writes=[Tt[2]])
                    S.op("dve", lambda e, sel=sel: e.scalar_tensor_tensor(out=sel[2], in0=sel[1], scalar=fl[:, 4:5], in1=sel[2], op0=ALU.mult, op1=ALU.add),
                         reads=[Tt[1], Tt[2], self.Tc], writes=[Tt[2]])
                    S.op("dve", lambda e, sel=sel: e.scalar_tensor_tensor(out=sel[2], in0=sel[0], scalar=fl[:, 3:4], in1=sel[2], op0=ALU.mult, op1=ALU.add),
                         reads=[Tt[0], Tt[2], self.Tc], writes=[Tt[2]])

            def head_blocks(j):
                cc, hh = j // 2, j % 2
                pb = 64 * hh
                chunk = 2 * g + cc
                qz = Qz[hh]
                S.op("dve", lambda e, qz=qz, pb=pb, chunk=chunk: e.tensor_copy(out=qz[pb:pb + 64, :], in_=QdT[pb:pb + 64, chunk, :]),
                     reads=[TQd[chunk]], writes=[TQz[hh]])
                blocks = {0: [], 1: []}
                if dil == 1:
                    for kc in range(8):
                        qb = kc // 4
                        if kc % 4 != 3:
                            blocks[qb].append((3, 128 * kc, kc, 128 * kc, 256, cmb[:, 0:256]))
                        else:
                            blocks[qb].append((3, 128 * kc, kc, 128 * kc, 128, cmb[:, 0:128]))
                            if kc < 7:
                                blocks[qb + 1].append((3, 128 * kc, kc, 128 * kc + 128, 128, cmb[:, 128:256]))
                    blocks[0].append((2, 896, 7, 0, 128, cmb[:, 128:256]))
                elif dil == 4:
                    for r in range(4):
                        qb = r // 2
                        base = 256 * r
                        blocks[qb].append((3, base, 2 * r, base, 256, cmb[:, 0:256]))
                        blocks[qb].append((3, base + 128, 2 * r + 1, base + 128, 128, cmb[:, 0:128]))
                        blocks[qb].append((2, base + 128, 2 * r + 1, base, 128, cmb[:, 128:256]))
                else:
                    for rp in range(8):
                        qb = rp // 4
                        base = 128 * rp
                        blocks[qb].append((3, base, rp, base, 128, cmb[:, 256:384]))
                        for rr in range(3):
                            blocks[qb].append((rr, base, rp, base, 128, mh[:, (3 + rr) * 128:(4 + rr) * 128]))
                return blocks

            def make_stream(j, qb, steps):
                cc, hh = j // 2, j % 2
                pb = 64 * hh
                ob = 64 - pb
                qz = Qz[hh]
                bo = 6 + (self.bank_i % 2)
                self.bank_i += 1
                st = {"i": 0, "n": len(steps), "pend": []}

                def pv(i, s, vtile, q0, nq, bp):
                    S.op("pe", lambda e: e.matmul(ps[bo][:, q0 - qb * 512:q0 - qb * 512 + nq], vd[s][:, vtile, j // 2, (j % 2) * 128:(j % 2) * 128 + 128],
                                                  Pt[bp][:, 0:nq], start=(i == 0), stop=(i == len(steps) - 1)),
                         reads=[Tvd[s], TPt[bp]], writes=[Tps[bo]])

                def step():
                    i = st["i"]
                    (s, k0, vtile, q0, nq, mask) = steps[i]
                    bs = self.dil_bs % 6
                    bp = self.dil_bs % 8
                    self.dil_bs += 1
                    S.op("pe", lambda e: e.matmul(ps[bs][:, 0:nq], kd[s][:, cc, k0:k0 + 128], qz[:, q0:q0 + nq], start=True, stop=True),
                         reads=[Tkd[s], TQz[hh]], writes=[Tps[bs]])
                    S.op("act", lambda e: e.activation(out=Pt[bp][:, 0:nq], in_=ps[bs][:, 0:nq], func=AF.Exp), reads=[Tps[bs]], writes=[TPt[bp]])
                    S.op("dve", lambda e: e.tensor_tensor(out=Pt[bp][:, 0:nq], in0=Pt[bp][:, 0:nq], in1=mask[:, 0:nq], op=ALU.mult),
                         reads=[TPt[bp], self.Tc], writes=[TPt[bp]])
                    st["pend"].append((i, s, vtile, q0, nq, bp))
                    if len(st["pend"]) > 2:
                        pv(*st["pend"].pop(0))
                    st["i"] += 1

                def finish():
                    while st["pend"]:
                        pv(*st["pend"].pop(0))
                    for (acc, rows) in ((Uacc, pb), (Dacc, ob)):
                        if dil == 1:
                            dv = acc[rows:rows + 64, cc, qb * 512:(qb + 1) * 512]
                            sv = ps[bo][rows:rows + 64, :]
                        elif dil == 4:
                            dv = acc[rows:rows + 64, cc, :].rearrange("p (i r) -> p r i", r=4)[:, 2 * qb:2 * qb + 2, :]
                            sv = ps[bo][rows:rows + 64, :].rearrange("p (r i) -> p r i", r=2)
                        else:
                            dv = acc[rows:rows + 64, cc, :].rearrange("p (i r) -> p r i", r=8)[:, 4 * qb:4 * qb + 4, :]
                            sv = ps[bo][rows:rows + 64, :].rearrange("p (r i) -> p r i", r=4)
                        if g == 0:
                            S.op("dve", lambda e, dv=dv, sv=sv: e.tensor_copy(out=dv, in_=sv), reads=[Tps[bo]], writes=[TU])
                        else:
                            S.op("dve", lambda e, dv=dv, sv=sv: e.tensor_tensor(out=dv, in0=dv, in1=sv, op=ALU.add), reads=[Tps[bo], TU], writes=[TU])
                st["step"], st["finish"] = step, finish
                return st

            for jp in (0, 2):
                blk = [head_blocks(jp), head_blocks(jp + 1)]
                for qb in range(2):
                    streams = [make_stream(jp + k, qb, blk[k][qb]) for k in range(2)]
                    active = list(streams)
                    while active:
                        for st in list(active):
                            if st["i"] < st["n"]:
                                st["step"]()
                            else:
                                st["finish"]()
                                active.remove(st)
        S.barrier()
        rtmp = self.av(QF + 4096, 4096, F32)
        Trt = T()
        for cc in range(2):
            for hh in range(2):
                pb = 64 * hh
                ob = 64 - pb
                S.op("dve", lambda e, cc=cc, pb=pb, ob=ob: e.reciprocal(out=rtmp[pb:pb + 64, :], in_=Dacc[ob:ob + 64, cc, :]), reads=[TU], writes=[Trt])
                S.op("dve", lambda e, cc=cc, pb=pb: e.tensor_tensor(out=o_d[pb:pb + 64, cc, :], in0=Uacc[pb:pb + 64, cc, :], in1=rtmp[pb:pb + 64, :], op=ALU.mult),
                     reads=[TU, Trt], writes=[Tod])

    def sgu(self, l, hT, ThT, o_c, Toc, SC):
        S, ps, Tps, cmb, d = self.S, self.ps, self.Tps, self.cmb, self.din
        S.barrier()
        big = [self.av(SC + i * 8192, 8192, BF16) for i in range(2)]
        Tbig = [T(), T()]
        vg = self.av(SC + 16384, 2048, F32)
        vn = self.av(SC + 18432, 4096, BF16, "p (a b) -> p a b", a=4)
        ut = self.av(SC + 22528, 2048, F32)
        wsT = self.av(SC + 24576, 2048, F32)
        wsb = self.av(SC + 26624, 1024, BF16)
        gt = self.av(SC + 27648, 2048, F32)
        sb_ = self.av(SC + 29696, 2048, F32)
        ssc = self.av(SC + 31744, 32, F32)
        Tvg, Tvn, Tut, Tw_, Tss = T(), T(), T(), T(), T()
        for pc in range(2):
            self.load_big(d["w_cv"][l, pc], big[pc], Tbig[pc], 4096)
        S.dma("sp", lambda e: e.dma_start(out=wsT, in_=d["sguwT"][l]), writes=[Tw_])
        S.dma("sp", lambda e: e.dma_start(out=gt, in_=d["sgug"][l]), writes=[Tw_])
        S.dma("sp", lambda e: e.dma_start(out=sb_[0:1, :], in_=d["sgub"][l]), writes=[Tw_])
        for g in range(4):
            S.op("dve", lambda e, g=g: e.tensor_tensor(out=wsb[:, g * 128:(g + 1) * 128], in0=wsT[:, g * 128:(g + 1) * 128], in1=cmb[:, 0:128], op=ALU.mult),
                 reads=[Tw_, self.Tc], writes=[Tw_])
        hrhs1 = lambda kc, half: hT[:, kc, half * 512:(half + 1) * 512]
        bi = 0
        for half in range(2):
            for t4 in range(4):
                tt = half * 4 + t4
                b = 4 + bi % 2
                bi += 1
                for pc in range(2):
                    fns = [lambda e, kc=kc, b=b, tt=tt, pc=pc: e.matmul(ps[b][:, pc * 256:(pc + 1) * 256], hT[:, kc, tt * 128:(tt + 1) * 128],
                                                                    big[pc][:, kc * 256:(kc + 1) * 256], start=(kc == 0), stop=(kc == 15))
                           for kc in range(16)]
                    S.op("pe", fns, reads=ThT + [Tbig[pc]], writes=[Tps[b]])
                S.op("act", lambda e, b=b: e.activation(out=vg, in_=ps[b][:, :], func=AF.Gelu_apprx_tanh), reads=[Tps[b]], writes=[Tvg])
                S.op("dve", lambda e, t4=t4: e.scalar_tensor_tensor(out=vn[:, t4, :], in0=vg, scalar=1.0, in1=vg, op0=ALU.mult, op1=ALU.mult,
                                                                   accum_out=ssc[:, 0:1]), reads=[Tvg], writes=[Tvn, Tss])
                S.op("act", lambda e: e.activation(out=ssc[:, 1:2], in_=ssc[:, 0:1], func=AF.Sqrt, bias=self.cst[:, 0:1], scale=1.0 / 512.0),
                     reads=[Tss, self.Tc], writes=[Tss])
                S.op("dve", lambda e: e.reciprocal(out=ssc[:, 1:2], in_=ssc[:, 1:2]), reads=[Tss], writes=[Tss])
                S.op("dve", lambda e, t4=t4: e.scalar_tensor_tensor(out=vn[:, t4, :], in0=vg, scalar=ssc[:, 1:2], in1=gt, op0=ALU.mult, op1=ALU.mult),
                     reads=[Tvg, Tss, Tw_], writes=[Tvn])
            for g in range(4):
                w, Tw = self.load_w(d["w_cu"][l, g], 2048)
                bu = self.next_pair()[0]
                self.proj_b(lambda kc, w=w: w[:, kc * 128:(kc + 1) * 128], Tw, 16, hrhs1, ThT, (bu,), halves=(half,))
                S.op("act", lambda e, bu=bu: e.activation(out=ut, in_=ps[bu][:, :], func=AF.Gelu_apprx_tanh), reads=[Tps[bu]], writes=[Tut])
                bm = 6 + g % 2
                for t4 in range(4):
                    S.op("pe", [lambda e, t4=t4, g=g, bm=bm: e.matmul(ps[bm][:, t4 * 128:(t4 + 1) * 128], vn[:, t4, g * 128:(g + 1) * 128],
                                                                   wsb[:, g * 128:(g + 1) * 128], start=True, stop=False),
                                lambda e, t4=t4, g=g, bm=bm: e.matmul(ps[bm][:, t4 * 128:(t4 + 1) * 128], self.onesf[0:1, 0:128],
                                                                   sb_[0:1, g * 128:(g + 1) * 128], start=False, stop=True)],
                         reads=[Tvn, Tw_, self.Tc], writes=[Tps[bm]])
                S.op("dve", lambda e, g=g, bm=bm, half=half: e.tensor_tensor(out=o_c[:, g, half * 512:(half + 1) * 512], in0=ut, in1=ps[bm][:, :], op=ALU.mult),
                     reads=[Tut, Tps[bm]], writes=[Toc])

    def ffn(self, l, hT, ThT, OB, R0):
        S, ps, Tps, d = self.S, self.ps, self.Tps, self.din
        xhs, xhd = self.xh_src.ap(), self.xh_dst.ap()
        S.dma("sp", lambda e: e.dma_start(out=xhs.rearrange("p (k t) -> p k t", t=2), in_=hT[:, :, 1022:1024]), reads=ThT, writes=[self.Txhs])
        S.dma("pool", lambda e: e.collective_compute("AllGather", ALU.bypass, replica_groups=RG, dma_qos="P3",
                                                     ins=[self.xh_src.ap().opt()], outs=[self.xh_dst.ap().opt()]),
              reads=[self.Txhs], writes=[self.Txhd], dsem=self.cc_ds, inc=1, track=False)
        Th = T()
        S.dma("sp", lambda e: e.dma_start(out=self.hcand[:, :].rearrange("p (r c) -> p r c", r=3),
                                          in_=xhd[0:384, :].rearrange("(r p) c -> p r c", p=128)), reads=[self.Txhd], writes=[Th])
        S.op("dve", lambda e: e.tensor_scalar(out=self.hhalo[:, :], in0=self.hcand[:, 0:32], scalar1=self.flags[:, 3:4], scalar2=None, op0=ALU.mult),
             reads=[Th, self.Tc], writes=[Th])
        for r in (1, 2):
            S.op("dve", lambda e, r=r: e.scalar_tensor_tensor(out=self.hhalo[:, :], in0=self.hcand[:, 32 * r:32 * r + 32], scalar=self.flags[:, 3 + r:4 + r],
                                                             in1=self.hhalo[:, :], op0=ALU.mult, op1=ALU.add), reads=[Th, self.Tc], writes=[Th])
        S.barrier()
        act = [self.av(R0 + i * 22528, 22528, BF16, "p (a b) -> p a b", a=11) for i in range(2)]
        Tact = [[T() for _ in range(11)] for _ in range(2)]
        t12 = [self.av(R0 + 45056 + i * 4096, 4096, F32) for i in range(2)]
        Tt12 = [T(), T()]
        bufs = [[self.av(OB + (k * 2 + i) * 4112, 4112, F32) for i in range(2)] for k in range(2)]
        Tbuf = [[T(), T()], [T(), T()]]
        sl = self.av(OB + 16448, 4096, F32)
        Tsl = T()
        hs_i = 0
        dn_i = 0
        hrhs = lambda kc, half: hT[:, kc, half * 512:(half + 1) * 512]
        for gk in range(4):
            for mm in range(11):
                m = gk * 11 + mm
                q = m % 2
                for k in range(2):
                    si = 2 * m + k
                    w, Tw = self.load_w(d["w_up"][l, si], 2048)
                    pr = self.next_pair()
                    self.proj_b(lambda kc, w=w: w[:, kc * 128:(kc + 1) * 128], Tw, 16, hrhs, ThT, pr)
                    hsl = (hs_i % 256) * 2
                    hs_i += 1
                    fns = [lambda e, kc=kc, w=w, hsl=hsl: e.matmul(ps[7][:, hsl:hsl + 2], w[:, kc * 128:(kc + 1) * 128], self.hhalo[:, 2 * kc:2 * kc + 2],
                                                                start=(kc == 0), stop=(kc == 15)) for kc in range(16)]
                    S.op("pe", fns, reads=[Tw, Th], writes=[Tps[7]])
                    bf = bufs[k][q]
                    for half in range(2):
                        S.op("act", lambda e, bf=bf, half=half, b=pr[half]: e.activation(out=bf[:, 2 + half * 512:2 + (half + 1) * 512], in_=ps[b][:, :], func=AF.Copy),
                             reads=[Tps[pr[half]]], writes=[Tbuf[k][q]])
                    S.op("dve", lambda e, bf=bf, hsl=hsl: e.tensor_copy(out=bf[:, 0:2], in_=ps[7][:, hsl:hsl + 2]), reads=[Tps[7]], writes=[Tbuf[k][q]])
                    wc = lambda tap, si=si: self.fcw[:, l * 264 + tap * 88 + si: l * 264 + tap * 88 + si + 1]
                    tk = t12[k]
                    S.op("dve", lambda e, bf=bf, tk=tk, wc=wc: e.tensor_scalar(out=tk, in0=bf[:, 2:1026], scalar1=wc(2), scalar2=None, op0=ALU.mult),
                         reads=[Tbuf[k][q], self.Tc], writes=[Tt12[k]])
                    S.op("dve", lambda e, bf=bf, tk=tk, wc=wc: e.scalar_tensor_tensor(out=tk, in0=bf[:, 1:1025], scalar=wc(1), in1=tk, op0=ALU.mult, op1=ALU.add),
                         reads=[Tbuf[k][q], self.Tc, Tt12[k]], writes=[Tt12[k]])
                    S.op("dve", lambda e, bf=bf, tk=tk, wc=wc: e.scalar_tensor_tensor(out=tk, in0=bf[:, 0:1024], scalar=wc(0), in1=tk, op0=ALU.mult, op1=ALU.add),
                         reads=[Tbuf[k][q], self.Tc, Tt12[k]], writes=[Tt12[k]])
                S.op("act", lambda e: e.activation(out=sl, in_=t12[0], func=AF.Silu), reads=[Tt12[0]], writes=[Tsl])
                S.op("dve", lambda e, gk=gk, mm=mm: e.tensor_tensor(out=act[gk % 2][:, mm, :], in0=sl, in1=t12[1], op=ALU.mult),
                     reads=[Tsl, Tt12[1]], writes=[Tact[gk % 2][mm]])
            for j in range(16):
                w, Tw = self.load_w(d["w_dn"][l, gk * 16 + j], 1408)
                for half in range(2):
                    b = 4 + dn_i % 3
                    dn_i += 1
                    fns = [lambda e, kc=kc, w=w, b=b, half=half, gk=gk: e.matmul(ps[b][:, :], w[:, kc * 128:(kc + 1) * 128],
                                                                              act[gk % 2][:, kc, half * 512:(half + 1) * 512],
                                                                              start=(kc == 0), stop=(kc == 10)) for kc in range(11)]
                    S.op("pe", fns, reads=[Tw] + Tact[gk % 2], writes=[Tps[b]])
                    hs = slice(half * 512, (half + 1) * 512)
                    S.op("dve", lambda e, j=j, hs=hs, b=b: e.tensor_tensor(out=self.xT[:, j, hs], in0=self.xT[:, j, hs], in1=ps[b][:, :], op=ALU.add),
                         reads=[Tps[b]], writes=[self.TxT[j]])

    def finish(self):
        S = self.S
        self.tick_deferred(flush=True)
        S.barrier()
        out = self.av(0, 65536, F32, "p (a b) -> p a b", a=16)
        Tout = [T() for _ in range(16)]
        self.rmsnorm(lambda kc: self.gfin[:, kc:kc + 1], None, None, 65536 + 8192, out_f32=out, Tout=Tout)
        tks = []
        for kc in range(16):
            tks.append(S.dma("sp", lambda e, kc=kc: e.dma_start(out=self.yT[:, kc, :], in_=out[:, kc, :]), reads=[Tout[kc]]))
        for tk in tks:
            S.wait_ticket("sp", tk)
        for tk in S.recent.values():
            S.wait_ticket("sp", tk)

    def build(self):
        self.setup()
        for l in range(self.L):
            self.layer(l)
        self.finish()
        self.st.close()
        return self.nc


TAPS = (("t_h", [128, 16, NTOK]), ("t_oa", [128, 4, NTOK]), ("t_od", [128, 2, NTOK]), ("t_ob", [128, 4, NTOK]),
        ("t_oc", [128, 4, NTOK]), ("t_x1", [128, 16, NTOK]), ("t_x2", [128, 16, NTOK]))


def run(inputs, depth=DEPTH, taps=(), n_cores=8, stop=99):
    maps = prep_inputs(inputs, depth)
    prog = Prog(depth, taps, stop)
    nc = prog.build()
    keep = set(input_specs(depth, stop).keys())
    maps = [{k: v for k, v in m.items() if k in keep} for m in maps]
    res = run_bass_kernel_spmd(nc, maps[:n_cores], core_ids=list(range(n_cores)))
    return res.results


def assemble(results, name="yT"):
    y = np.empty((2, 4096, results[0][name].shape[1] * 128), np.float32)
    for c in range(8):
        b, R = c // 4, c % 4
        t = results[c][name]
        y[b, R * NTOK:(R + 1) * NTOK, :] = t.transpose(2, 1, 0).reshape(NTOK, -1)
    return y


def kernel(**inputs):
    results = run(inputs, DEPTH)
    return assemble(results)
```
